# Optimizing a Trainium2 kernel written in Bass

```python
import math
import jax, jax.numpy as jnp
from jax import lax
import numpy as np

D_MODEL = 2048
BATCH = 16
SEQ = 2048
DEPTH = 2

HEAD_DIM = 128
N_HEADS = D_MODEL // HEAD_DIM
A_Q_HEADS = N_HEADS // 2
A_KV_HEADS = A_Q_HEADS // 4
A_HALF_WINDOW = 128
B_HEADS = N_HEADS - A_Q_HEADS
B_CONFIGS = ((128, 1), (512, 4), (2048, 16))
C_HEADS = N_HEADS
GRID_W = 64
NA_KH = 8
NA_KW = 16
RMS_EPS = 1e-5
NEG = -1e30

A_Q_COLS = A_Q_HEADS * HEAD_DIM
A_KV_COLS = A_KV_HEADS * HEAD_DIM
B_COLS = B_HEADS * HEAD_DIM
AB_WIDTH = A_Q_COLS + B_COLS
AB_SPLITS = (A_Q_COLS,
             A_Q_COLS + A_KV_COLS,
             A_Q_COLS + 2 * A_KV_COLS,
             A_Q_COLS + 2 * A_KV_COLS + B_COLS,
             A_Q_COLS + 2 * A_KV_COLS + 2 * B_COLS,
             A_Q_COLS + 2 * A_KV_COLS + 3 * B_COLS)
AB_IN_COLS = A_Q_COLS + 2 * A_KV_COLS + 3 * B_COLS + AB_WIDTH
C_WIDTH = C_HEADS * HEAD_DIM
C_IN_COLS = 4 * C_WIDTH

kernel_name = "hybrid_window_dilated_neighbourhood_encoder"


def _rmsnorm(x, g):
    xf = x.astype(jnp.float32)
    y = xf * lax.rsqrt(jnp.mean(xf * xf, axis=-1, keepdims=True) + RMS_EPS)
    return (y * g.astype(jnp.float32)).astype(x.dtype)


def _alibi_slopes(n):
    return jnp.exp2(-8.0 * jnp.arange(1, n + 1, dtype=jnp.float32) / n)


def _banded_attention(q, k, v, half, dist_scale, slopes, sink=None):
    n, L, hq, dh = q.shape
    hkv = k.shape[2]
    g = hq // hkv
    blk = half
    nb = -(-L // blk)
    lp = nb * blk
    qp = jnp.pad(q, ((0, 0), (0, lp - L), (0, 0), (0, 0)))
    kvpad = ((0, 0), (blk, lp - L + blk), (0, 0), (0, 0))
    kp = jnp.pad(k, kvpad).reshape(n, nb + 2, blk, hkv, dh)
    vp = jnp.pad(v, kvpad).reshape(n, nb + 2, blk, hkv, dh)
    kw = jnp.concatenate([kp[:, :-2], kp[:, 1:-1], kp[:, 2:]], axis=2)
    vw = jnp.concatenate([vp[:, :-2], vp[:, 1:-1], vp[:, 2:]], axis=2)
    qb = qp.reshape(n, nb, blk, hkv, g, dh)
    s = jnp.einsum('nbqkgd,nbskd->nbkgqs', qb, kw).astype(jnp.float32) * (dh ** -0.5)
    rel = jnp.arange(3 * blk)[None, :] - blk - jnp.arange(blk)[:, None]
    kpos = jnp.arange(nb)[:, None] * blk - blk + jnp.arange(3 * blk)[None, :]
    valid = (jnp.abs(rel) <= half)[None] & ((kpos >= 0) & (kpos < L))[:, None, :]
    dist = (jnp.abs(rel) * dist_scale).astype(jnp.float32)
    bias = -slopes.astype(jnp.float32).reshape(hkv, g)[:, :, None, None] * dist
    s = jnp.where(valid[None, :, None, None], s + bias, NEG)
    m = jnp.max(s, axis=-1)
    if sink is not None:
        sk = sink.astype(jnp.float32).reshape(hkv, g)[None, None, :, :, None]
        m = jnp.maximum(m, sk)
    e = jnp.exp(s - m[..., None])
    denom = jnp.sum(e, axis=-1)
    if sink is not None:
        denom = denom + jnp.exp(sk - m)
    p = e / denom[..., None]
    o = jnp.einsum('nbkgqs,nbskd->nbqkgd', p.astype(v.dtype), vw).reshape(n, lp, hq, dh)[:, :L]
    lse = (m + jnp.log(denom)).transpose(0, 1, 4, 2, 3).reshape(n, lp, hq)[:, :L]
    return o, lse


def _dilated_mixture(q, k, v, slopes):
    b, s, h, dh = q.shape
    outs, lses = [], []
    for window, d in B_CONFIGS:
        ls = s // d

        def fold(t):
            return t.reshape(b, ls, d, h, dh).transpose(0, 2, 1, 3, 4).reshape(b * d, ls, h, dh)

        o, lse = _banded_attention(fold(q), fold(k), fold(v), window // (2 * d), d, slopes)
        outs.append(o.reshape(b, d, ls, h, dh).transpose(0, 2, 1, 3, 4).reshape(b, s, h, dh))
        lses.append(lse.reshape(b, d, ls, h).transpose(0, 2, 1, 3).reshape(b, s, h))
    w = jax.nn.softmax(jnp.stack(lses), axis=0)
    out = jnp.einsum('cbsh,cbshd->bshd', w, jnp.stack(outs).astype(jnp.float32))
    return out.astype(q.dtype)


def _neighbourhood_attention(q, k, v, rpb):
    b, s, h, dh = q.shape
    rows = s // GRID_W
    kh = min(NA_KH, rows)
    kw = NA_KW
    qg = q.reshape(b, rows, GRID_W, h, dh).transpose(1, 0, 2, 3, 4)
    kg = k.reshape(b, rows, GRID_W, h, dh)
    vg = v.reshape(b, rows, GRID_W, h, dh)
    col = jnp.arange(GRID_W)
    col_start = jnp.clip(col - kw // 2, 0, GRID_W - kw)
    col_mask = (col[None, :] >= col_start[:, None]) & (col[None, :] < col_start[:, None] + kw)
    dc = jnp.clip(col[None, :] - col[:, None] + NA_KW - 1, 0, 2 * NA_KW - 2)
    rpb_cols = rpb[:, :, dc]
    scale = dh ** -0.5

    def one_row(args):
        r, q_r = args
        r0 = jnp.clip(r - kh // 2, 0, rows - kh)
        k_r = lax.dynamic_slice_in_dim(kg, r0, kh, axis=1)
        v_r = lax.dynamic_slice_in_dim(vg, r0, kh, axis=1)
        dr = r0 + jnp.arange(kh) - r + NA_KH - 1
        bias = rpb_cols[:, dr].transpose(0, 2, 1, 3).astype(jnp.float32)
        sc = jnp.einsum('bqhd,bkwhd->bhqkw', q_r, k_r).astype(jnp.float32) * scale + bias[None]
        sc = jnp.where(col_mask[None, None, :, None, :], sc, NEG)
        p = jax.nn.softmax(sc.reshape(b, h, GRID_W, kh * GRID_W), axis=-1).reshape(b, h, GRID_W, kh, GRID_W)
        return jnp.einsum('bhqkw,bkwhd->bqhd', p.astype(v.dtype), v_r)

    out = lax.map(one_row, (jnp.arange(rows), qg))
    return out.transpose(1, 0, 2, 3, 4).reshape(b, s, h, dh)


def _layer_ab(x, ln, w_in, sink, w_out):
    b, s, _ = x.shape
    hn = _rmsnorm(x, ln)
    proj = hn @ w_in
    aq, ak, av, bq, bk, bv, z = jnp.split(proj, AB_SPLITS, axis=-1)
    slopes = _alibi_slopes(A_Q_HEADS + B_HEADS)
    ya, _ = _banded_attention(aq.reshape(b, s, A_Q_HEADS, HEAD_DIM),
                              ak.reshape(b, s, A_KV_HEADS, HEAD_DIM),
                              av.reshape(b, s, A_KV_HEADS, HEAD_DIM),
                              A_HALF_WINDOW, 1, slopes[:A_Q_HEADS], sink)
    yb = _dilated_mixture(bq.reshape(b, s, B_HEADS, HEAD_DIM),
                          bk.reshape(b, s, B_HEADS, HEAD_DIM),
                          bv.reshape(b, s, B_HEADS, HEAD_DIM),
                          slopes[A_Q_HEADS:])
    y = jnp.concatenate([ya.reshape(b, s, A_Q_COLS), yb.reshape(b, s, B_COLS)], axis=-1)
    y = y * jax.nn.silu(z)
    return x + y @ w_out


def _layer_c(x, ln, w_in, rpb, w_out):
    b, s, _ = x.shape
    hn = _rmsnorm(x, ln)
    proj = hn @ w_in
    cq, ck, cv, z = jnp.split(proj, 4, axis=-1)
    y = _neighbourhood_attention(cq.reshape(b, s, C_HEADS, HEAD_DIM),
                                 ck.reshape(b, s, C_HEADS, HEAD_DIM),
                                 cv.reshape(b, s, C_HEADS, HEAD_DIM), rpb)
    y = y.reshape(b, s, C_WIDTH) * jax.nn.silu(z)
    return x + y @ w_out


def setup_inputs(seed: int = 0) -> dict:
    key = jax.random.key(seed)
    ks = jax.random.split(key, 11)
    n_even = (DEPTH + 1) // 2
    n_odd = DEPTH // 2
    d = D_MODEL
    f32 = jnp.float32
    x = jax.random.normal(ks[0], (BATCH, SEQ, d), f32)
    ln_ab = 1.0 + 0.02 * jax.random.normal(ks[1], (n_even, d), f32)
    w_in_ab = jax.random.normal(ks[2], (n_even, d, AB_IN_COLS), f32) * d ** -0.5
    sink_a = 0.5 * jax.random.normal(ks[3], (n_even, A_Q_HEADS), f32)
    w_out_ab = jax.random.normal(ks[4], (n_even, AB_WIDTH, d), f32) * AB_WIDTH ** -0.5
    ln_c = 1.0 + 0.02 * jax.random.normal(ks[5], (n_odd, d), f32)
    w_in_c = jax.random.normal(ks[6], (n_odd, d, C_IN_COLS), f32) * d ** -0.5
    rpb_c = 0.1 * jax.random.normal(ks[7], (n_odd, C_HEADS, 2 * NA_KH - 1, 2 * NA_KW - 1), f32)
    w_out_c = jax.random.normal(ks[8], (n_odd, C_WIDTH, d), f32) * C_WIDTH ** -0.5
    ln_f = 1.0 + 0.02 * jax.random.normal(ks[9], (d,), f32)
    return {"x": x, "ln_ab": ln_ab, "w_in_ab": w_in_ab, "sink_a": sink_a, "w_out_ab": w_out_ab,
            "ln_c": ln_c, "w_in_c": w_in_c, "rpb_c": rpb_c, "w_out_c": w_out_c, "ln_f": ln_f}


def reference(x, ln_ab, w_in_ab, sink_a, w_out_ab, ln_c, w_in_c, rpb_c, w_out_c, ln_f):
    for layer in range(DEPTH):
        i = layer // 2
        if layer % 2 == 0:
            x = _layer_ab(x, ln_ab[i], w_in_ab[i], sink_a[i], w_out_ab[i])
        else:
            x = _layer_c(x, ln_c[i], w_in_c[i], rpb_c[i], w_out_c[i])
    return _rmsnorm(x, ln_f)
```

```python
import contextlib
import numpy as np
import concourse.bass as bass
import concourse.mybir as mybir
from concourse.bass_utils import run_bass_kernel_spmd

F32 = mybir.dt.float32
BF16 = mybir.dt.bfloat16
AF = mybir.ActivationFunctionType
ALU = mybir.AluOpType

D = 2048
S = 2048
NT = 16
SCALE = float(128 ** -0.5)
NEGB = -30000.0
N_CORES = 8
EW = 3072
NWS = 8

ENGS = ("pe", "act", "dve", "pool", "sp")


class Sem:
    def __init__(self, h):
        self.h = h
        self.count = 0


class KB:
    def __init__(self, nc, es):
        self.nc = nc
        self.es = es
        self.q = {e: [] for e in ENGS}
        self.waited = {}
        self.S = {}
        for e in ("pe", "act", "dve"):
            self.S[e] = self.new_sem("S_" + e)
        self.last = {e: None for e in ("pe", "act", "dve")}

    def new_sem(self, name):
        return Sem(self.es.enter_context(self.nc.semaphore(name)))

    def wait(self, eng, cond):
        if cond is None:
            return
        sem, val = cond
        assert val <= sem.count, (eng, val, sem.count)
        key = (eng, id(sem))
        if self.waited.get(key, 0) >= val:
            return
        self.waited[key] = val
        h = sem.h
        self.q[eng].append(lambda e: e.wait_ge(h, val))

    def op(self, eng, fn, ms=True):
        if not ms:
            self.q[eng].append(fn)
            return None
        s = self.S[eng]
        s.count += 1
        h = s.h
        self.q[eng].append(lambda e: fn(e).then_inc(h, 1))
        self.last[eng] = (s, s.count)
        return (s, s.count)

    def dma(self, eng, out, in_, sem):
        sem.count += 16
        h = sem.h
        self.q[eng].append(lambda e: e.dma_start(out=out, in_=in_).then_inc(h, 16))
        return (sem, sem.count)

    def barrier(self):
        for e in ("pe", "act", "dve", "sp"):
            for o in ("pe", "act", "dve"):
                if o != e:
                    self.wait(e, self.last[o])


class WStream:
    def __init__(self, kb, Wt):
        self.kb = kb
        self.Wt = Wt
        self.jobs = []
        self.pos = 0
        self.next = 0
        self.slot_free = [None] * NWS
        self.slot_pending = [False] * NWS
        self.ld = [kb.new_sem("wld%d" % i) for i in range(NWS)]

    def add_job(self, src, n):
        if n == 4 and self.pos % 4:
            self.pos = (self.pos + 4 - self.pos % 4) % NWS
        job = {"src": src, "n": n, "slot0": self.pos, "ld": None}
        self.pos = (self.pos + n) % NWS
        self.jobs.append(job)
        return job

    def try_issue(self):
        kb = self.kb
        while self.next < len(self.jobs):
            job = self.jobs[self.next]
            slots = range(job["slot0"], job["slot0"] + job["n"])
            if any(self.slot_pending[s] for s in slots):
                return
            for s in slots:
                kb.wait("pool", self.slot_free[s])
                self.slot_pending[s] = True
            dst = self.Wt[:, job["slot0"] * 2048:(job["slot0"] + job["n"]) * 2048]
            job["ld"] = kb.dma("pool", dst, job["src"], self.ld[job["slot0"]])
            self.next += 1

    def release(self, job, cond):
        for s in range(job["slot0"], job["slot0"] + job["n"]):
            self.slot_free[s] = cond
            self.slot_pending[s] = False
        self.try_issue()


def blocks_A():
    res = []
    for b in range(4):
        lst = []
        for j in range(4 * b - 1, 4 * b + 5):
            if 0 <= j < 16:
                c0 = (max(j - 1, 4 * b) - 4 * b) * 128
                c1 = (min(j + 1, 4 * b + 3) + 1 - 4 * b) * 128
                lst.append((j, 512 * b - 128 * j + 512, c0, c1))
        res.append((512 * b, 512, lst))
    return res


def blocks_B():
    res = []
    for b in range(4):
        js = [j for j in range(16) if -1535 <= 512 * b - 128 * j <= 1151]
        res.append((512 * b, 512, [(j, 512 * b - 128 * j + 1408, 0, 512) for j in js]))
    return res


def blocks_C():
    res = [(0, 320, [(j, 1536 + (11 - 2 * j) * 64, 0, 320) for j in range(4)])]
    for ra, nr, j0 in ((5, 8, 0), (13, 8, 4), (21, 7, 8)):
        res.append((64 * ra, 64 * nr, [(j, (11 - 2 * j + ra) * 64, 0, 64 * nr) for j in range(j0, j0 + 8)]))
    res.append((64 * 28, 256, [(j, 1536 + (11 - 2 * j + 28) * 64, 0, 256) for j in range(12, 16)]))
    return res


def build(nseq, do_layers=(0, 1), final_norm=True):
    nc = bass.Bass("TRN2", target_bir_lowering=False)
    R = nseq * S
    x_d = nc.dram_tensor("x", [R, D], F32, kind="ExternalInput").ap()
    out_d = nc.dram_tensor("out", [R, D], F32, kind="ExternalOutput").ap()
    x1_d = nc.dram_tensor("x1s", [S, D], F32).ap()
    wab_in = nc.dram_tensor("wab_in", [52, 128, 2048], F32, kind="ExternalInput").ap()
    wab_out = nc.dram_tensor("wab_out", [4, 128, 8192], F32, kind="ExternalInput").ap()
    wc_in = nc.dram_tensor("wc_in", [64, 128, 2048], F32, kind="ExternalInput").ap()
    wc_out = nc.dram_tensor("wc_out", [4, 128, 8192], F32, kind="ExternalInput").ap()
    ln_d = nc.dram_tensor("ln", [3, D], F32, kind="ExternalInput").ap()
    sink_d = nc.dram_tensor("sink", [8], F32, kind="ExternalInput").ap()
    ident_d = nc.dram_tensor("ident", [128, 128], F32, kind="ExternalInput").ap()
    EA_d = nc.dram_tensor("EA", [8, 128, 1152], F32, kind="ExternalInput").ap()
    EB_d = nc.dram_tensor("EB", [8, 128, 2944], F32, kind="ExternalInput").ap()
    BC_d = nc.dram_tensor("BC", [16, 128, 3072], F32, kind="ExternalInput").ap()

    with contextlib.ExitStack() as es:
        def sb(name, shape, dt):
            return es.enter_context(nc.sbuf_tensor(name, shape, dt))

        hnT = sb("hnT", [128, 16, 2048], BF16)
        yT = sb("yT", [128, 16, 2048], BF16)
        Wt = sb("Wt", [128, NWS * 2048], BF16)
        Eb = sb("Eb", [128, EW], BF16)
        arena = sb("arena", [128, 20480], BF16)
        ident = sb("ident_sb", [128, 128], BF16)
        ones = sb("ones_sb", [128, 128], BF16)
        stats = sb("stats", [128, 16], F32)
        esink = sb("esink", [128, 16], F32)
        epsT = sb("epsT", [128, 1], F32)
        onesf = sb("onesf", [128, 1], F32)
        ps = es.enter_context(nc.psum_tensor("ps", [128, 4096], F32))

        kb = KB(nc, es)
        ws = WStream(kb, Wt)

        def bank(b):
            return ps[:, 512 * b:512 * (b + 1)]

        def bank2_bf(b):
            return ps[:, 512 * b:512 * (b + 2)].bitcast(BF16)

        bank_free = [None] * 8

        def abf(off, n):
            return arena[:, off:off + n]

        def af32(off, n):
            return arena[:, off:off + 2 * n].bitcast(F32)

        qT = abf(0, 2048)
        kTb = [abf(2048, 2048), abf(4096, 2048)]
        vT = abf(6144, 2048)
        sz2 = abf(8192, 2048)
        vtokb = [abf(10240, 2048), abf(12288, 2048)]
        NP = 6
        Pb = [abf(14336 + 512 * i, 512) for i in range(NP)]
        thf = af32(17408, 512)
        rbuf = af32(18432, 512)
        tbuf = af32(19456, 512)
        xt = [af32(0, 2048), af32(4096, 2048), af32(8192, 2048)]
        hnb = [abf(8192, 2048), abf(10240, 2048)]
        lnB = af32(12288, 2048)
        junk = ps[:, 2048:4096]
        resb = [af32(1024 * i, 512) for i in range(3)]
        xob = [af32(3072 + 1024 * i, 512) for i in range(3)]

        x_ld = [kb.new_sem("xld%d" % i) for i in range(3)]
        x_st = [kb.new_sem("xst%d" % i) for i in range(3)]
        ln_ld = kb.new_sem("lnld")
        e_ld = kb.new_sem("eld")
        c_ld = kb.new_sem("cld")
        r_ld = [kb.new_sem("rld%d" % i) for i in range(3)]
        xo_st = [kb.new_sem("xost%d" % i) for i in range(3)]

        def layer_items(L):
            items = []
            if L == 0:
                for kv in range(2):
                    for gi in range(4):
                        h = kv * 4 + gi
                        items.append(dict(hout=h, gv=10 + kv if gi == 0 else None, gk=8 + kv if gi == 0 else None,
                                          gq=h, gz=36 + h, blocks=blocks_A(), esrc=EA_d[h], ew=1152,
                                          need_exp=False, scol=h))
                for hb in range(8):
                    items.append(dict(hout=8 + hb, gv=28 + hb, gk=20 + hb, gq=12 + hb, gz=44 + hb,
                                      blocks=blocks_B(), esrc=EB_d[hb], ew=2944, need_exp=False, scol=8))
            else:
                for h in range(16):
                    items.append(dict(hout=h, gv=32 + h, gk=16 + h, gq=h, gz=48 + h, blocks=blocks_C(),
                                      esrc=BC_d[h], ew=3072, need_exp=True, scol=8))
            return items

        plan = []
        for sq in range(nseq):
            for L in do_layers:
                w_in = wab_in if L == 0 else wc_in
                w_out = wab_out if L == 0 else wc_out
                items = layer_items(L)
                for it in items:
                    it["jobs"] = {}
                    for kind in ("v", "k", "q", "z"):
                        g = it["g" + kind]
                        if g is not None:
                            it["jobs"][kind] = ws.add_job(w_in[g], 1)
                qjobs = [ws.add_job(w_out[qd], 4) for qd in range(4)]
                plan.append((sq, L, items, qjobs))

        kb.dma("pool", ident[:], ident_d[:, :], c_ld)
        kb.dma("pool", esink[:, 0:8], sink_d.partition_broadcast(128), c_ld)
        kb.op("dve", lambda e: e.memset(ones[:], 1.0))
        kb.op("dve", lambda e: e.memset(epsT[:], 1e-5))
        kb.op("dve", lambda e: e.memset(onesf[:], 1.0))
        kb.op("dve", lambda e: e.memset(esink[:, 8:16], 0.0))
        kb.wait("act", (c_ld, c_ld.count))
        kb.op("act", lambda e: e.activation(out=esink[:, 0:8], in_=esink[:, 0:8], func=AF.Exp))
        kb.wait("pe", (c_ld, c_ld.count))
        ws.try_issue()
        state = {"e_reader": None, "p1n": 0, "p4n": 0, "xt_free": [None, None, None], "hn_free": [None, None],
                 "rb": 0, "qbn": 0, "r_free": None, "grp": 0,
                 "p3n": 0, "res_free": [None] * 3, "xo_free": [None] * 3, "th_free": None}

        def phase_norm(src, ln_idx, dst):
            kb.barrier()
            lncond = kb.dma("sp", lnB, ln_d[ln_idx].partition_broadcast(128), ln_ld)
            pend = None
            for t in range(NT + 1):
                if t < NT:
                    if dst is None:
                        n = state["p1n"]
                        state["p1n"] += 1
                        sl = n % 2
                    else:
                        n = state["p4n"]
                        state["p4n"] += 1
                        sl = n % 3
                    kb.wait("sp", state["xt_free"][sl])
                    ldc = kb.dma("sp", xt[sl], src[t * 128:(t + 1) * 128, :], x_ld[sl])
                    kb.wait("act", ldc)
                    a1 = kb.op("act", lambda e, sl=sl: e.activation(
                        out=junk, in_=xt[sl], func=AF.Square, scale=float(2048 ** -0.5),
                        accum_out=stats[:, sl:sl + 1]))
                    kb.wait("act", a1)
                    a2 = kb.op("act", lambda e, sl=sl: e.activation(
                        out=stats[:, 3 + sl:4 + sl], in_=stats[:, sl:sl + 1], func=AF.Sqrt, bias=epsT[:, 0:1]))
                    kb.wait("dve", a2)
                    d1 = kb.op("dve", lambda e, sl=sl: e.reciprocal(out=stats[:, 6 + sl:7 + sl],
                                                                    in_=stats[:, 3 + sl:4 + sl]))
                    kb.wait("dve", d1)
                    kb.wait("dve", lncond)
                    if dst is None:
                        kb.wait("dve", state["hn_free"][sl])
                        d2 = kb.op("dve", lambda e, sl=sl: e.scalar_tensor_tensor(
                            out=hnb[sl], in0=xt[sl], scalar=stats[:, 6 + sl:7 + sl], in1=lnB,
                            op0=ALU.mult, op1=ALU.mult))
                        state["xt_free"][sl] = d2
                        kb.wait("pe", d2)
                        kb.wait("pe", bank_free[2 * sl])
                        kb.wait("pe", bank_free[2 * sl + 1])
                        pst = bank2_bf(2 * sl)
                        for fc in range(16):
                            pc = kb.op("pe", lambda e, sl=sl, fc=fc, pst=pst: e.transpose(
                                out=pst[:, fc * 128:(fc + 1) * 128], in_=hnb[sl][:, fc * 128:(fc + 1) * 128],
                                identity=ident[:]), ms=(fc == 15))
                        state["hn_free"][sl] = pc
                        cur = (sl, t, pc, pst)
                    else:
                        d2 = kb.op("dve", lambda e, sl=sl: e.scalar_tensor_tensor(
                            out=xt[sl], in0=xt[sl], scalar=stats[:, 6 + sl:7 + sl], in1=lnB,
                            op0=ALU.mult, op1=ALU.mult))
                        kb.wait("sp", d2)
                        stc = kb.dma("sp", dst[t * 128:(t + 1) * 128, :], xt[sl], x_st[sl])
                        state["xt_free"][sl] = stc
                        cur = None
                else:
                    cur = None
                if pend is not None:
                    sl0, t0, pc0, pst0 = pend
                    kb.wait("act", pc0)
                    ev = kb.op("act", lambda e, t0=t0, pst0=pst0: e.activation(
                        out=hnT[:, :, t0 * 128:(t0 + 1) * 128],
                        in_=pst0.rearrange("p (a b) -> p a b", a=16), func=AF.Copy))
                    bank_free[2 * sl0] = ev
                    bank_free[2 * sl0 + 1] = ev
                pend = cur

        def proj_fill(job, kind, tg, dst):
            slot = job["slot0"]
            kb.wait("pe", job["ld"])
            bk = state["rb"] % 4
            state["rb"] += 1
            kb.wait("pe", bank_free[bk])
            for kc in range(16):
                mc = kb.op("pe", lambda e, tg=tg, kc=kc, slot=slot, bk=bk: e.matmul(
                    bank(bk), lhsT=Wt[:, slot * 2048 + kc * 128: slot * 2048 + (kc + 1) * 128],
                    rhs=hnT[:, kc, tg * 512:(tg + 1) * 512], start=(kc == 0), stop=(kc == 15)),
                    ms=(kc == 15))
            if tg == 3:
                ws.release(job, mc)
            cols = slice(tg * 512, (tg + 1) * 512)
            if kind == "q":
                kb.wait("act", mc)
                ev = kb.op("act", lambda e, bk=bk, cols=cols: e.activation(
                    out=dst[:, cols], in_=bank(bk), func=AF.Copy))
            elif kind in ("k", "v"):
                kb.wait("dve", mc)
                ev = kb.op("dve", lambda e, bk=bk, cols=cols: e.tensor_copy(out=dst[:, cols], in_=bank(bk)))
            else:
                kb.wait("act", mc)
                kb.wait("act", state["th_free"])
                a1 = kb.op("act", lambda e, bk=bk: e.activation(out=thf, in_=bank(bk), func=AF.Exp, scale=-1.0))
                kb.wait("act", a1)
                a2 = kb.op("act", lambda e: e.activation(out=thf, in_=thf, func=AF.Ln, bias=onesf[:, 0:1]))
                kb.wait("act", a2)
                a3 = kb.op("act", lambda e: e.activation(out=thf, in_=thf, func=AF.Exp, scale=-1.0))
                kb.wait("dve", a3)
                ev = kb.op("dve", lambda e, bk=bk, cols=cols: e.tensor_tensor(
                    out=dst[:, cols], in0=thf, in1=bank(bk), op=ALU.mult))
                state["th_free"] = ev
            bank_free[bk] = ev
            return ev

        def proj(job, kind, dst):
            ev = None
            for tg in range(4):
                ev = proj_fill(job, kind, tg, dst)
            return ev

        def vtrans(vcond, vdst):
            kb.wait("pe", vcond)
            kb.wait("pe", bank_free[0])
            kb.wait("pe", bank_free[1])
            psv = bank2_bf(0)
            for j in range(16):
                pc = kb.op("pe", lambda e, j=j: e.transpose(
                    out=psv[:, j * 128:(j + 1) * 128], in_=vT[:, j * 128:(j + 1) * 128], identity=ident[:]),
                    ms=(j == 15))
            kb.wait("dve", pc)
            ev = kb.op("dve", lambda e: e.tensor_copy(out=vdst, in_=psv))
            bank_free[0] = ev
            bank_free[1] = ev
            return ev

        def attention(it, econd, qcond, kcond, vcond, zcond, kT, vtok, fillers):
            LA = 3
            G = []
            for bi, (q0, n, lst) in enumerate(it["blocks"]):
                for i, (j, eoff, c0, c1) in enumerate(lst):
                    G.append((q0, n, j, eoff, i == 0, i == len(lst) - 1, c0, c1))
            hout = it["hout"]
            scol = it["scol"]
            p_free = state.setdefault("p_free", [None] * NP)
            gb = state.setdefault("gblk", 0)
            scond = {}
            sbank = {}
            pend_fin = []
            stride = max(1, len(G) // (len(fillers) + 1)) if fillers else 0

            def emit_S(g):
                q0, n, j, eoff, first, last, c0, c1 = G[g]
                bk = state["rb"] % 4
                state["rb"] += 1
                sbank[g] = bk
                kb.wait("pe", bank_free[bk])
                if g == 0:
                    kb.wait("pe", qcond)
                    kb.wait("pe", kcond)
                scond[g] = kb.op("pe", lambda e, bk=bk, j=j, q0=q0, c0=c0, c1=c1: e.matmul(
                    bank(bk)[:, c0:c1], lhsT=kT[:, j * 128:(j + 1) * 128], rhs=qT[:, q0 + c0:q0 + c1],
                    start=True, stop=True))

            for g in range(min(LA, len(G))):
                emit_S(g)
            mul_last = None
            for g in range(len(G)):
                q0, n, j, eoff, first, last, c0, c1 = G[g]
                bk = sbank[g]
                psl = (gb + g) % NP
                if first:
                    state["qbn"] += 1
                ob = 4 + 2 * (state["qbn"] % 2)
                kb.wait("act", scond[g])
                kb.wait("act", p_free[psl])
                ec = kb.op("act", lambda e, bk=bk, psl=psl, c0=c0, c1=c1: e.activation(
                    out=Pb[psl][:, c0:c1], in_=bank(bk)[:, c0:c1], func=AF.Exp, scale=SCALE))
                bank_free[bk] = ec
                kb.wait("dve", ec)
                kb.wait("dve", econd)
                mc = kb.op("dve", lambda e, psl=psl, eoff=eoff, c0=c0, c1=c1: e.tensor_tensor(
                    out=Pb[psl][:, c0:c1], in0=Pb[psl][:, c0:c1], in1=Eb[:, eoff + c0:eoff + c1], op=ALU.mult))
                mul_last = mc
                if g + LA < len(G):
                    emit_S(g + LA)
                if fillers and stride and (g % stride == stride - 1):
                    fillers.pop(0)()
                kb.wait("pe", mc)
                if first:
                    kb.wait("pe", bank_free[ob])
                    kb.wait("pe", bank_free[ob + 1])
                if g == 0:
                    kb.wait("pe", vcond)
                kb.op("pe", lambda e, j=j, psl=psl, first=first, last=last, ob=ob, c0=c0, c1=c1: e.matmul(
                    bank(ob)[:, c0:c1], lhsT=vtok[:, j * 128:(j + 1) * 128], rhs=Pb[psl][:, c0:c1],
                    start=first, stop=last, skip_group_check=True), ms=False)
                pv = kb.op("pe", lambda e, psl=psl, first=first, last=last, ob=ob, c0=c0, c1=c1: e.matmul(
                    bank(ob + 1)[:, c0:c1], lhsT=ones[:], rhs=Pb[psl][:, c0:c1], start=first, stop=last,
                    skip_group_check=True))
                p_free[psl] = pv
                if last:
                    pend_fin.append((g + 2, pv, n, q0, ob))
                while pend_fin and (pend_fin[0][0] <= g or g == len(G) - 1):
                    _, pvc, fn_, fq0, fob = pend_fin.pop(0)
                    kb.wait("act", pvc)
                    kb.wait("act", state["r_free"])
                    f1 = kb.op("act", lambda e, n=fn_, ob=fob: e.activation(
                        out=rbuf[:, 0:n], in_=bank(ob + 1)[:, 0:n], func=AF.Ln, bias=esink[:, scol:scol + 1]))
                    kb.wait("act", f1)
                    f2 = kb.op("act", lambda e, n=fn_: e.activation(
                        out=rbuf[:, 0:n], in_=rbuf[:, 0:n], func=AF.Exp, scale=-1.0))
                    kb.wait("dve", f2)
                    f3 = kb.op("dve", lambda e, n=fn_, ob=fob: e.tensor_tensor(
                        out=tbuf[:, 0:n], in0=bank(ob)[:, 0:n], in1=rbuf[:, 0:n], op=ALU.mult))
                    bank_free[fob] = f3
                    bank_free[fob + 1] = f3
                    state["r_free"] = f3
                    kb.wait("dve", f3)
                    kb.wait("dve", zcond)
                    kb.op("dve", lambda e, n=fn_, q0=fq0: e.tensor_tensor(
                        out=yT[:, hout, q0:q0 + n], in0=tbuf[:, 0:n], in1=sz2[:, q0:q0 + n], op=ALU.mult))
            state["gblk"] = gb + len(G)
            state["e_reader"] = mul_last

        def phase_heads(items):
            kb.barrier()
            grp = state["grp"]
            kvst = {}

            def kv_units(it, alt):
                st = {}
                units = []
                for tg in range(4):
                    units.append(lambda tg=tg: st.__setitem__("v", proj_fill(it["jobs"]["v"], "v", tg, vT)))
                for tg in range(4):
                    units.append(lambda tg=tg: st.__setitem__("k", proj_fill(it["jobs"]["k"], "k", tg, kTb[alt])))
                return st, units

            pending = None
            for i, it in enumerate(items):
                kb.wait("pool", state["e_reader"])
                ldc = kb.dma("pool", Eb[:, 0:it["ew"]], it["esrc"], e_ld)
                if it["need_exp"]:
                    kb.wait("act", ldc)
                    econd = kb.op("act", lambda e, w=it["ew"]: e.activation(
                        out=Eb[:, 0:w], in_=Eb[:, 0:w], func=AF.Exp))
                else:
                    econd = ldc
                jobs = it["jobs"]
                if "v" in jobs:
                    if pending is None or pending[0] != i:
                        grp += 1
                        st, units = kv_units(it, grp % 2)
                        pending = (i, st, units, grp % 2)
                    _, st, units, alt = pending
                    while units:
                        units.pop(0)()
                    kvst = {"k": st["k"], "alt": alt}
                    kvst["v"] = vtrans(st["v"], vtokb[alt])
                    pending = None
                qcond = proj(jobs["q"], "q", qT)
                zcond = proj(jobs["z"], "z", sz2)
                fillers = []
                if i + 1 < len(items) and "v" in items[i + 1]["jobs"]:
                    grp += 1
                    st2, units2 = kv_units(items[i + 1], grp % 2)
                    pending = (i + 1, st2, units2, grp % 2)
                    fillers = units2
                attention(it, econd, qcond, kvst["k"], kvst["v"], zcond, kTb[kvst["alt"]], vtokb[kvst["alt"]], fillers)
            state["grp"] = grp

        def phase_out(qjobs, res_src, dst):
            kb.barrier()
            for qd, job in enumerate(qjobs):
                slot = job["slot0"]
                kb.wait("pe", job["ld"])
                for t in range(NT):
                    n = state["p3n"]
                    state["p3n"] += 1
                    sl = n % 3
                    bk = n % 4
                    kb.wait("sp", state["res_free"][sl])
                    rc = kb.dma("sp", resb[sl], res_src[t * 128:(t + 1) * 128, qd * 512:(qd + 1) * 512], r_ld[sl])
                    kb.wait("pe", bank_free[bk])
                    for fc in range(16):
                        mc = kb.op("pe", lambda e, bk=bk, fc=fc, t=t, slot=slot: e.matmul(
                            bank(bk), lhsT=yT[:, fc, t * 128:(t + 1) * 128],
                            rhs=Wt[:, slot * 2048 + fc * 512: slot * 2048 + (fc + 1) * 512],
                            start=(fc == 0), stop=(fc == 15)), ms=(fc == 15))
                    if t == NT - 1:
                        ws.release(job, mc)
                    kb.wait("dve", mc)
                    kb.wait("dve", rc)
                    kb.wait("dve", state["xo_free"][sl])
                    dc = kb.op("dve", lambda e, bk=bk, sl=sl: e.tensor_tensor(
                        out=xob[sl], in0=bank(bk), in1=resb[sl], op=ALU.add))
                    bank_free[bk] = dc
                    state["res_free"][sl] = dc
                    kb.wait("sp", dc)
                    stc = kb.dma("sp", dst[t * 128:(t + 1) * 128, qd * 512:(qd + 1) * 512], xob[sl], xo_st[sl])
                    state["xo_free"][sl] = stc

        def wait_stores(eng):
            for s in xo_st + x_st:
                if s.count:
                    kb.wait(eng, (s, s.count))

        for (sq, L, items, qjobs) in plan:
            rows = slice(sq * S, (sq + 1) * S)
            first_layer = (L == do_layers[0])
            last_layer = (L == do_layers[-1])
            src = x_d[rows, :] if first_layer else x1_d
            dst = out_d[rows, :] if last_layer else x1_d
            wait_stores("sp")
            phase_norm(src, 0 if L == 0 else 1, None)
            phase_heads(items)
            phase_out(qjobs, src, dst)
            if last_layer and final_norm:
                wait_stores("sp")
                phase_norm(dst, 2, dst)
        kb.barrier()
        wait_stores("sp")

        with nc.Block() as block:
            @block.tensor
            def _(e):
                for f in kb.q["pe"]:
                    f(e)

            @block.scalar
            def _(e):
                for f in kb.q["act"]:
                    f(e)

            @block.vector
            def _(e):
                for f in kb.q["dve"]:
                    f(e)

            @block.gpsimd
            def _(e):
                for f in kb.q["pool"]:
                    f(e)

            @block.sync
            def _(e):
                for f in kb.q["sp"]:
                    f(e)
    return nc


def _w_in_layout(w):
    C = w.shape[1]
    g = C // 128
    return np.ascontiguousarray(w.reshape(16, 128, g, 128).transpose(2, 1, 0, 3)).reshape(g, 128, 2048)


def _w_out_layout(w):
    return np.ascontiguousarray(w.reshape(16, 128, 4, 512).transpose(2, 1, 0, 3)).reshape(4, 128, 8192)


def _alibi_tables():
    slopes = np.exp2(-8.0 * np.arange(1, 17, dtype=np.float64) / 16)
    p = np.arange(128)[:, None]
    EA = np.zeros((8, 128, 1152), np.float32)
    u = np.arange(1152)[None, :]
    dl = u - 512 - p
    for h in range(8):
        EA[h] = np.where(np.abs(dl) <= 128, np.exp(-slopes[h] * np.abs(dl)), 0.0)
    EB = np.zeros((8, 128, 2944), np.float32)
    u = np.arange(2944)[None, :]
    dl = u - 1408 - p
    ad = np.abs(dl)
    mult = (ad <= 64).astype(np.float64) + ((dl % 4 == 0) & (ad <= 256)) + ((dl % 16 == 0) & (ad <= 1024))
    for h in range(8):
        EB[h] = mult * np.exp(-slopes[8 + h] * ad)
    return EA, EB


def _rpb_strips(rpb):
    p = np.arange(128)
    rl = (p // 64)[:, None, None]
    kc = (p % 64)[:, None, None]
    i = np.arange(24)[None, :, None]
    qc = np.arange(64)[None, None, :]
    dr = 14 - (i - rl - 4) + 0 * qc
    dc = kc - qc + 15 + 0 * i
    c0 = np.clip(qc - 8, 0, 48)
    colok = (kc >= c0) & (kc < c0 + 16) & (i >= 0)
    out = np.full((16, 128, 2, 24, 64), NEGB, np.float32)
    for var, (lo, hi) in enumerate(((3, 10), (0, 14))):
        ok = colok & (dr >= lo) & (dr <= hi)
        drc = np.clip(dr, 0, 14)
        dcc = np.clip(dc, 0, 30)
        gathered = rpb[:, drc, dcc]
        out[:, :, var] = np.where(ok[None], gathered, np.float32(NEGB))
    return out.reshape(16, 128, 3072)


_CACHE = {}


def _get_nc(nseq, do_layers=(0, 1), final_norm=True):
    key = (nseq, tuple(do_layers), final_norm)
    if key not in _CACHE:
        _CACHE[key] = build(nseq, do_layers, final_norm)
    return _CACHE[key]


def _common_inputs(ln_ab, w_in_ab, sink_a, w_out_ab, ln_c, w_in_c, rpb_c, w_out_c, ln_f):
    EA, EB = _alibi_tables()
    return {
        "wab_in": _w_in_layout(np.asarray(w_in_ab[0], np.float32)),
        "wab_out": _w_out_layout(np.asarray(w_out_ab[0], np.float32)),
        "wc_in": _w_in_layout(np.asarray(w_in_c[0], np.float32)),
        "wc_out": _w_out_layout(np.asarray(w_out_c[0], np.float32)),
        "ln": np.ascontiguousarray(np.stack([np.asarray(ln_ab[0]), np.asarray(ln_c[0]), np.asarray(ln_f)]).astype(np.float32)),
        "sink": np.ascontiguousarray(np.asarray(sink_a[0], np.float32)),
        "ident": np.eye(128, dtype=np.float32),
        "EA": EA, "EB": EB,
        "BC": _rpb_strips(np.asarray(rpb_c[0], np.float32)),
    }


def kernel(x, ln_ab, w_in_ab, sink_a, w_out_ab, ln_c, w_in_c, rpb_c, w_out_c, ln_f):
    x = np.asarray(x, np.float32)
    B = x.shape[0]
    nseq = B // N_CORES
    common = _common_inputs(ln_ab, w_in_ab, sink_a, w_out_ab, ln_c, w_in_c, rpb_c, w_out_c, ln_f)
    nc = _get_nc(nseq)
    in_maps = []
    for c in range(N_CORES):
        m = dict(common)
        m["x"] = np.ascontiguousarray(x[c * nseq:(c + 1) * nseq].reshape(nseq * S, D))
        in_maps.append(m)
    res = run_bass_kernel_spmd(nc, in_maps, core_ids=list(range(N_CORES)))
    outs = [np.asarray(r["out"]).reshape(nseq, S, D) for r in res.results]
    return np.concatenate(outs, axis=0).astype(np.float32)
```

```python
import contextlib
import numpy as np
import concourse.bass as bass
import concourse.mybir as mybir
from concourse.bass_utils import run_bass_kernel_spmd

F32 = mybir.dt.float32
BF16 = mybir.dt.bfloat16
AF = mybir.ActivationFunctionType
ALU = mybir.AluOpType

D = 2048
S = 2048
NT = 16
SCALE = float(128 ** -0.5)
NEGB = -30000.0
N_CORES = 8
EW = 3072
NWS = 8

ENGS = ("pe", "act", "dve", "pool", "sp")


class Sem:
    def __init__(self, h):
        self.h = h
        self.count = 0


class KB:
    def __init__(self, nc, es):
        self.nc = nc
        self.es = es
        self.q = {e: [] for e in ENGS}
        self.waited = {}
        self.S = {}
        for e in ("pe", "act", "dve"):
            self.S[e] = self.new_sem("S_" + e)
        self.last = {e: None for e in ("pe", "act", "dve")}

    def new_sem(self, name):
        return Sem(self.es.enter_context(self.nc.semaphore(name)))

    def wait(self, eng, cond):
        if cond is None:
            return
        sem, val = cond
        assert val <= sem.count, (eng, val, sem.count)
        key = (eng, id(sem))
        if self.waited.get(key, 0) >= val:
            return
        self.waited[key] = val
        h = sem.h
        self.q[eng].append(lambda e: e.wait_ge(h, val))

    def op(self, eng, fn, ms=True):
        if not ms:
            self.q[eng].append(fn)
            return None
        s = self.S[eng]
        s.count += 1
        h = s.h
        self.q[eng].append(lambda e: fn(e).then_inc(h, 1))
        self.last[eng] = (s, s.count)
        return (s, s.count)

    def dma(self, eng, out, in_, sem):
        sem.count += 16
        h = sem.h
        self.q[eng].append(lambda e: e.dma_start(out=out, in_=in_).then_inc(h, 16))
        return (sem, sem.count)

    def barrier(self):
        for e in ("pe", "act", "dve", "sp"):
            for o in ("pe", "act", "dve"):
                if o != e:
                    self.wait(e, self.last[o])


class WStream:
    def __init__(self, kb, Wt):
        self.kb = kb
        self.Wt = Wt
        self.jobs = []
        self.pos = 0
        self.next = 0
        self.slot_free = [None] * NWS
        self.slot_pending = [False] * NWS
        self.ld = [kb.new_sem("wld%d" % i) for i in range(NWS)]

    def add_job(self, src, n):
        if n == 4 and self.pos % 4:
            self.pos = (self.pos + 4 - self.pos % 4) % NWS
        job = {"src": src, "n": n, "slot0": self.pos, "ld": None}
        self.pos = (self.pos + n) % NWS
        self.jobs.append(job)
        return job

    def try_issue(self):
        kb = self.kb
        while self.next < len(self.jobs):
            job = self.jobs[self.next]
            slots = range(job["slot0"], job["slot0"] + job["n"])
            if any(self.slot_pending[s] for s in slots):
                return
            for s in slots:
                kb.wait("pool", self.slot_free[s])
                self.slot_pending[s] = True
            dst = self.Wt[:, job["slot0"] * 2048:(job["slot0"] + job["n"]) * 2048]
            job["ld"] = kb.dma("pool", dst, job["src"], self.ld[job["slot0"]])
            self.next += 1

    def release(self, job, cond):
        for s in range(job["slot0"], job["slot0"] + job["n"]):
            self.slot_free[s] = cond
            self.slot_pending[s] = False
        self.try_issue()


def blocks_A():
    res = []
    for b in range(4):
        lst = []
        for j in range(4 * b - 1, 4 * b + 5):
            if 0 <= j < 16:
                c0 = (max(j - 1, 4 * b) - 4 * b) * 128
                c1 = (min(j + 1, 4 * b + 3) + 1 - 4 * b) * 128
                lst.append((j, 512 * b - 128 * j + 512, c0, c1))
        res.append((512 * b, 512, lst))
    return res


def blocks_B():
    res = []
    for b in range(4):
        js = [j for j in range(16) if -1535 <= 512 * b - 128 * j <= 1151]
        res.append((512 * b, 512, [(j, 512 * b - 128 * j + 1408, 0, 512) for j in js]))
    return res


def blocks_C():
    res = [(0, 320, [(j, 1536 + (11 - 2 * j) * 64, 0, 320) for j in range(4)])]
    for ra, nr, j0 in ((5, 8, 0), (13, 8, 4), (21, 7, 8)):
        res.append((64 * ra, 64 * nr, [(j, (11 - 2 * j + ra) * 64, 0, 64 * nr) for j in range(j0, j0 + 8)]))
    res.append((64 * 28, 256, [(j, 1536 + (11 - 2 * j + 28) * 64, 0, 256) for j in range(12, 16)]))
    return res


def build(nseq, do_layers=(0, 1), final_norm=True):
    nc = bass.Bass("TRN2", target_bir_lowering=False)
    R = nseq * S
    x_d = nc.dram_tensor("x", [R, D], F32, kind="ExternalInput").ap()
    out_d = nc.dram_tensor("out", [R, D], F32, kind="ExternalOutput").ap()
    x1_d = nc.dram_tensor("x1s", [S, D], F32).ap()
    wab_in = nc.dram_tensor("wab_in", [52, 128, 2048], F32, kind="ExternalInput").ap()
    wab_out = nc.dram_tensor("wab_out", [4, 128, 8192], F32, kind="ExternalInput").ap()
    wc_in = nc.dram_tensor("wc_in", [64, 128, 2048], F32, kind="ExternalInput").ap()
    wc_out = nc.dram_tensor("wc_out", [4, 128, 8192], F32, kind="ExternalInput").ap()
    ln_d = nc.dram_tensor("ln", [3, D], F32, kind="ExternalInput").ap()
    sink_d = nc.dram_tensor("sink", [8], F32, kind="ExternalInput").ap()
    ident_d = nc.dram_tensor("ident", [128, 128], F32, kind="ExternalInput").ap()
    EA_d = nc.dram_tensor("EA", [8, 128, 1152], F32, kind="ExternalInput").ap()
    EB_d = nc.dram_tensor("EB", [8, 128, 2944], F32, kind="ExternalInput").ap()
    BC_d = nc.dram_tensor("BC", [16, 128, 3072], F32, kind="ExternalInput").ap()

    with contextlib.ExitStack() as es:
        def sb(name, shape, dt):
            return es.enter_context(nc.sbuf_tensor(name, shape, dt))

        hnT = sb("hnT", [128, 16, 2048], BF16)
        yT = sb("yT", [128, 16, 2048], BF16)
        Wt = sb("Wt", [128, NWS * 2048], BF16)
        Eb = sb("Eb", [128, EW], BF16)
        arena = sb("arena", [128, 20480], BF16)
        ident = sb("ident_sb", [128, 128], BF16)
        ones = sb("ones_sb", [128, 128], BF16)
        stats = sb("stats", [128, 16], F32)
        esink = sb("esink", [128, 16], F32)
        epsT = sb("epsT", [128, 1], F32)
        onesf = sb("onesf", [128, 1], F32)
        ps = es.enter_context(nc.psum_tensor("ps", [128, 4096], F32))

        kb = KB(nc, es)
        ws = WStream(kb, Wt)

        def bank(b):
            return ps[:, 512 * b:512 * (b + 1)]

        def bank2_bf(b):
            return ps[:, 512 * b:512 * (b + 2)].bitcast(BF16)

        bank_free = [None] * 8

        def abf(off, n):
            return arena[:, off:off + n]

        def af32(off, n):
            return arena[:, off:off + 2 * n].bitcast(F32)

        qT = abf(0, 2048)
        kTb = [abf(2048, 2048), abf(4096, 2048)]
        vT = abf(6144, 2048)
        sz2 = abf(8192, 2048)
        vtokb = [abf(10240, 2048), abf(12288, 2048)]
        NP = 6
        Pb = [abf(14336 + 512 * i, 512) for i in range(NP)]
        thf = af32(17408, 512)
        rbuf = af32(18432, 512)
        tbuf = af32(19456, 512)
        xt = [af32(0, 2048), af32(4096, 2048), af32(8192, 2048)]
        hnb = [abf(8192, 2048), abf(10240, 2048)]
        lnB = af32(12288, 2048)
        junk = ps[:, 2048:4096]
        resb = [af32(1024 * i, 512) for i in range(3)]
        xob = [af32(3072 + 1024 * i, 512) for i in range(3)]

        x_ld = [kb.new_sem("xld%d" % i) for i in range(3)]
        x_st = [kb.new_sem("xst%d" % i) for i in range(3)]
        ln_ld = kb.new_sem("lnld")
        e_ld = kb.new_sem("eld")
        c_ld = kb.new_sem("cld")
        r_ld = [kb.new_sem("rld%d" % i) for i in range(3)]
        xo_st = [kb.new_sem("xost%d" % i) for i in range(3)]

        def layer_items(L):
            items = []
            if L == 0:
                for kv in range(2):
                    for gi in range(4):
                        h = kv * 4 + gi
                        items.append(dict(hout=h, gv=10 + kv if gi == 0 else None, gk=8 + kv if gi == 0 else None,
                                          gq=h, gz=36 + h, blocks=blocks_A(), esrc=EA_d[h], ew=1152,
                                          need_exp=False, scol=h))
                for hb in range(8):
                    items.append(dict(hout=8 + hb, gv=28 + hb, gk=20 + hb, gq=12 + hb, gz=44 + hb,
                                      blocks=blocks_B(), esrc=EB_d[hb], ew=2944, need_exp=False, scol=8))
            else:
                for h in range(16):
                    items.append(dict(hout=h, gv=32 + h, gk=16 + h, gq=h, gz=48 + h, blocks=blocks_C(),
                                      esrc=BC_d[h], ew=3072, need_exp=True, scol=8))
            return items

        plan = []
        for sq in range(nseq):
            for L in do_layers:
                w_in = wab_in if L == 0 else wc_in
                w_out = wab_out if L == 0 else wc_out
                items = layer_items(L)
                for it in items:
                    it["jobs"] = {}
                    for kind in ("v", "k", "q", "z"):
                        g = it["g" + kind]
                        if g is not None:
                            it["jobs"][kind] = ws.add_job(w_in[g], 1)
                qjobs = [ws.add_job(w_out[qd], 4) for qd in range(4)]
                plan.append((sq, L, items, qjobs))

        kb.dma("pool", ident[:], ident_d[:, :], c_ld)
        kb.dma("pool", esink[:, 0:8], sink_d.partition_broadcast(128), c_ld)
        kb.op("dve", lambda e: e.memset(ones[:], 1.0))
        kb.op("dve", lambda e: e.memset(epsT[:], 1e-5))
        kb.op("dve", lambda e: e.memset(onesf[:], 1.0))
        kb.op("dve", lambda e: e.memset(esink[:, 8:16], 0.0))
        kb.wait("act", (c_ld, c_ld.count))
        kb.op("act", lambda e: e.activation(out=esink[:, 0:8], in_=esink[:, 0:8], func=AF.Exp))
        kb.wait("pe", (c_ld, c_ld.count))
        ws.try_issue()
        state = {"e_reader": None, "p1n": 0, "p4n": 0, "xt_free": [None, None, None], "hn_free": [None, None],
                 "rb": 0, "sb": 0, "qbn": 0, "r_free": None, "grp": 0,
                 "p3n": 0, "res_free": [None] * 3, "xo_free": [None] * 3, "th_free": None}

        def phase_norm(src, ln_idx, dst):
            kb.barrier()
            lncond = kb.dma("sp", lnB, ln_d[ln_idx].partition_broadcast(128), ln_ld)
            pend = None
            for t in range(NT + 1):
                if t < NT:
                    if dst is None:
                        n = state["p1n"]
                        state["p1n"] += 1
                        sl = n % 2
                    else:
                        n = state["p4n"]
                        state["p4n"] += 1
                        sl = n % 3
                    kb.wait("sp", state["xt_free"][sl])
                    ldc = kb.dma("sp", xt[sl], src[t * 128:(t + 1) * 128, :], x_ld[sl])
                    kb.wait("act", ldc)
                    a1 = kb.op("act", lambda e, sl=sl: e.activation(
                        out=junk, in_=xt[sl], func=AF.Square, scale=float(2048 ** -0.5),
                        accum_out=stats[:, sl:sl + 1]))
                    kb.wait("act", a1)
                    a2 = kb.op("act", lambda e, sl=sl: e.activation(
                        out=stats[:, 3 + sl:4 + sl], in_=stats[:, sl:sl + 1], func=AF.Sqrt, bias=epsT[:, 0:1]))
                    kb.wait("dve", a2)
                    d1 = kb.op("dve", lambda e, sl=sl: e.reciprocal(out=stats[:, 6 + sl:7 + sl],
                                                                    in_=stats[:, 3 + sl:4 + sl]))
                    kb.wait("dve", d1)
                    kb.wait("dve", lncond)
                    if dst is None:
                        kb.wait("dve", state["hn_free"][sl])
                        d2 = kb.op("dve", lambda e, sl=sl: e.scalar_tensor_tensor(
                            out=hnb[sl], in0=xt[sl], scalar=stats[:, 6 + sl:7 + sl], in1=lnB,
                            op0=ALU.mult, op1=ALU.mult))
                        state["xt_free"][sl] = d2
                        kb.wait("pe", d2)
                        kb.wait("pe", bank_free[2 * sl])
                        kb.wait("pe", bank_free[2 * sl + 1])
                        pst = bank2_bf(2 * sl)
                        for fc in range(16):
                            pc = kb.op("pe", lambda e, sl=sl, fc=fc, pst=pst: e.transpose(
                                out=pst[:, fc * 128:(fc + 1) * 128], in_=hnb[sl][:, fc * 128:(fc + 1) * 128],
                                identity=ident[:]), ms=(fc == 15))
                        state["hn_free"][sl] = pc
                        cur = (sl, t, pc, pst)
                    else:
                        d2 = kb.op("dve", lambda e, sl=sl: e.scalar_tensor_tensor(
                            out=xt[sl], in0=xt[sl], scalar=stats[:, 6 + sl:7 + sl], in1=lnB,
                            op0=ALU.mult, op1=ALU.mult))
                        kb.wait("sp", d2)
                        stc = kb.dma("sp", dst[t * 128:(t + 1) * 128, :], xt[sl], x_st[sl])
                        state["xt_free"][sl] = stc
                        cur = None
                else:
                    cur = None
                if pend is not None:
                    sl0, t0, pc0, pst0 = pend
                    kb.wait("act", pc0)
                    ev = kb.op("act", lambda e, t0=t0, pst0=pst0: e.activation(
                        out=hnT[:, :, t0 * 128:(t0 + 1) * 128],
                        in_=pst0.rearrange("p (a b) -> p a b", a=16), func=AF.Copy))
                    bank_free[2 * sl0] = ev
                    bank_free[2 * sl0 + 1] = ev
                pend = cur

        def proj_fill(job, kind, tg, dst):
            slot = job["slot0"]
            kb.wait("pe", job["ld"])
            bk = state["rb"] % 4
            state["rb"] += 1
            kb.wait("pe", bank_free[bk])
            for kc in range(16):
                mc = kb.op("pe", lambda e, tg=tg, kc=kc, slot=slot, bk=bk: e.matmul(
                    bank(bk), lhsT=Wt[:, slot * 2048 + kc * 128: slot * 2048 + (kc + 1) * 128],
                    rhs=hnT[:, kc, tg * 512:(tg + 1) * 512], start=(kc == 0), stop=(kc == 15)),
                    ms=(kc == 15))
            if tg == 3:
                ws.release(job, mc)
            cols = slice(tg * 512, (tg + 1) * 512)
            if kind == "q":
                kb.wait("act", mc)
                ev = kb.op("act", lambda e, bk=bk, cols=cols: e.activation(
                    out=dst[:, cols], in_=bank(bk), func=AF.Copy))
            elif kind in ("k", "v"):
                kb.wait("dve", mc)
                ev = kb.op("dve", lambda e, bk=bk, cols=cols: e.tensor_copy(out=dst[:, cols], in_=bank(bk)))
            else:
                kb.wait("act", mc)
                kb.wait("act", state["th_free"])
                a1 = kb.op("act", lambda e, bk=bk: e.activation(out=thf, in_=bank(bk), func=AF.Exp, scale=-1.0))
                kb.wait("act", a1)
                a2 = kb.op("act", lambda e: e.activation(out=thf, in_=thf, func=AF.Ln, bias=onesf[:, 0:1]))
                kb.wait("act", a2)
                a3 = kb.op("act", lambda e: e.activation(out=thf, in_=thf, func=AF.Exp, scale=-1.0))
                kb.wait("dve", a3)
                ev = kb.op("dve", lambda e, bk=bk, cols=cols: e.tensor_tensor(
                    out=dst[:, cols], in0=thf, in1=bank(bk), op=ALU.mult))
                state["th_free"] = ev
            bank_free[bk] = ev
            return ev

        def proj(job, kind, dst):
            ev = None
            for tg in range(4):
                ev = proj_fill(job, kind, tg, dst)
            return ev

        def vtrans(vcond, vdst):
            kb.wait("pe", vcond)
            kb.wait("pe", bank_free[0])
            kb.wait("pe", bank_free[1])
            psv = bank2_bf(0)
            for j in range(16):
                pc = kb.op("pe", lambda e, j=j: e.transpose(
                    out=psv[:, j * 128:(j + 1) * 128], in_=vT[:, j * 128:(j + 1) * 128], identity=ident[:]),
                    ms=(j == 15))
            kb.wait("dve", pc)
            ev = kb.op("dve", lambda e: e.tensor_copy(out=vdst, in_=psv))
            bank_free[0] = ev
            bank_free[1] = ev
            return ev

        class FillStream:
            def __init__(self, tasks):
                self.tasks = tasks
                self.ti = 0
                self.kc = 0

            def remaining(self):
                return sum(t[6] for t in self.tasks[self.ti:]) - self.kc

            def emit(self, nmm):
                while nmm > 0 and self.ti < len(self.tasks):
                    job, kind, tg, dst, st, key, nops = self.tasks[self.ti]
                    bk = 3
                    if self.kc == 0:
                        if kind == "vt":
                            kb.wait("pe", st["v"])
                        else:
                            kb.wait("pe", job["ld"])
                        kb.wait("pe", bank_free[bk])
                    kc = self.kc
                    if kind == "vt":
                        psv = bank(bk).bitcast(BF16)
                        jt = tg * 8 + kc
                        mc = kb.op("pe", lambda e, kc=kc, jt=jt, psv=psv: e.transpose(
                            out=psv[:, kc * 128:(kc + 1) * 128], in_=vT[:, jt * 128:(jt + 1) * 128],
                            identity=ident[:]), ms=(kc == nops - 1))
                    else:
                        slot = job["slot0"]
                        mc = kb.op("pe", lambda e, tg=tg, kc=kc, slot=slot, bk=bk: e.matmul(
                            bank(bk), lhsT=Wt[:, slot * 2048 + kc * 128: slot * 2048 + (kc + 1) * 128],
                            rhs=hnT[:, kc, tg * 512:(tg + 1) * 512], start=(kc == 0), stop=(kc == 15)),
                            ms=(kc == 15))
                    self.kc += 1
                    nmm -= 1
                    if self.kc == nops:
                        kb.wait("dve", mc)
                        if kind == "vt":
                            ev = kb.op("dve", lambda e, tg=tg, dst=dst, psv=psv: e.tensor_copy(
                                out=dst[:, tg * 1024:(tg + 1) * 1024], in_=psv))
                        else:
                            if tg == 3:
                                ws.release(job, mc)
                            cols = slice(tg * 512, (tg + 1) * 512)
                            ev = kb.op("dve", lambda e, bk=bk, cols=cols, dst=dst: e.tensor_copy(
                                out=dst[:, cols], in_=bank(bk)))
                        bank_free[bk] = ev
                        st[key] = ev
                        self.ti += 1
                        self.kc = 0
                        return

            def flush(self):
                while self.ti < len(self.tasks):
                    self.emit(16)

        def attention(it, econd, qcond, kcond, vcond, zcond, kT, vtok, fillers):
            LA = 2
            G = []
            for bi, (q0, n, lst) in enumerate(it["blocks"]):
                for i, (j, eoff, c0, c1) in enumerate(lst):
                    G.append((q0, n, j, eoff, i == 0, i == len(lst) - 1, c0, c1))
            hout = it["hout"]
            scol = it["scol"]
            p_free = state.setdefault("p_free", [None] * NP)
            gb = state.setdefault("gblk", 0)
            scond = {}
            sbank = {}
            pend_fin = []

            def emit_S(g):
                q0, n, j, eoff, first, last, c0, c1 = G[g]
                bk = state["sb"] % 3
                state["sb"] += 1
                sbank[g] = bk
                kb.wait("pe", bank_free[bk])
                if g == 0:
                    kb.wait("pe", qcond)
                    kb.wait("pe", kcond)
                scond[g] = kb.op("pe", lambda e, bk=bk, j=j, q0=q0, c0=c0, c1=c1: e.matmul(
                    bank(bk)[:, c0:c1], lhsT=kT[:, j * 128:(j + 1) * 128], rhs=qT[:, q0 + c0:q0 + c1],
                    start=True, stop=True))

            for g in range(min(LA, len(G))):
                emit_S(g)
            mul_last = None
            for g in range(len(G)):
                q0, n, j, eoff, first, last, c0, c1 = G[g]
                bk = sbank[g]
                psl = (gb + g) % NP
                if first:
                    state["qbn"] += 1
                ob = 4 + 2 * (state["qbn"] % 2)
                kb.wait("act", scond[g])
                kb.wait("act", p_free[psl])
                ec = kb.op("act", lambda e, bk=bk, psl=psl, c0=c0, c1=c1: e.activation(
                    out=Pb[psl][:, c0:c1], in_=bank(bk)[:, c0:c1], func=AF.Exp, scale=SCALE))
                bank_free[bk] = ec
                kb.wait("dve", ec)
                kb.wait("dve", econd)
                mc = kb.op("dve", lambda e, psl=psl, eoff=eoff, c0=c0, c1=c1: e.tensor_tensor(
                    out=Pb[psl][:, c0:c1], in0=Pb[psl][:, c0:c1], in1=Eb[:, eoff + c0:eoff + c1], op=ALU.mult))
                mul_last = mc
                if g + LA < len(G):
                    emit_S(g + LA)
                if fillers is not None and fillers.remaining() > 0:
                    nb = len(G) - g
                    fillers.emit(-(-fillers.remaining() // nb))
                kb.wait("pe", mc)
                if first:
                    kb.wait("pe", bank_free[ob])
                    kb.wait("pe", bank_free[ob + 1])
                if g == 0:
                    kb.wait("pe", vcond)
                kb.op("pe", lambda e, j=j, psl=psl, first=first, last=last, ob=ob, c0=c0, c1=c1: e.matmul(
                    bank(ob)[:, c0:c1], lhsT=vtok[:, j * 128:(j + 1) * 128], rhs=Pb[psl][:, c0:c1],
                    start=first, stop=last, skip_group_check=True), ms=False)
                pv = kb.op("pe", lambda e, psl=psl, first=first, last=last, ob=ob, c0=c0, c1=c1: e.matmul(
                    bank(ob + 1)[:, c0:c1], lhsT=ones[:], rhs=Pb[psl][:, c0:c1], start=first, stop=last,
                    skip_group_check=True))
                p_free[psl] = pv
                if last:
                    pend_fin.append((g + 2, pv, n, q0, ob))
                while pend_fin and (pend_fin[0][0] <= g or g == len(G) - 1):
                    _, pvc, fn_, fq0, fob = pend_fin.pop(0)
                    kb.wait("act", pvc)
                    kb.wait("act", state["r_free"])
                    f1 = kb.op("act", lambda e, n=fn_, ob=fob: e.activation(
                        out=rbuf[:, 0:n], in_=bank(ob + 1)[:, 0:n], func=AF.Ln, bias=esink[:, scol:scol + 1]))
                    kb.wait("act", f1)
                    f2 = kb.op("act", lambda e, n=fn_: e.activation(
                        out=rbuf[:, 0:n], in_=rbuf[:, 0:n], func=AF.Exp, scale=-1.0))
                    kb.wait("dve", f2)
                    f3 = kb.op("dve", lambda e, n=fn_, ob=fob: e.tensor_tensor(
                        out=tbuf[:, 0:n], in0=bank(ob)[:, 0:n], in1=rbuf[:, 0:n], op=ALU.mult))
                    bank_free[fob] = f3
                    bank_free[fob + 1] = f3
                    state["r_free"] = f3
                    kb.wait("dve", f3)
                    kb.wait("dve", zcond)
                    kb.op("dve", lambda e, n=fn_, q0=fq0: e.tensor_tensor(
                        out=yT[:, hout, q0:q0 + n], in0=tbuf[:, 0:n], in1=sz2[:, q0:q0 + n], op=ALU.mult))
            state["gblk"] = gb + len(G)
            state["e_reader"] = mul_last

        def phase_heads(items):
            kb.barrier()
            grp = state["grp"]
            kvst = {}

            def kv_stream(it, alt):
                st = {}
                tasks = [(it["jobs"]["v"], "v", tg, vT, st, "v", 16) for tg in range(4)]
                tasks += [(None, "vt", hf, vtokb[alt], st, "vt", 8) for hf in range(2)]
                tasks += [(it["jobs"]["k"], "k", tg, kTb[alt], st, "k", 16) for tg in range(4)]
                return st, FillStream(tasks)

            pending = None
            for i, it in enumerate(items):
                kb.wait("pool", state["e_reader"])
                ldc = kb.dma("pool", Eb[:, 0:it["ew"]], it["esrc"], e_ld)
                if it["need_exp"]:
                    kb.wait("act", ldc)
                    econd = kb.op("act", lambda e, w=it["ew"]: e.activation(
                        out=Eb[:, 0:w], in_=Eb[:, 0:w], func=AF.Exp))
                else:
                    econd = ldc
                jobs = it["jobs"]
                if "v" in jobs:
                    if pending is None or pending[0] != i:
                        grp += 1
                        st, stream = kv_stream(it, grp % 2)
                        pending = (i, st, stream, grp % 2)
                    _, st, stream, alt = pending
                    stream.flush()
                    kvst = {"k": st["k"], "alt": alt, "v": st["vt"]}
                    pending = None
                qcond = proj(jobs["q"], "q", qT)
                zcond = proj(jobs["z"], "z", sz2)
                fillers = None
                if i + 1 < len(items) and "v" in items[i + 1]["jobs"]:
                    grp += 1
                    st2, stream2 = kv_stream(items[i + 1], grp % 2)
                    pending = (i + 1, st2, stream2, grp % 2)
                    fillers = stream2
                attention(it, econd, qcond, kvst["k"], kvst["v"], zcond, kTb[kvst["alt"]], vtokb[kvst["alt"]], fillers)
            state["grp"] = grp

        def phase_out(qjobs, res_src, dst):
            kb.barrier()
            for qd, job in enumerate(qjobs):
                slot = job["slot0"]
                kb.wait("pe", job["ld"])
                for t in range(NT):
                    n = state["p3n"]
                    state["p3n"] += 1
                    sl = n % 3
                    bk = n % 4
                    kb.wait("sp", state["res_free"][sl])
                    rc = kb.dma("sp", resb[sl], res_src[t * 128:(t + 1) * 128, qd * 512:(qd + 1) * 512], r_ld[sl])
                    kb.wait("pe", bank_free[bk])
                    for fc in range(16):
                        mc = kb.op("pe", lambda e, bk=bk, fc=fc, t=t, slot=slot: e.matmul(
                            bank(bk), lhsT=yT[:, fc, t * 128:(t + 1) * 128],
                            rhs=Wt[:, slot * 2048 + fc * 512: slot * 2048 + (fc + 1) * 512],
                            start=(fc == 0), stop=(fc == 15)), ms=(fc == 15))
                    if t == NT - 1:
                        ws.release(job, mc)
                    kb.wait("dve", mc)
                    kb.wait("dve", rc)
                    kb.wait("dve", state["xo_free"][sl])
                    dc = kb.op("dve", lambda e, bk=bk, sl=sl: e.tensor_tensor(
                        out=xob[sl], in0=bank(bk), in1=resb[sl], op=ALU.add))
                    bank_free[bk] = dc
                    state["res_free"][sl] = dc
                    kb.wait("sp", dc)
                    stc = kb.dma("sp", dst[t * 128:(t + 1) * 128, qd * 512:(qd + 1) * 512], xob[sl], xo_st[sl])
                    state["xo_free"][sl] = stc

        def wait_stores(eng):
            for s in xo_st + x_st:
                if s.count:
                    kb.wait(eng, (s, s.count))

        for (sq, L, items, qjobs) in plan:
            rows = slice(sq * S, (sq + 1) * S)
            first_layer = (L == do_layers[0])
            last_layer = (L == do_layers[-1])
            src = x_d[rows, :] if first_layer else x1_d
            dst = out_d[rows, :] if last_layer else x1_d
            wait_stores("sp")
            phase_norm(src, 0 if L == 0 else 1, None)
            phase_heads(items)
            phase_out(qjobs, src, dst)
            if last_layer and final_norm:
                wait_stores("sp")
                phase_norm(dst, 2, dst)
        kb.barrier()
        wait_stores("sp")

        with nc.Block() as block:
            @block.tensor
            def _(e):
                for f in kb.q["pe"]:
                    f(e)

            @block.scalar
            def _(e):
                for f in kb.q["act"]:
                    f(e)

            @block.vector
            def _(e):
                for f in kb.q["dve"]:
                    f(e)

            @block.gpsimd
            def _(e):
                for f in kb.q["pool"]:
                    f(e)

            @block.sync
            def _(e):
                for f in kb.q["sp"]:
                    f(e)
    return nc


def _w_in_layout(w):
    C = w.shape[1]
    g = C // 128
    return np.ascontiguousarray(w.reshape(16, 128, g, 128).transpose(2, 1, 0, 3)).reshape(g, 128, 2048)


def _w_out_layout(w):
    return np.ascontiguousarray(w.reshape(16, 128, 4, 512).transpose(2, 1, 0, 3)).reshape(4, 128, 8192)


def _alibi_tables():
    slopes = np.exp2(-8.0 * np.arange(1, 17, dtype=np.float64) / 16)
    p = np.arange(128)[:, None]
    EA = np.zeros((8, 128, 1152), np.float32)
    u = np.arange(1152)[None, :]
    dl = u - 512 - p
    for h in range(8):
        EA[h] = np.where(np.abs(dl) <= 128, np.exp(-slopes[h] * np.abs(dl)), 0.0)
    EB = np.zeros((8, 128, 2944), np.float32)
    u = np.arange(2944)[None, :]
    dl = u - 1408 - p
    ad = np.abs(dl)
    mult = (ad <= 64).astype(np.float64) + ((dl % 4 == 0) & (ad <= 256)) + ((dl % 16 == 0) & (ad <= 1024))
    for h in range(8):
        EB[h] = mult * np.exp(-slopes[8 + h] * ad)
    return EA, EB


def _rpb_strips(rpb):
    p = np.arange(128)
    rl = (p // 64)[:, None, None]
    kc = (p % 64)[:, None, None]
    i = np.arange(24)[None, :, None]
    qc = np.arange(64)[None, None, :]
    dr = 14 - (i - rl - 4) + 0 * qc
    dc = kc - qc + 15 + 0 * i
    c0 = np.clip(qc - 8, 0, 48)
    colok = (kc >= c0) & (kc < c0 + 16) & (i >= 0)
    out = np.full((16, 128, 2, 24, 64), NEGB, np.float32)
    for var, (lo, hi) in enumerate(((3, 10), (0, 14))):
        ok = colok & (dr >= lo) & (dr <= hi)
        drc = np.clip(dr, 0, 14)
        dcc = np.clip(dc, 0, 30)
        gathered = rpb[:, drc, dcc]
        out[:, :, var] = np.where(ok[None], gathered, np.float32(NEGB))
    return out.reshape(16, 128, 3072)


_CACHE = {}


def _get_nc(nseq, do_layers=(0, 1), final_norm=True):
    key = (nseq, tuple(do_layers), final_norm)
    if key not in _CACHE:
        _CACHE[key] = build(nseq, do_layers, final_norm)
    return _CACHE[key]


def _common_inputs(ln_ab, w_in_ab, sink_a, w_out_ab, ln_c, w_in_c, rpb_c, w_out_c, ln_f):
    EA, EB = _alibi_tables()
    return {
        "wab_in": _w_in_layout(np.asarray(w_in_ab[0], np.float32)),
        "wab_out": _w_out_layout(np.asarray(w_out_ab[0], np.float32)),
        "wc_in": _w_in_layout(np.asarray(w_in_c[0], np.float32)),
        "wc_out": _w_out_layout(np.asarray(w_out_c[0], np.float32)),
        "ln": np.ascontiguousarray(np.stack([np.asarray(ln_ab[0]), np.asarray(ln_c[0]), np.asarray(ln_f)]).astype(np.float32)),
        "sink": np.ascontiguousarray(np.asarray(sink_a[0], np.float32)),
        "ident": np.eye(128, dtype=np.float32),
        "EA": EA, "EB": EB,
        "BC": _rpb_strips(np.asarray(rpb_c[0], np.float32)),
    }


def kernel(x, ln_ab, w_in_ab, sink_a, w_out_ab, ln_c, w_in_c, rpb_c, w_out_c, ln_f):
    x = np.asarray(x, np.float32)
    B = x.shape[0]
    nseq = B // N_CORES
    common = _common_inputs(ln_ab, w_in_ab, sink_a, w_out_ab, ln_c, w_in_c, rpb_c, w_out_c, ln_f)
    nc = _get_nc(nseq)
    in_maps = []
    for c in range(N_CORES):
        m = dict(common)
        m["x"] = np.ascontiguousarray(x[c * nseq:(c + 1) * nseq].reshape(nseq * S, D))
        in_maps.append(m)
    res = run_bass_kernel_spmd(nc, in_maps, core_ids=list(range(N_CORES)))
    outs = [np.asarray(r["out"]).reshape(nseq, S, D) for r in res.results]
    return np.concatenate(outs, axis=0).astype(np.float32)
```

```python
import contextlib
import numpy as np
import concourse.bass as bass
import concourse.mybir as mybir
from concourse.bass_utils import run_bass_kernel_spmd

F32 = mybir.dt.float32
BF16 = mybir.dt.bfloat16
AF = mybir.ActivationFunctionType
ALU = mybir.AluOpType

D = 2048
S = 2048
NT = 16
SCALE = float(128 ** -0.5)
NEGB = -30000.0
N_CORES = 8
EW = 3072
NWS = 8

ENGS = ("pe", "act", "dve", "pool", "sp")


class Sem:
    def __init__(self, h):
        self.h = h
        self.count = 0


class KB:
    def __init__(self, nc, es):
        self.nc = nc
        self.es = es
        self.q = {e: [] for e in ENGS}
        self.waited = {}
        self.S = {}
        for e in ("pe", "act", "dve"):
            self.S[e] = self.new_sem("S_" + e)
        self.last = {e: None for e in ("pe", "act", "dve")}

    def new_sem(self, name):
        return Sem(self.es.enter_context(self.nc.semaphore(name)))

    def wait(self, eng, cond):
        if cond is None:
            return
        sem, val = cond
        assert val <= sem.count, (eng, val, sem.count)
        key = (eng, id(sem))
        if self.waited.get(key, 0) >= val:
            return
        self.waited[key] = val
        h = sem.h
        self.q[eng].append(lambda e: e.wait_ge(h, val))

    def op(self, eng, fn, ms=True):
        if not ms:
            self.q[eng].append(fn)
            return None
        s = self.S[eng]
        s.count += 1
        h = s.h
        self.q[eng].append(lambda e: fn(e).then_inc(h, 1))
        self.last[eng] = (s, s.count)
        return (s, s.count)

    def dma(self, eng, out, in_, sem):
        sem.count += 16
        h = sem.h
        self.q[eng].append(lambda e: e.dma_start(out=out, in_=in_).then_inc(h, 16))
        return (sem, sem.count)

    def barrier(self):
        for e in ("pe", "act", "dve", "sp"):
            for o in ("pe", "act", "dve"):
                if o != e:
                    self.wait(e, self.last[o])


class WStream:
    def __init__(self, kb, Wt):
        self.kb = kb
        self.Wt = Wt
        self.jobs = []
        self.pos = 0
        self.next = 0
        self.slot_free = [None] * NWS
        self.slot_pending = [False] * NWS
        self.ld = [kb.new_sem("wld%d" % i) for i in range(NWS)]

    def add_job(self, src, n):
        if n == 4 and self.pos % 4:
            self.pos = (self.pos + 4 - self.pos % 4) % NWS
        job = {"src": src, "n": n, "slot0": self.pos, "ld": None}
        self.pos = (self.pos + n) % NWS
        self.jobs.append(job)
        return job

    def try_issue(self):
        kb = self.kb
        while self.next < len(self.jobs):
            job = self.jobs[self.next]
            slots = range(job["slot0"], job["slot0"] + job["n"])
            if any(self.slot_pending[s] for s in slots):
                return
            for s in slots:
                kb.wait("pool", self.slot_free[s])
                self.slot_pending[s] = True
            dst = self.Wt[:, job["slot0"] * 2048:(job["slot0"] + job["n"]) * 2048]
            job["ld"] = kb.dma("pool", dst, job["src"], self.ld[job["slot0"]])
            self.next += 1

    def release(self, job, cond):
        for s in range(job["slot0"], job["slot0"] + job["n"]):
            self.slot_free[s] = cond
            self.slot_pending[s] = False
        self.try_issue()


def blocks_A():
    res = []
    for b in range(4):
        lst = []
        for j in range(4 * b - 1, 4 * b + 5):
            if 0 <= j < 16:
                c0 = (max(j - 1, 4 * b) - 4 * b) * 128
                c1 = (min(j + 1, 4 * b + 3) + 1 - 4 * b) * 128
                lst.append((j, 512 * b - 128 * j + 512, c0, c1))
        res.append((512 * b, 512, lst))
    return res


def blocks_B():
    res = []
    for b in range(4):
        js = [j for j in range(16) if -1535 <= 512 * b - 128 * j <= 1151]
        res.append((512 * b, 512, [(j, 512 * b - 128 * j + 1408, 0, 512) for j in js]))
    return res


def blocks_C():
    res = [(0, 320, [(j, 1536 + (11 - 2 * j) * 64, 0, 320) for j in range(4)])]
    for ra, nr, j0 in ((5, 8, 0), (13, 8, 4), (21, 7, 8)):
        res.append((64 * ra, 64 * nr, [(j, (11 - 2 * j + ra) * 64, 0, 64 * nr) for j in range(j0, j0 + 8)]))
    res.append((64 * 28, 256, [(j, 1536 + (11 - 2 * j + 28) * 64, 0, 256) for j in range(12, 16)]))
    return res


def build(nseq, do_layers=(0, 1), final_norm=True):
    nc = bass.Bass("TRN2", target_bir_lowering=False)
    R = nseq * S
    x_d = nc.dram_tensor("x", [R, D], F32, kind="ExternalInput").ap()
    out_d = nc.dram_tensor("out", [R, D], F32, kind="ExternalOutput").ap()
    x1_d = nc.dram_tensor("x1s", [S, D], F32).ap()
    wab_in = nc.dram_tensor("wab_in", [52, 128, 2048], F32, kind="ExternalInput").ap()
    wab_out = nc.dram_tensor("wab_out", [4, 128, 8192], F32, kind="ExternalInput").ap()
    wc_in = nc.dram_tensor("wc_in", [64, 128, 2048], F32, kind="ExternalInput").ap()
    wc_out = nc.dram_tensor("wc_out", [4, 128, 8192], F32, kind="ExternalInput").ap()
    ln_d = nc.dram_tensor("ln", [3, D], F32, kind="ExternalInput").ap()
    sink_d = nc.dram_tensor("sink", [8], F32, kind="ExternalInput").ap()
    ident_d = nc.dram_tensor("ident", [128, 128], F32, kind="ExternalInput").ap()
    EA_d = nc.dram_tensor("EA", [8, 128, 1152], F32, kind="ExternalInput").ap()
    EB_d = nc.dram_tensor("EB", [8, 128, 2944], F32, kind="ExternalInput").ap()
    BC_d = nc.dram_tensor("BC", [16, 128, 3072], F32, kind="ExternalInput").ap()

    with contextlib.ExitStack() as es:
        def sb(name, shape, dt):
            return es.enter_context(nc.sbuf_tensor(name, shape, dt))

        bufA = sb("bufA", [128, 32768], BF16)
        bufB = sb("bufB", [128, 32768], BF16)
        bufA3 = bufA[:, :].rearrange("p (a b) -> p a b", a=16)
        bufB3 = bufB[:, :].rearrange("p (a b) -> p a b", a=16)
        role = {"hnT": bufA3, "yT": bufB3, "hflat": bufA, "yflat": bufB}
        Wt = sb("Wt", [128, NWS * 2048], BF16)
        Eb = sb("Eb", [128, EW], BF16)
        arena = sb("arena", [128, 20480], BF16)
        ident = sb("ident_sb", [128, 128], BF16)
        ones = sb("ones_sb", [128, 128], BF16)
        stats = sb("stats", [128, 16], F32)
        esink = sb("esink", [128, 16], F32)
        epsT = sb("epsT", [128, 1], F32)
        onesf = sb("onesf", [128, 1], F32)
        ps = es.enter_context(nc.psum_tensor("ps", [128, 4096], F32))

        kb = KB(nc, es)
        ws = WStream(kb, Wt)

        def bank(b):
            return ps[:, 512 * b:512 * (b + 1)]

        def bank2_bf(b):
            return ps[:, 512 * b:512 * (b + 2)].bitcast(BF16)

        bank_free = [None] * 8

        def abf(off, n):
            return arena[:, off:off + n]

        def af32(off, n):
            return arena[:, off:off + 2 * n].bitcast(F32)

        qT = abf(0, 2048)
        kTb = [abf(2048, 2048), abf(4096, 2048)]
        vT = abf(6144, 2048)
        sz2 = abf(8192, 2048)
        vtokb = [abf(10240, 2048), abf(12288, 2048)]
        NP = 6
        Pb = [abf(14336 + 512 * i, 512) for i in range(NP)]
        thf = af32(17408, 512)
        rbuf = af32(18432, 512)
        tbuf = af32(19456, 512)
        xt = [af32(0, 2048), af32(4096, 2048), af32(8192, 2048), af32(16384, 2048)]
        hnb = [abf(8192, 2048), abf(10240, 2048)]
        lnB = af32(12288, 2048)
        junk = ps[:, 2048:4096]
        resb = [af32(1024 * i, 512) for i in range(3)]
        xob = [af32(3072 + 1024 * i, 512) for i in range(3)]

        x_ld = [kb.new_sem("xld%d" % i) for i in range(4)]
        x_st = [kb.new_sem("xst%d" % i) for i in range(4)]
        ln_ld = kb.new_sem("lnld")
        e_ld = kb.new_sem("eld")
        c_ld = kb.new_sem("cld")
        r_ld = [kb.new_sem("rld%d" % i) for i in range(3)]
        xo_st = [kb.new_sem("xost%d" % i) for i in range(3)]

        def layer_items(L):
            items = []
            if L == 0:
                for kv in range(2):
                    for gi in range(4):
                        h = kv * 4 + gi
                        items.append(dict(hout=h, gv=10 + kv if gi == 0 else None, gk=8 + kv if gi == 0 else None,
                                          gq=h, gz=36 + h, blocks=blocks_A(), esrc=EA_d[h], ew=1152,
                                          need_exp=False, scol=h))
                for hb in range(8):
                    items.append(dict(hout=8 + hb, gv=28 + hb, gk=20 + hb, gq=12 + hb, gz=44 + hb,
                                      blocks=blocks_B(), esrc=EB_d[hb], ew=2944, need_exp=False, scol=8))
            else:
                for h in range(16):
                    items.append(dict(hout=h, gv=32 + h, gk=16 + h, gq=h, gz=48 + h, blocks=blocks_C(),
                                      esrc=BC_d[h], ew=3072, need_exp=True, scol=8))
            return items

        plan = []
        for sq in range(nseq):
            for L in do_layers:
                w_in = wab_in if L == 0 else wc_in
                w_out = wab_out if L == 0 else wc_out
                items = layer_items(L)
                for it in items:
                    it["jobs"] = {}
                    for kind in ("v", "k", "q", "z"):
                        g = it["g" + kind]
                        if g is not None:
                            it["jobs"][kind] = ws.add_job(w_in[g], 1)
                qjobs = [ws.add_job(w_out[qd], 4) for qd in range(2)]
                plan.append((sq, L, items, qjobs))

        kb.dma("pool", ident[:], ident_d[:, :], c_ld)
        kb.dma("pool", esink[:, 0:8], sink_d.partition_broadcast(128), c_ld)
        kb.op("dve", lambda e: e.memset(ones[:], 1.0))
        kb.op("dve", lambda e: e.memset(epsT[:], 1e-5))
        kb.op("dve", lambda e: e.memset(onesf[:], 1.0))
        kb.op("dve", lambda e: e.memset(esink[:, 8:16], 0.0))
        kb.wait("act", (c_ld, c_ld.count))
        kb.op("act", lambda e: e.activation(out=esink[:, 0:8], in_=esink[:, 0:8], func=AF.Exp))
        kb.wait("pe", (c_ld, c_ld.count))
        ws.try_issue()
        state = {"e_reader": None, "p1n": 0, "p4n": 0, "xt_free": [None, None, None, None], "hn_free": [None, None],
                 "rb": 0, "sb": 0, "qbn": 0, "r_free": None, "grp": 0,
                 "p3n": 0, "res_free": [None] * 3, "xo_free": [None] * 3, "th_free": None}

        def phase_norm(src, ln_idx, dst):
            kb.barrier()
            lncond = kb.dma("sp", lnB, ln_d[ln_idx].partition_broadcast(128), ln_ld)
            sids = [0, 1, 3] if dst is None else [0, 1, 2, 3]
            ns = len(sids)
            ldc = {}

            def issue_load(t):
                sid = sids[t % ns]
                kb.wait("sp", state["xt_free"][sid])
                ldc[t] = kb.dma("sp", xt[sid], src[t * 128:(t + 1) * 128, :], x_ld[sid])

            for t in range(min(ns - 1, NT)):
                issue_load(t)
            pend = None
            for t in range(NT + 1):
                if t < NT:
                    if t + ns - 1 < NT:
                        issue_load(t + ns - 1)
                    sid = sids[t % ns]
                    hs = t % 2
                    kb.wait("act", ldc[t])
                    a1 = kb.op("act", lambda e, sid=sid: e.activation(
                        out=junk, in_=xt[sid], func=AF.Square, scale=float(2048 ** -0.5),
                        accum_out=stats[:, sid:sid + 1]))
                    kb.wait("act", a1)
                    a2 = kb.op("act", lambda e, sid=sid: e.activation(
                        out=stats[:, 4 + sid:5 + sid], in_=stats[:, sid:sid + 1], func=AF.Sqrt, bias=epsT[:, 0:1]))
                    kb.wait("dve", a2)
                    d1 = kb.op("dve", lambda e, sid=sid: e.reciprocal(out=stats[:, 8 + sid:9 + sid],
                                                                      in_=stats[:, 4 + sid:5 + sid]))
                    kb.wait("dve", d1)
                    kb.wait("dve", lncond)
                    if dst is None:
                        kb.wait("dve", state["hn_free"][hs])
                        d2 = kb.op("dve", lambda e, sid=sid, hs=hs: e.scalar_tensor_tensor(
                            out=hnb[hs], in0=xt[sid], scalar=stats[:, 8 + sid:9 + sid], in1=lnB,
                            op0=ALU.mult, op1=ALU.mult))
                        state["xt_free"][sid] = d2
                        kb.wait("pe", d2)
                        kb.wait("pe", bank_free[2 * hs])
                        kb.wait("pe", bank_free[2 * hs + 1])
                        pst = bank2_bf(2 * hs)
                        for fc in range(16):
                            pc = kb.op("pe", lambda e, hs=hs, fc=fc, pst=pst: e.transpose(
                                out=pst[:, fc * 128:(fc + 1) * 128], in_=hnb[hs][:, fc * 128:(fc + 1) * 128],
                                identity=ident[:]), ms=(fc == 15))
                        state["hn_free"][hs] = pc
                        cur = (hs, t, pc, pst)
                    else:
                        d2 = kb.op("dve", lambda e, sid=sid: e.scalar_tensor_tensor(
                            out=xt[sid], in0=xt[sid], scalar=stats[:, 8 + sid:9 + sid], in1=lnB,
                            op0=ALU.mult, op1=ALU.mult))
                        kb.wait("sp", d2)
                        stc = kb.dma("sp", dst[t * 128:(t + 1) * 128, :], xt[sid], x_st[sid])
                        state["xt_free"][sid] = stc
                        cur = None
                else:
                    cur = None
                if pend is not None:
                    hs0, t0, pc0, pst0 = pend
                    kb.wait("act", pc0)
                    hdst = role["hnT"][:, :, t0 * 128:(t0 + 1) * 128]
                    ev = kb.op("act", lambda e, hdst=hdst, pst0=pst0: e.activation(
                        out=hdst, in_=pst0.rearrange("p (a b) -> p a b", a=16), func=AF.Copy))
                    bank_free[2 * hs0] = ev
                    bank_free[2 * hs0 + 1] = ev
                pend = cur

        def proj_fill(job, kind, tg, dst):
            slot = job["slot0"]
            kb.wait("pe", job["ld"])
            bk = state["rb"] % 4
            state["rb"] += 1
            kb.wait("pe", bank_free[bk])
            for kc in range(16):
                rhs_ap = role["hnT"][:, kc, tg * 512:(tg + 1) * 512]
                mc = kb.op("pe", lambda e, rhs_ap=rhs_ap, kc=kc, slot=slot, bk=bk: e.matmul(
                    bank(bk), lhsT=Wt[:, slot * 2048 + kc * 128: slot * 2048 + (kc + 1) * 128],
                    rhs=rhs_ap, start=(kc == 0), stop=(kc == 15)),
                    ms=(kc == 15))
            if tg == 3:
                ws.release(job, mc)
            cols = slice(tg * 512, (tg + 1) * 512)
            if kind == "q":
                kb.wait("act", mc)
                ev = kb.op("act", lambda e, bk=bk, cols=cols: e.activation(
                    out=dst[:, cols], in_=bank(bk), func=AF.Copy))
            elif kind in ("k", "v"):
                kb.wait("dve", mc)
                ev = kb.op("dve", lambda e, bk=bk, cols=cols: e.tensor_copy(out=dst[:, cols], in_=bank(bk)))
            else:
                kb.wait("act", mc)
                kb.wait("act", state["th_free"])
                a1 = kb.op("act", lambda e, bk=bk: e.activation(out=thf, in_=bank(bk), func=AF.Exp, scale=-1.0))
                kb.wait("act", a1)
                a2 = kb.op("act", lambda e: e.activation(out=thf, in_=thf, func=AF.Ln, bias=onesf[:, 0:1]))
                kb.wait("act", a2)
                a3 = kb.op("act", lambda e: e.activation(out=thf, in_=thf, func=AF.Exp, scale=-1.0))
                kb.wait("dve", a3)
                ev = kb.op("dve", lambda e, bk=bk, cols=cols: e.tensor_tensor(
                    out=dst[:, cols], in0=thf, in1=bank(bk), op=ALU.mult))
                state["th_free"] = ev
            bank_free[bk] = ev
            return ev

        def proj(job, kind, dst):
            ev = None
            for tg in range(4):
                ev = proj_fill(job, kind, tg, dst)
            return ev

        def vtrans(vcond, vdst):
            kb.wait("pe", vcond)
            kb.wait("pe", bank_free[0])
            kb.wait("pe", bank_free[1])
            psv = bank2_bf(0)
            for j in range(16):
                pc = kb.op("pe", lambda e, j=j: e.transpose(
                    out=psv[:, j * 128:(j + 1) * 128], in_=vT[:, j * 128:(j + 1) * 128], identity=ident[:]),
                    ms=(j == 15))
            kb.wait("dve", pc)
            ev = kb.op("dve", lambda e: e.tensor_copy(out=vdst, in_=psv))
            bank_free[0] = ev
            bank_free[1] = ev
            return ev

        class FillStream:
            def __init__(self, tasks):
                self.tasks = tasks
                self.ti = 0
                self.kc = 0

            def remaining(self):
                return sum(t[6] for t in self.tasks[self.ti:]) - self.kc

            def emit(self, nmm):
                while nmm > 0 and self.ti < len(self.tasks):
                    job, kind, tg, dst, st, key, nops = self.tasks[self.ti]
                    bk = 3
                    if self.kc == 0:
                        if kind == "vt":
                            kb.wait("pe", st["v"])
                        else:
                            kb.wait("pe", job["ld"])
                        kb.wait("pe", bank_free[bk])
                    kc = self.kc
                    if kind == "vt":
                        psv = bank(bk).bitcast(BF16)
                        jt = tg * 8 + kc
                        mc = kb.op("pe", lambda e, kc=kc, jt=jt, psv=psv: e.transpose(
                            out=psv[:, kc * 128:(kc + 1) * 128], in_=vT[:, jt * 128:(jt + 1) * 128],
                            identity=ident[:]), ms=(kc == nops - 1))
                    else:
                        slot = job["slot0"]
                        rhs_ap = role["hnT"][:, kc, tg * 512:(tg + 1) * 512]
                        mc = kb.op("pe", lambda e, rhs_ap=rhs_ap, kc=kc, slot=slot, bk=bk: e.matmul(
                            bank(bk), lhsT=Wt[:, slot * 2048 + kc * 128: slot * 2048 + (kc + 1) * 128],
                            rhs=rhs_ap, start=(kc == 0), stop=(kc == 15)),
                            ms=(kc == 15))
                    self.kc += 1
                    nmm -= 1
                    if self.kc == nops:
                        kb.wait("dve", mc)
                        if kind == "vt":
                            ev = kb.op("dve", lambda e, tg=tg, dst=dst, psv=psv: e.tensor_copy(
                                out=dst[:, tg * 1024:(tg + 1) * 1024], in_=psv))
                        else:
                            if tg == 3:
                                ws.release(job, mc)
                            cols = slice(tg * 512, (tg + 1) * 512)
                            ev = kb.op("dve", lambda e, bk=bk, cols=cols, dst=dst: e.tensor_copy(
                                out=dst[:, cols], in_=bank(bk)))
                        bank_free[bk] = ev
                        st[key] = ev
                        self.ti += 1
                        self.kc = 0
                        return

            def flush(self):
                while self.ti < len(self.tasks):
                    self.emit(16)

        def attention(it, econd, qcond, kcond, vcond, zcond, kT, vtok, fillers):
            LA = 2
            G = []
            for bi, (q0, n, lst) in enumerate(it["blocks"]):
                for i, (j, eoff, c0, c1) in enumerate(lst):
                    G.append((q0, n, j, eoff, i == 0, i == len(lst) - 1, c0, c1))
            hout = it["hout"]
            scol = it["scol"]
            p_free = state.setdefault("p_free", [None] * NP)
            gb = state.setdefault("gblk", 0)
            scond = {}
            sbank = {}
            pend_fin = []

            def emit_S(g):
                q0, n, j, eoff, first, last, c0, c1 = G[g]
                bk = state["sb"] % 3
                state["sb"] += 1
                sbank[g] = bk
                kb.wait("pe", bank_free[bk])
                if g == 0:
                    kb.wait("pe", qcond)
                    kb.wait("pe", kcond)
                scond[g] = kb.op("pe", lambda e, bk=bk, j=j, q0=q0, c0=c0, c1=c1: e.matmul(
                    bank(bk)[:, c0:c1], lhsT=kT[:, j * 128:(j + 1) * 128], rhs=qT[:, q0 + c0:q0 + c1],
                    start=True, stop=True))

            for g in range(min(LA, len(G))):
                emit_S(g)
            mul_last = None
            for g in range(len(G)):
                q0, n, j, eoff, first, last, c0, c1 = G[g]
                bk = sbank[g]
                psl = (gb + g) % NP
                if first:
                    state["qbn"] += 1
                ob = 4 + 2 * (state["qbn"] % 2)
                kb.wait("act", scond[g])
                kb.wait("act", p_free[psl])
                ec = kb.op("act", lambda e, bk=bk, psl=psl, c0=c0, c1=c1: e.activation(
                    out=Pb[psl][:, c0:c1], in_=bank(bk)[:, c0:c1], func=AF.Exp, scale=SCALE))
                bank_free[bk] = ec
                kb.wait("dve", ec)
                kb.wait("dve", econd)
                mc = kb.op("dve", lambda e, psl=psl, eoff=eoff, c0=c0, c1=c1: e.tensor_tensor(
                    out=Pb[psl][:, c0:c1], in0=Pb[psl][:, c0:c1], in1=Eb[:, eoff + c0:eoff + c1], op=ALU.mult))
                mul_last = mc
                if g + LA < len(G):
                    emit_S(g + LA)
                if fillers is not None and fillers.remaining() > 0:
                    nb = len(G) - g
                    fillers.emit(-(-fillers.remaining() // nb))
                kb.wait("pe", mc)
                if first:
                    kb.wait("pe", bank_free[ob])
                    kb.wait("pe", bank_free[ob + 1])
                if g == 0:
                    kb.wait("pe", vcond)
                kb.op("pe", lambda e, j=j, psl=psl, first=first, last=last, ob=ob, c0=c0, c1=c1: e.matmul(
                    bank(ob)[:, c0:c1], lhsT=vtok[:, j * 128:(j + 1) * 128], rhs=Pb[psl][:, c0:c1],
                    start=first, stop=last, skip_group_check=True), ms=False)
                pv = kb.op("pe", lambda e, psl=psl, first=first, last=last, ob=ob, c0=c0, c1=c1: e.matmul(
                    bank(ob + 1)[:, c0:c1], lhsT=ones[:], rhs=Pb[psl][:, c0:c1], start=first, stop=last,
                    skip_group_check=True))
                p_free[psl] = pv
                if last:
                    pend_fin.append((g + 2, pv, n, q0, ob))
                while pend_fin and (pend_fin[0][0] <= g or g == len(G) - 1):
                    _, pvc, fn_, fq0, fob = pend_fin.pop(0)
                    kb.wait("act", pvc)
                    kb.wait("act", state["r_free"])
                    f1 = kb.op("act", lambda e, n=fn_, ob=fob: e.activation(
                        out=rbuf[:, 0:n], in_=bank(ob + 1)[:, 0:n], func=AF.Ln, bias=esink[:, scol:scol + 1]))
                    kb.wait("act", f1)
                    f2 = kb.op("act", lambda e, n=fn_: e.activation(
                        out=rbuf[:, 0:n], in_=rbuf[:, 0:n], func=AF.Exp, scale=-1.0))
                    kb.wait("dve", f2)
                    f3 = kb.op("dve", lambda e, n=fn_, ob=fob: e.tensor_tensor(
                        out=tbuf[:, 0:n], in0=bank(ob)[:, 0:n], in1=rbuf[:, 0:n], op=ALU.mult))
                    bank_free[fob] = f3
                    bank_free[fob + 1] = f3
                    state["r_free"] = f3
                    kb.wait("dve", f3)
                    kb.wait("dve", zcond)
                    ydst = role["yT"][:, hout, fq0:fq0 + fn_]
                    kb.op("dve", lambda e, n=fn_, q0=fq0, ydst=ydst: e.tensor_tensor(
                        out=ydst, in0=tbuf[:, 0:n], in1=sz2[:, q0:q0 + n], op=ALU.mult))
            state["gblk"] = gb + len(G)
            state["e_reader"] = mul_last

        def phase_heads(items, after_last_proj=None):
            kb.barrier()
            grp = state["grp"]
            kvst = {}

            def kv_stream(it, alt):
                st = {}
                tasks = [(it["jobs"]["v"], "v", tg, vT, st, "v", 16) for tg in range(4)]
                tasks += [(None, "vt", hf, vtokb[alt], st, "vt", 8) for hf in range(2)]
                tasks += [(it["jobs"]["k"], "k", tg, kTb[alt], st, "k", 16) for tg in range(4)]
                return st, FillStream(tasks)

            pending = None
            for i, it in enumerate(items):
                kb.wait("pool", state["e_reader"])
                ldc = kb.dma("pool", Eb[:, 0:it["ew"]], it["esrc"], e_ld)
                if it["need_exp"]:
                    kb.wait("act", ldc)
                    econd = kb.op("act", lambda e, w=it["ew"]: e.activation(
                        out=Eb[:, 0:w], in_=Eb[:, 0:w], func=AF.Exp))
                else:
                    econd = ldc
                jobs = it["jobs"]
                if "v" in jobs:
                    if pending is None or pending[0] != i:
                        grp += 1
                        st, stream = kv_stream(it, grp % 2)
                        pending = (i, st, stream, grp % 2)
                    _, st, stream, alt = pending
                    stream.flush()
                    kvst = {"k": st["k"], "alt": alt, "v": st["vt"]}
                    pending = None
                qcond = proj(jobs["q"], "q", qT)
                zcond = proj(jobs["z"], "z", sz2)
                if i == len(items) - 1 and after_last_proj is not None:
                    after_last_proj()
                fillers = None
                if i + 1 < len(items) and "v" in items[i + 1]["jobs"]:
                    grp += 1
                    st2, stream2 = kv_stream(items[i + 1], grp % 2)
                    pending = (i + 1, st2, stream2, grp % 2)
                    fillers = stream2
                attention(it, econd, qcond, kvst["k"], kvst["v"], zcond, kTb[kvst["alt"]], vtokb[kvst["alt"]], fillers)
            state["grp"] = grp

        xr = [af32(0, 2048), af32(4096, 2048), af32(8192, 2048)]
        hn3 = [abf(12288, 2048), abf(14336, 2048)]
        lnB3 = af32(16384, 2048)
        wq_ld = kb.new_sem("wqld")
        xr_ld = [kb.new_sem("xrld%d" % i) for i in range(3)]
        xr_st = [kb.new_sem("xrst%d" % i) for i in range(3)]
        state["xr_free"] = [[], [], []]

        def issue_wq(w_out, dead_flat):
            kb.wait("pool", kb.last["pe"])
            conds = []
            for i in range(2):
                conds.append(kb.dma("pool", dead_flat[:, i * 8192:(i + 1) * 8192], w_out[2 + i], wq_ld))
            state["wq"] = (conds, dead_flat)

        def phase_out(qjobs, res_src, mode, x1_dst, out_dst, ln_idx):
            kb.barrier()
            yT3 = role["yT"]
            wq_conds, dead_flat = state["wq"]
            if mode != "plain":
                lncond = kb.dma("sp", lnB3, ln_d[ln_idx].partition_broadcast(128), ln_ld)
            ldc = {}

            def issue_load(t):
                sl = t % 3
                for c_ in state["xr_free"][sl]:
                    kb.wait("sp", c_)
                ldc[t] = kb.dma("sp", xr[sl], res_src[t * 128:(t + 1) * 128, :], xr_ld[sl])

            def rhs_q(qd, fc):
                if qd < 2:
                    slot = qjobs[qd]["slot0"]
                    return Wt[:, slot * 2048 + fc * 512: slot * 2048 + (fc + 1) * 512]
                return dead_flat[:, (qd - 2) * 8192 + fc * 512:(qd - 2) * 8192 + (fc + 1) * 512]

            issue_load(0)
            issue_load(1)
            pend = None
            for t in range(NT + 1):
                cur = None
                if t < NT:
                    if t + 2 < NT:
                        issue_load(t + 2)
                    sl = t % 3
                    hs = t % 2
                    addc = None
                    for qd in range(4):
                        if t == 0:
                            kb.wait("pe", qjobs[qd]["ld"] if qd < 2 else wq_conds[qd - 2])
                        kb.wait("pe", bank_free[qd])
                        for fc in range(16):
                            lhs_ap = yT3[:, fc, t * 128:(t + 1) * 128]
                            rhs_ap = rhs_q(qd, fc)
                            mc = kb.op("pe", lambda e, qd=qd, fc=fc, lhs_ap=lhs_ap, rhs_ap=rhs_ap: e.matmul(
                                bank(qd), lhsT=lhs_ap, rhs=rhs_ap, start=(fc == 0), stop=(fc == 15)),
                                ms=(fc == 15))
                        if t == NT - 1 and qd < 2:
                            ws.release(qjobs[qd], mc)
                        kb.wait("dve", mc)
                        kb.wait("dve", ldc[t])
                        addc = kb.op("dve", lambda e, qd=qd, sl=sl: e.tensor_tensor(
                            out=xr[sl][:, qd * 512:(qd + 1) * 512], in0=bank(qd),
                            in1=xr[sl][:, qd * 512:(qd + 1) * 512], op=ALU.add))
                        bank_free[qd] = addc
                    frees = []
                    if mode in ("mid", "plain"):
                        kb.wait("act", addc)
                        dstd = x1_dst if mode == "mid" else out_dst
                        frees.append(kb.dma("act", dstd[t * 128:(t + 1) * 128, :], xr[sl], xr_st[sl]))
                    if mode != "plain":
                        kb.wait("act", addc)
                        if mode == "mid":
                            kb.wait("act", state["hn_free"][hs])
                        jk = hn3[hs] if mode == "mid" else hn3[0]
                        a1 = kb.op("act", lambda e, sl=sl, jk=jk: e.activation(
                            out=jk, in_=xr[sl], func=AF.Square, scale=float(2048 ** -0.5),
                            accum_out=stats[:, sl:sl + 1]))
                        kb.wait("act", a1)
                        a2 = kb.op("act", lambda e, sl=sl: e.activation(
                            out=stats[:, 4 + sl:5 + sl], in_=stats[:, sl:sl + 1], func=AF.Sqrt, bias=epsT[:, 0:1]))
                        kb.wait("dve", a2)
                        d1 = kb.op("dve", lambda e, sl=sl: e.reciprocal(out=stats[:, 8 + sl:9 + sl],
                                                                        in_=stats[:, 4 + sl:5 + sl]))
                        kb.wait("dve", d1)
                        kb.wait("dve", lncond)
                        if mode == "mid":
                            d2 = kb.op("dve", lambda e, sl=sl, hs=hs: e.scalar_tensor_tensor(
                                out=hn3[hs], in0=xr[sl], scalar=stats[:, 8 + sl:9 + sl], in1=lnB3,
                                op0=ALU.mult, op1=ALU.mult))
                            frees.append(d2)
                            cur = (hs, t, d2)
                        else:
                            d2 = kb.op("dve", lambda e, sl=sl: e.scalar_tensor_tensor(
                                out=xr[sl], in0=xr[sl], scalar=stats[:, 8 + sl:9 + sl], in1=lnB3,
                                op0=ALU.mult, op1=ALU.mult))
                            kb.wait("act", d2)
                            frees.append(kb.dma("act", out_dst[t * 128:(t + 1) * 128, :], xr[sl], xr_st[sl]))
                    state["xr_free"][sl] = frees
                if pend is not None:
                    hs0, t0, d20 = pend
                    kb.wait("pe", d20)
                    kb.wait("pe", bank_free[4 + 2 * hs0])
                    kb.wait("pe", bank_free[5 + 2 * hs0])
                    pst = bank2_bf(4 + 2 * hs0)
                    for fc in range(16):
                        pc = kb.op("pe", lambda e, hs0=hs0, fc=fc, pst=pst: e.transpose(
                            out=pst[:, fc * 128:(fc + 1) * 128], in_=hn3[hs0][:, fc * 128:(fc + 1) * 128],
                            identity=ident[:]), ms=(fc == 15))
                    state["hn_free"][hs0] = pc
                    kb.wait("act", pc)
                    hdst = yT3[:, :, t0 * 128:(t0 + 1) * 128]
                    ev = kb.op("act", lambda e, hdst=hdst, pst=pst: e.activation(
                        out=hdst, in_=pst.rearrange("p (a b) -> p a b", a=16), func=AF.Copy))
                    bank_free[4 + 2 * hs0] = ev
                    bank_free[5 + 2 * hs0] = ev
                pend = cur

        def wait_stores(eng):
            for s in xo_st + x_st + xr_st:
                if s.count:
                    kb.wait(eng, (s, s.count))

        for (sq, L, items, qjobs) in plan:
            rows = slice(sq * S, (sq + 1) * S)
            first_layer = (L == do_layers[0])
            last_layer = (L == do_layers[-1])
            if L == 0:
                role.update(hnT=bufA3, yT=bufB3, hflat=bufA, yflat=bufB)
            else:
                role.update(hnT=bufB3, yT=bufA3, hflat=bufB, yflat=bufA)
            src = x_d[rows, :] if first_layer else x1_d
            w_out = wab_out if L == 0 else wc_out
            if first_layer:
                wait_stores("sp")
                phase_norm(src, 0 if L == 0 else 1, None)
            hflat = role["hflat"]
            phase_heads(items, after_last_proj=lambda w_out=w_out, hflat=hflat: issue_wq(w_out, hflat))
            wait_stores("sp")
            if not last_layer:
                phase_out(qjobs, src, "mid", x1_d, None, 1)
            elif final_norm:
                phase_out(qjobs, src, "final", None, out_d[rows, :], 2)
            else:
                phase_out(qjobs, src, "plain", None, out_d[rows, :], 0)
        kb.barrier()
        wait_stores("sp")

        with nc.Block() as block:
            @block.tensor
            def _(e):
                for f in kb.q["pe"]:
                    f(e)

            @block.scalar
            def _(e):
                for f in kb.q["act"]:
                    f(e)

            @block.vector
            def _(e):
                for f in kb.q["dve"]:
                    f(e)

            @block.gpsimd
            def _(e):
                for f in kb.q["pool"]:
                    f(e)

            @block.sync
            def _(e):
                for f in kb.q["sp"]:
                    f(e)
    return nc


def _w_in_layout(w):
    C = w.shape[1]
    g = C // 128
    return np.ascontiguousarray(w.reshape(16, 128, g, 128).transpose(2, 1, 0, 3)).reshape(g, 128, 2048)


def _w_out_layout(w):
    return np.ascontiguousarray(w.reshape(16, 128, 4, 512).transpose(2, 1, 0, 3)).reshape(4, 128, 8192)


def _alibi_tables():
    slopes = np.exp2(-8.0 * np.arange(1, 17, dtype=np.float64) / 16)
    p = np.arange(128)[:, None]
    EA = np.zeros((8, 128, 1152), np.float32)
    u = np.arange(1152)[None, :]
    dl = u - 512 - p
    for h in range(8):
        EA[h] = np.where(np.abs(dl) <= 128, np.exp(-slopes[h] * np.abs(dl)), 0.0)
    EB = np.zeros((8, 128, 2944), np.float32)
    u = np.arange(2944)[None, :]
    dl = u - 1408 - p
    ad = np.abs(dl)
    mult = (ad <= 64).astype(np.float64) + ((dl % 4 == 0) & (ad <= 256)) + ((dl % 16 == 0) & (ad <= 1024))
    for h in range(8):
        EB[h] = mult * np.exp(-slopes[8 + h] * ad)
    return EA, EB


def _rpb_strips(rpb):
    p = np.arange(128)
    rl = (p // 64)[:, None, None]
    kc = (p % 64)[:, None, None]
    i = np.arange(24)[None, :, None]
    qc = np.arange(64)[None, None, :]
    dr = 14 - (i - rl - 4) + 0 * qc
    dc = kc - qc + 15 + 0 * i
    c0 = np.clip(qc - 8, 0, 48)
    colok = (kc >= c0) & (kc < c0 + 16) & (i >= 0)
    out = np.full((16, 128, 2, 24, 64), NEGB, np.float32)
    for var, (lo, hi) in enumerate(((3, 10), (0, 14))):
        ok = colok & (dr >= lo) & (dr <= hi)
        drc = np.clip(dr, 0, 14)
        dcc = np.clip(dc, 0, 30)
        gathered = rpb[:, drc, dcc]
        out[:, :, var] = np.where(ok[None], gathered, np.float32(NEGB))
    return out.reshape(16, 128, 3072)


_CACHE = {}


def _get_nc(nseq, do_layers=(0, 1), final_norm=True):
    key = (nseq, tuple(do_layers), final_norm)
    if key not in _CACHE:
        _CACHE[key] = build(nseq, do_layers, final_norm)
    return _CACHE[key]


def _common_inputs(ln_ab, w_in_ab, sink_a, w_out_ab, ln_c, w_in_c, rpb_c, w_out_c, ln_f):
    EA, EB = _alibi_tables()
    return {
        "wab_in": _w_in_layout(np.asarray(w_in_ab[0], np.float32)),
        "wab_out": _w_out_layout(np.asarray(w_out_ab[0], np.float32)),
        "wc_in": _w_in_layout(np.asarray(w_in_c[0], np.float32)),
        "wc_out": _w_out_layout(np.asarray(w_out_c[0], np.float32)),
        "ln": np.ascontiguousarray(np.stack([np.asarray(ln_ab[0]), np.asarray(ln_c[0]), np.asarray(ln_f)]).astype(np.float32)),
        "sink": np.ascontiguousarray(np.asarray(sink_a[0], np.float32)),
        "ident": np.eye(128, dtype=np.float32),
        "EA": EA, "EB": EB,
        "BC": _rpb_strips(np.asarray(rpb_c[0], np.float32)),
    }


def kernel(x, ln_ab, w_in_ab, sink_a, w_out_ab, ln_c, w_in_c, rpb_c, w_out_c, ln_f):
    x = np.asarray(x, np.float32)
    B = x.shape[0]
    nseq = B // N_CORES
    common = _common_inputs(ln_ab, w_in_ab, sink_a, w_out_ab, ln_c, w_in_c, rpb_c, w_out_c, ln_f)
    nc = _get_nc(nseq)
    in_maps = []
    for c in range(N_CORES):
        m = dict(common)
        m["x"] = np.ascontiguousarray(x[c * nseq:(c + 1) * nseq].reshape(nseq * S, D))
        in_maps.append(m)
    res = run_bass_kernel_spmd(nc, in_maps, core_ids=list(range(N_CORES)))
    outs = [np.asarray(r["out"]).reshape(nseq, S, D) for r in res.results]
    return np.concatenate(outs, axis=0).astype(np.float32)
```

```python
import contextlib
import numpy as np
import concourse.bass as bass
import concourse.mybir as mybir
from concourse.bass_utils import run_bass_kernel_spmd

F32 = mybir.dt.float32
BF16 = mybir.dt.bfloat16
AF = mybir.ActivationFunctionType
ALU = mybir.AluOpType

D = 2048
S = 2048
NT = 16
SCALE = float(128 ** -0.5)
NEGB = -30000.0
N_CORES = 8
EW = 3072
NWS = 8

ENGS = ("pe", "act", "dve", "pool", "sp")


class Sem:
    def __init__(self, h):
        self.h = h
        self.count = 0


class KB:
    def __init__(self, nc, es):
        self.nc = nc
        self.es = es
        self.q = {e: [] for e in ENGS}
        self.waited = {}
        self.S = {}
        for e in ("pe", "act", "dve"):
            self.S[e] = self.new_sem("S_" + e)
        self.last = {e: None for e in ("pe", "act", "dve")}

    def new_sem(self, name):
        return Sem(self.es.enter_context(self.nc.semaphore(name)))

    def wait(self, eng, cond):
        if cond is None:
            return
        sem, val = cond
        assert val <= sem.count, (eng, val, sem.count)
        key = (eng, id(sem))
        if self.waited.get(key, 0) >= val:
            return
        self.waited[key] = val
        h = sem.h
        self.q[eng].append(lambda e: e.wait_ge(h, val))

    def op(self, eng, fn, ms=True):
        if not ms:
            self.q[eng].append(fn)
            return None
        s = self.S[eng]
        s.count += 1
        h = s.h
        self.q[eng].append(lambda e: fn(e).then_inc(h, 1))
        self.last[eng] = (s, s.count)
        return (s, s.count)

    def dma(self, eng, out, in_, sem):
        sem.count += 16
        h = sem.h
        self.q[eng].append(lambda e: e.dma_start(out=out, in_=in_).then_inc(h, 16))
        return (sem, sem.count)

    def barrier(self):
        for e in ("pe", "act", "dve", "sp"):
            for o in ("pe", "act", "dve"):
                if o != e:
                    self.wait(e, self.last[o])


class WStream:
    def __init__(self, kb, Wt):
        self.kb = kb
        self.Wt = Wt
        self.jobs = []
        self.pos = 0
        self.next = 0
        self.slot_free = [None] * NWS
        self.slot_pending = [False] * NWS
        self.ld = [kb.new_sem("wld%d" % i) for i in range(NWS)]

    def add_job(self, src, n):
        if n == 4 and self.pos % 4:
            self.pos = (self.pos + 4 - self.pos % 4) % NWS
        job = {"src": src, "n": n, "slot0": self.pos, "ld": None}
        self.pos = (self.pos + n) % NWS
        self.jobs.append(job)
        return job

    def try_issue(self):
        kb = self.kb
        while self.next < len(self.jobs):
            job = self.jobs[self.next]
            slots = range(job["slot0"], job["slot0"] + job["n"])
            if any(self.slot_pending[s] for s in slots):
                return
            for s in slots:
                kb.wait("pool", self.slot_free[s])
                self.slot_pending[s] = True
            dst = self.Wt[:, job["slot0"] * 2048:(job["slot0"] + job["n"]) * 2048]
            job["ld"] = kb.dma("pool", dst, job["src"], self.ld[job["slot0"]])
            self.next += 1

    def release(self, job, cond):
        for s in range(job["slot0"], job["slot0"] + job["n"]):
            self.slot_free[s] = cond
            self.slot_pending[s] = False
        self.try_issue()


def blocks_A():
    res = []
    for b in range(4):
        lst = []
        for j in range(4 * b - 1, 4 * b + 5):
            if 0 <= j < 16:
                c0 = (max(j - 1, 4 * b) - 4 * b) * 128
                c1 = (min(j + 1, 4 * b + 3) + 1 - 4 * b) * 128
                lst.append((j, 512 * b - 128 * j + 512, c0, c1))
        res.append((512 * b, 512, lst))
    return res


def blocks_B():
    res = []
    for b in range(4):
        js = [j for j in range(16) if -1535 <= 512 * b - 128 * j <= 1151]
        res.append((512 * b, 512, [(j, 512 * b - 128 * j + 1408, 0, 512) for j in js]))
    return res


def blocks_C():
    res = [(0, 320, [(j, 1536 + (11 - 2 * j) * 64, 0, 320) for j in range(4)])]
    for ra, nr, j0 in ((5, 8, 0), (13, 8, 4), (21, 7, 8)):
        res.append((64 * ra, 64 * nr, [(j, (11 - 2 * j + ra) * 64, 0, 64 * nr) for j in range(j0, j0 + 8)]))
    res.append((64 * 28, 256, [(j, 1536 + (11 - 2 * j + 28) * 64, 0, 256) for j in range(12, 16)]))
    return res


def build(nseq, do_layers=(0, 1), final_norm=True):
    nc = bass.Bass("TRN2", target_bir_lowering=False)
    R = nseq * S
    x_d = nc.dram_tensor("x", [R, D], F32, kind="ExternalInput").ap()
    out_d = nc.dram_tensor("out", [R, D], F32, kind="ExternalOutput").ap()
    x1_d = nc.dram_tensor("x1s", [S, D], F32).ap()
    wab_in = nc.dram_tensor("wab_in", [52, 128, 2048], F32, kind="ExternalInput").ap()
    wab_out = nc.dram_tensor("wab_out", [4, 128, 8192], F32, kind="ExternalInput").ap()
    wc_in = nc.dram_tensor("wc_in", [64, 128, 2048], F32, kind="ExternalInput").ap()
    wc_out = nc.dram_tensor("wc_out", [4, 128, 8192], F32, kind="ExternalInput").ap()
    ln_d = nc.dram_tensor("ln", [3, D], F32, kind="ExternalInput").ap()
    sink_d = nc.dram_tensor("sink", [8], F32, kind="ExternalInput").ap()
    ident_d = nc.dram_tensor("ident", [128, 128], F32, kind="ExternalInput").ap()
    EA_d = nc.dram_tensor("EA", [8, 128, 1152], F32, kind="ExternalInput").ap()
    EB_d = nc.dram_tensor("EB", [8, 128, 2944], F32, kind="ExternalInput").ap()
    BC_d = nc.dram_tensor("BC", [16, 128, 3072], F32, kind="ExternalInput").ap()

    with contextlib.ExitStack() as es:
        def sb(name, shape, dt):
            return es.enter_context(nc.sbuf_tensor(name, shape, dt))

        bufA = sb("bufA", [128, 32768], BF16)
        bufB = sb("bufB", [128, 32768], BF16)
        bufA3 = bufA[:, :].rearrange("p (a b) -> p a b", a=16)
        bufB3 = bufB[:, :].rearrange("p (a b) -> p a b", a=16)
        role = {"hnT": bufA3, "yT": bufB3, "hflat": bufA, "yflat": bufB}
        Wt = sb("Wt", [128, NWS * 2048], BF16)
        Eb = sb("Eb", [128, EW], BF16)
        arena = sb("arena", [128, 20480], BF16)
        ident = sb("ident_sb", [128, 128], BF16)
        ones = sb("ones_sb", [128, 128], BF16)
        stats = sb("stats", [128, 16], F32)
        esink = sb("esink", [128, 16], F32)
        epsT = sb("epsT", [128, 1], F32)
        onesf = sb("onesf", [128, 1], F32)
        ps = es.enter_context(nc.psum_tensor("ps", [128, 4096], F32))

        kb = KB(nc, es)
        ws = WStream(kb, Wt)

        def bank(b):
            return ps[:, 512 * b:512 * (b + 1)]

        def bank2_bf(b):
            return ps[:, 512 * b:512 * (b + 2)].bitcast(BF16)

        bank_free = [None] * 8

        def abf(off, n):
            return arena[:, off:off + n]

        def af32(off, n):
            return arena[:, off:off + 2 * n].bitcast(F32)

        qT = abf(0, 2048)
        kTb = [abf(2048, 2048), abf(4096, 2048)]
        vT = abf(6144, 2048)
        sz2 = abf(8192, 2048)
        vtokb = [abf(10240, 2048), abf(12288, 2048)]
        NP = 6
        Pb = [abf(14336 + 512 * i, 512) for i in range(NP)]
        thf = af32(17408, 512)
        rbuf = af32(18432, 512)
        tbuf = af32(19456, 512)
        xt = [af32(0, 2048), af32(4096, 2048), af32(8192, 2048), af32(16384, 2048)]
        hnb = [abf(8192, 2048), abf(10240, 2048)]
        lnB = af32(12288, 2048)
        junk = ps[:, 2048:4096]
        resb = [af32(1024 * i, 512) for i in range(3)]
        xob = [af32(3072 + 1024 * i, 512) for i in range(3)]

        x_ld = [kb.new_sem("xld%d" % i) for i in range(4)]
        x_st = [kb.new_sem("xst%d" % i) for i in range(4)]
        ln_ld = kb.new_sem("lnld")
        e_ld = kb.new_sem("eld")
        c_ld = kb.new_sem("cld")
        r_ld = [kb.new_sem("rld%d" % i) for i in range(3)]
        xo_st = [kb.new_sem("xost%d" % i) for i in range(3)]

        def layer_items(L):
            items = []
            if L == 0:
                for kv in range(2):
                    for gi in range(4):
                        h = kv * 4 + gi
                        items.append(dict(hout=h, gv=10 + kv if gi == 0 else None, gk=8 + kv if gi == 0 else None,
                                          gq=h, gz=36 + h, blocks=blocks_A(), esrc=EA_d[h], ew=1152,
                                          need_exp=False, scol=h))
                for hb in range(8):
                    items.append(dict(hout=8 + hb, gv=28 + hb, gk=20 + hb, gq=12 + hb, gz=44 + hb,
                                      blocks=blocks_B(), esrc=EB_d[hb], ew=2944, need_exp=False, scol=8))
            else:
                for h in range(16):
                    items.append(dict(hout=h, gv=32 + h, gk=16 + h, gq=h, gz=48 + h, blocks=blocks_C(),
                                      esrc=BC_d[h], ew=3072, need_exp=True, scol=8))
            return items

        plan = []
        for sq in range(nseq):
            for L in do_layers:
                w_in = wab_in if L == 0 else wc_in
                w_out = wab_out if L == 0 else wc_out
                items = layer_items(L)
                for it in items:
                    it["jobs"] = {}
                    for kind in ("v", "k", "z", "q"):
                        g = it["g" + kind]
                        if g is not None:
                            it["jobs"][kind] = ws.add_job(w_in[g], 1)
                qjobs = [ws.add_job(w_out[qd], 4) for qd in range(2)]
                plan.append((sq, L, items, qjobs))

        kb.dma("pool", ident[:], ident_d[:, :], c_ld)
        kb.dma("pool", esink[:, 0:8], sink_d.partition_broadcast(128), c_ld)
        kb.op("dve", lambda e: e.memset(ones[:], 1.0))
        kb.op("dve", lambda e: e.memset(epsT[:], 1e-5))
        kb.op("dve", lambda e: e.memset(onesf[:], 1.0))
        kb.op("dve", lambda e: e.memset(esink[:, 8:16], 0.0))
        kb.wait("act", (c_ld, c_ld.count))
        kb.op("act", lambda e: e.activation(out=esink[:, 0:8], in_=esink[:, 0:8], func=AF.Exp))
        kb.wait("pe", (c_ld, c_ld.count))
        ws.try_issue()
        state = {"e_reader": None, "p1n": 0, "p4n": 0, "xt_free": [None, None, None, None], "hn_free": [None, None],
                 "rb": 0, "sb": 0, "qbn": 0, "r_free": None, "grp": 0,
                 "p3n": 0, "res_free": [None] * 3, "xo_free": [None] * 3, "th_free": None}

        def phase_norm(src, ln_idx, dst):
            kb.barrier()
            lncond = kb.dma("sp", lnB, ln_d[ln_idx].partition_broadcast(128), ln_ld)
            sids = [0, 1, 3] if dst is None else [0, 1, 2, 3]
            ns = len(sids)
            ldc = {}

            def issue_load(t):
                sid = sids[t % ns]
                kb.wait("sp", state["xt_free"][sid])
                ldc[t] = kb.dma("sp", xt[sid], src[t * 128:(t + 1) * 128, :], x_ld[sid])

            for t in range(min(ns - 1, NT)):
                issue_load(t)
            pend = None
            for t in range(NT + 1):
                if t < NT:
                    if t + ns - 1 < NT:
                        issue_load(t + ns - 1)
                    sid = sids[t % ns]
                    hs = t % 2
                    kb.wait("act", ldc[t])
                    a1 = kb.op("act", lambda e, sid=sid: e.activation(
                        out=junk, in_=xt[sid], func=AF.Square, scale=float(2048 ** -0.5),
                        accum_out=stats[:, sid:sid + 1]))
                    kb.wait("act", a1)
                    a2 = kb.op("act", lambda e, sid=sid: e.activation(
                        out=stats[:, 4 + sid:5 + sid], in_=stats[:, sid:sid + 1], func=AF.Sqrt, bias=epsT[:, 0:1]))
                    kb.wait("dve", a2)
                    d1 = kb.op("dve", lambda e, sid=sid: e.reciprocal(out=stats[:, 8 + sid:9 + sid],
                                                                      in_=stats[:, 4 + sid:5 + sid]))
                    kb.wait("dve", d1)
                    kb.wait("dve", lncond)
                    if dst is None:
                        kb.wait("dve", state["hn_free"][hs])
                        d2 = kb.op("dve", lambda e, sid=sid, hs=hs: e.scalar_tensor_tensor(
                            out=hnb[hs], in0=xt[sid], scalar=stats[:, 8 + sid:9 + sid], in1=lnB,
                            op0=ALU.mult, op1=ALU.mult))
                        state["xt_free"][sid] = d2
                        kb.wait("pe", d2)
                        kb.wait("pe", bank_free[2 * hs])
                        kb.wait("pe", bank_free[2 * hs + 1])
                        pst = bank2_bf(2 * hs)
                        for fc in range(16):
                            pc = kb.op("pe", lambda e, hs=hs, fc=fc, pst=pst: e.transpose(
                                out=pst[:, fc * 128:(fc + 1) * 128], in_=hnb[hs][:, fc * 128:(fc + 1) * 128],
                                identity=ident[:]), ms=(fc == 15))
                        state["hn_free"][hs] = pc
                        cur = (hs, t, pc, pst)
                    else:
                        d2 = kb.op("dve", lambda e, sid=sid: e.scalar_tensor_tensor(
                            out=xt[sid], in0=xt[sid], scalar=stats[:, 8 + sid:9 + sid], in1=lnB,
                            op0=ALU.mult, op1=ALU.mult))
                        kb.wait("sp", d2)
                        stc = kb.dma("sp", dst[t * 128:(t + 1) * 128, :], xt[sid], x_st[sid])
                        state["xt_free"][sid] = stc
                        cur = None
                else:
                    cur = None
                if pend is not None:
                    hs0, t0, pc0, pst0 = pend
                    kb.wait("act", pc0)
                    hdst = role["hnT"][:, :, t0 * 128:(t0 + 1) * 128]
                    ev = kb.op("act", lambda e, hdst=hdst, pst0=pst0: e.activation(
                        out=hdst, in_=pst0.rearrange("p (a b) -> p a b", a=16), func=AF.Copy))
                    bank_free[2 * hs0] = ev
                    bank_free[2 * hs0 + 1] = ev
                pend = cur

        def proj_fill(job, kind, tg, dst):
            slot = job["slot0"]
            kb.wait("pe", job["ld"])
            bk = state["rb"] % 4
            state["rb"] += 1
            kb.wait("pe", bank_free[bk])
            for kc in range(16):
                rhs_ap = role["hnT"][:, kc, tg * 512:(tg + 1) * 512]
                mc = kb.op("pe", lambda e, rhs_ap=rhs_ap, kc=kc, slot=slot, bk=bk: e.matmul(
                    bank(bk), lhsT=Wt[:, slot * 2048 + kc * 128: slot * 2048 + (kc + 1) * 128],
                    rhs=rhs_ap, start=(kc == 0), stop=(kc == 15)),
                    ms=(kc == 15))
            if tg == 3:
                ws.release(job, mc)
            cols = slice(tg * 512, (tg + 1) * 512)
            if kind == "q":
                kb.wait("act", mc)
                ev = kb.op("act", lambda e, bk=bk, cols=cols: e.activation(
                    out=dst[:, cols], in_=bank(bk), func=AF.Copy))
            elif kind in ("k", "v"):
                kb.wait("dve", mc)
                ev = kb.op("dve", lambda e, bk=bk, cols=cols: e.tensor_copy(out=dst[:, cols], in_=bank(bk)))
            else:
                kb.wait("act", mc)
                kb.wait("act", state["th_free"])
                a1 = kb.op("act", lambda e, bk=bk: e.activation(out=thf, in_=bank(bk), func=AF.Exp, scale=-1.0))
                kb.wait("act", a1)
                a2 = kb.op("act", lambda e: e.activation(out=thf, in_=thf, func=AF.Ln, bias=onesf[:, 0:1]))
                kb.wait("act", a2)
                a3 = kb.op("act", lambda e: e.activation(out=thf, in_=thf, func=AF.Exp, scale=-1.0))
                kb.wait("dve", a3)
                ev = kb.op("dve", lambda e, bk=bk, cols=cols: e.tensor_tensor(
                    out=dst[:, cols], in0=thf, in1=bank(bk), op=ALU.mult))
                state["th_free"] = ev
            bank_free[bk] = ev
            return ev

        def proj(job, kind, dst):
            ev = None
            for tg in range(4):
                ev = proj_fill(job, kind, tg, dst)
            return ev

        def vtrans(vcond, vdst):
            kb.wait("pe", vcond)
            kb.wait("pe", bank_free[0])
            kb.wait("pe", bank_free[1])
            psv = bank2_bf(0)
            for j in range(16):
                pc = kb.op("pe", lambda e, j=j: e.transpose(
                    out=psv[:, j * 128:(j + 1) * 128], in_=vT[:, j * 128:(j + 1) * 128], identity=ident[:]),
                    ms=(j == 15))
            kb.wait("dve", pc)
            ev = kb.op("dve", lambda e: e.tensor_copy(out=vdst, in_=psv))
            bank_free[0] = ev
            bank_free[1] = ev
            return ev

        class FillStream:
            def __init__(self, tasks):
                self.tasks = tasks
                self.ti = 0
                self.kc = 0

            def remaining(self):
                return sum(t[6] for t in self.tasks[self.ti:]) - self.kc

            def emit(self, nmm):
                while nmm > 0 and self.ti < len(self.tasks):
                    job, kind, tg, dst, st, key, nops = self.tasks[self.ti]
                    bk = 3
                    if self.kc == 0:
                        if kind == "vt":
                            kb.wait("pe", st["v"])
                        else:
                            kb.wait("pe", job["ld"])
                        kb.wait("pe", bank_free[bk])
                    kc = self.kc
                    if kind == "vt":
                        psv = bank(bk).bitcast(BF16)
                        jt = tg * 8 + kc
                        mc = kb.op("pe", lambda e, kc=kc, jt=jt, psv=psv: e.transpose(
                            out=psv[:, kc * 128:(kc + 1) * 128], in_=vT[:, jt * 128:(jt + 1) * 128],
                            identity=ident[:]), ms=(kc == nops - 1))
                    else:
                        slot = job["slot0"]
                        rhs_ap = role["hnT"][:, kc, tg * 512:(tg + 1) * 512]
                        mc = kb.op("pe", lambda e, rhs_ap=rhs_ap, kc=kc, slot=slot, bk=bk: e.matmul(
                            bank(bk), lhsT=Wt[:, slot * 2048 + kc * 128: slot * 2048 + (kc + 1) * 128],
                            rhs=rhs_ap, start=(kc == 0), stop=(kc == 15)),
                            ms=(kc == 15))
                    self.kc += 1
                    nmm -= 1
                    if self.kc == nops:
                        kb.wait("dve", mc)
                        if kind == "vt":
                            ev = kb.op("dve", lambda e, tg=tg, dst=dst, psv=psv: e.tensor_copy(
                                out=dst[:, tg * 1024:(tg + 1) * 1024], in_=psv))
                        else:
                            if tg == 3:
                                ws.release(job, mc)
                            cols = slice(tg * 512, (tg + 1) * 512)
                            ev = kb.op("dve", lambda e, bk=bk, cols=cols, dst=dst: e.tensor_copy(
                                out=dst[:, cols], in_=bank(bk)))
                        bank_free[bk] = ev
                        st[key] = ev
                        self.ti += 1
                        self.kc = 0
                        return

            def flush(self):
                while self.ti < len(self.tasks):
                    self.emit(16)

        def attention(it, econd, qcond, kcond, vcond, zcond, kT, vtok, fillers):
            LA = 2
            G = []
            for bi, (q0, n, lst) in enumerate(it["blocks"]):
                for i, (j, eoff, c0, c1) in enumerate(lst):
                    G.append((q0, n, j, eoff, i == 0, i == len(lst) - 1, c0, c1))
            hout = it["hout"]
            scol = it["scol"]
            p_free = state.setdefault("p_free", [None] * NP)
            gb = state.setdefault("gblk", 0)
            scond = {}
            sbank = {}
            pend_fin = []

            def emit_S(g):
                q0, n, j, eoff, first, last, c0, c1 = G[g]
                bk = state["sb"] % 3
                state["sb"] += 1
                sbank[g] = bk
                kb.wait("pe", bank_free[bk])
                if g == 0:
                    kb.wait("pe", qcond)
                    kb.wait("pe", kcond)
                scond[g] = kb.op("pe", lambda e, bk=bk, j=j, q0=q0, c0=c0, c1=c1: e.matmul(
                    bank(bk)[:, c0:c1], lhsT=kT[:, j * 128:(j + 1) * 128], rhs=qT[:, q0 + c0:q0 + c1],
                    start=True, stop=True))

            for g in range(min(LA, len(G))):
                emit_S(g)
            mul_last = None
            for g in range(len(G)):
                q0, n, j, eoff, first, last, c0, c1 = G[g]
                bk = sbank[g]
                psl = (gb + g) % NP
                if first:
                    state["qbn"] += 1
                ob = 4 + 2 * (state["qbn"] % 2)
                kb.wait("act", scond[g])
                kb.wait("act", p_free[psl])
                ec = kb.op("act", lambda e, bk=bk, psl=psl, c0=c0, c1=c1: e.activation(
                    out=Pb[psl][:, c0:c1], in_=bank(bk)[:, c0:c1], func=AF.Exp, scale=SCALE))
                bank_free[bk] = ec
                kb.wait("dve", ec)
                kb.wait("dve", econd)
                mc = kb.op("dve", lambda e, psl=psl, eoff=eoff, c0=c0, c1=c1: e.tensor_tensor(
                    out=Pb[psl][:, c0:c1], in0=Pb[psl][:, c0:c1], in1=Eb[:, eoff + c0:eoff + c1], op=ALU.mult))
                mul_last = mc
                if g + LA < len(G):
                    emit_S(g + LA)
                if fillers is not None and fillers.remaining() > 0:
                    nb = len(G) - g
                    fillers.emit(-(-fillers.remaining() // nb) + (4 if g == 0 else 0))
                kb.wait("pe", mc)
                if first:
                    kb.wait("pe", bank_free[ob])
                    kb.wait("pe", bank_free[ob + 1])
                if g == 0:
                    kb.wait("pe", vcond)
                kb.op("pe", lambda e, j=j, psl=psl, first=first, last=last, ob=ob, c0=c0, c1=c1: e.matmul(
                    bank(ob)[:, c0:c1], lhsT=vtok[:, j * 128:(j + 1) * 128], rhs=Pb[psl][:, c0:c1],
                    start=first, stop=last, skip_group_check=True), ms=False)
                pv = kb.op("pe", lambda e, psl=psl, first=first, last=last, ob=ob, c0=c0, c1=c1: e.matmul(
                    bank(ob + 1)[:, c0:c1], lhsT=ones[:], rhs=Pb[psl][:, c0:c1], start=first, stop=last,
                    skip_group_check=True))
                p_free[psl] = pv
                if last:
                    pend_fin.append((g + 2, pv, n, q0, ob))
                while pend_fin and (pend_fin[0][0] <= g or g == len(G) - 1):
                    _, pvc, fn_, fq0, fob = pend_fin.pop(0)
                    kb.wait("act", pvc)
                    kb.wait("act", state["r_free"])
                    f1 = kb.op("act", lambda e, n=fn_, ob=fob: e.activation(
                        out=rbuf[:, 0:n], in_=bank(ob + 1)[:, 0:n], func=AF.Ln, bias=esink[:, scol:scol + 1]))
                    kb.wait("act", f1)
                    f2 = kb.op("act", lambda e, n=fn_: e.activation(
                        out=rbuf[:, 0:n], in_=rbuf[:, 0:n], func=AF.Exp, scale=-1.0))
                    kb.wait("dve", f2)
                    f3 = kb.op("dve", lambda e, n=fn_, ob=fob: e.tensor_tensor(
                        out=tbuf[:, 0:n], in0=bank(ob)[:, 0:n], in1=rbuf[:, 0:n], op=ALU.mult))
                    bank_free[fob] = f3
                    bank_free[fob + 1] = f3
                    state["r_free"] = f3
                    kb.wait("dve", f3)
                    kb.wait("dve", zcond)
                    ydst = role["yT"][:, hout, fq0:fq0 + fn_]
                    kb.op("dve", lambda e, n=fn_, q0=fq0, ydst=ydst: e.tensor_tensor(
                        out=ydst, in0=tbuf[:, 0:n], in1=sz2[:, q0:q0 + n], op=ALU.mult))
            state["gblk"] = gb + len(G)
            state["e_reader"] = mul_last

        def phase_heads(items, after_last_proj=None):
            kb.barrier()
            grp = state["grp"]
            kvst = {}

            def kv_stream(it, alt):
                st = {}
                tasks = [(it["jobs"]["v"], "v", tg, vT, st, "v", 16) for tg in range(4)]
                tasks += [(None, "vt", hf, vtokb[alt], st, "vt", 8) for hf in range(2)]
                tasks += [(it["jobs"]["k"], "k", tg, kTb[alt], st, "k", 16) for tg in range(4)]
                return st, FillStream(tasks)

            pending = None
            for i, it in enumerate(items):
                kb.wait("pool", state["e_reader"])
                ldc = kb.dma("pool", Eb[:, 0:it["ew"]], it["esrc"], e_ld)
                if it["need_exp"]:
                    kb.wait("act", ldc)
                    econd = kb.op("act", lambda e, w=it["ew"]: e.activation(
                        out=Eb[:, 0:w], in_=Eb[:, 0:w], func=AF.Exp))
                else:
                    econd = ldc
                jobs = it["jobs"]
                if "v" in jobs:
                    if pending is None or pending[0] != i:
                        grp += 1
                        st, stream = kv_stream(it, grp % 2)
                        pending = (i, st, stream, grp % 2)
                    _, st, stream, alt = pending
                    stream.flush()
                    kvst = {"k": st["k"], "alt": alt, "v": st["vt"]}
                    pending = None
                zcond = proj(jobs["z"], "z", sz2)
                qcond = proj(jobs["q"], "q", qT)
                if i == len(items) - 1 and after_last_proj is not None:
                    after_last_proj()
                fillers = None
                if i + 1 < len(items) and "v" in items[i + 1]["jobs"]:
                    grp += 1
                    st2, stream2 = kv_stream(items[i + 1], grp % 2)
                    pending = (i + 1, st2, stream2, grp % 2)
                    fillers = stream2
                attention(it, econd, qcond, kvst["k"], kvst["v"], zcond, kTb[kvst["alt"]], vtokb[kvst["alt"]], fillers)
            state["grp"] = grp

        xr = [af32(0, 2048), af32(4096, 2048), af32(8192, 2048)]
        hn3 = [abf(12288, 2048), abf(14336, 2048)]
        lnB3 = af32(16384, 2048)
        wq_ld = [kb.new_sem("wqld%d" % i) for i in range(2)]
        xr_ld = [kb.new_sem("xrld%d" % i) for i in range(3)]
        xr_st = [kb.new_sem("xrst%d" % i) for i in range(3)]
        state["xr_free"] = [[], [], []]

        def issue_wq(w_out, dead_flat):
            kb.wait("pool", kb.last["pe"])
            conds = []
            for i in range(2):
                conds.append(kb.dma("pool", dead_flat[:, i * 8192:(i + 1) * 8192], w_out[2 + i], wq_ld[i]))
            state["wq"] = (conds, dead_flat)

        def phase_out(qjobs, res_src, mode, x1_dst, out_dst, ln_idx):
            kb.barrier()
            yT3 = role["yT"]
            wq_conds, dead_flat = state["wq"]
            if mode != "plain":
                lncond = kb.dma("sp", lnB3, ln_d[ln_idx].partition_broadcast(128), ln_ld)
            ldc = {}

            def issue_load(t):
                sl = t % 3
                for c_ in state["xr_free"][sl]:
                    kb.wait("sp", c_)
                ldc[t] = kb.dma("sp", xr[sl], res_src[t * 128:(t + 1) * 128, :], xr_ld[sl])

            def rhs_q(qd, fc):
                if qd < 2:
                    slot = qjobs[qd]["slot0"]
                    return Wt[:, slot * 2048 + fc * 512: slot * 2048 + (fc + 1) * 512]
                return dead_flat[:, (qd - 2) * 8192 + fc * 512:(qd - 2) * 8192 + (fc + 1) * 512]

            issue_load(0)
            issue_load(1)
            pend = None
            for t in range(NT + 1):
                cur = None
                if t < NT:
                    if t + 2 < NT:
                        issue_load(t + 2)
                    sl = t % 3
                    hs = t % 2
                    addc = None
                    for qd in range(4):
                        if t == 0:
                            kb.wait("pe", qjobs[qd]["ld"] if qd < 2 else wq_conds[qd - 2])
                        kb.wait("pe", bank_free[qd])
                        for fc in range(16):
                            lhs_ap = yT3[:, fc, t * 128:(t + 1) * 128]
                            rhs_ap = rhs_q(qd, fc)
                            mc = kb.op("pe", lambda e, qd=qd, fc=fc, lhs_ap=lhs_ap, rhs_ap=rhs_ap: e.matmul(
                                bank(qd), lhsT=lhs_ap, rhs=rhs_ap, start=(fc == 0), stop=(fc == 15)),
                                ms=(fc == 15))
                        if t == NT - 1 and qd < 2:
                            ws.release(qjobs[qd], mc)
                        kb.wait("dve", mc)
                        kb.wait("dve", ldc[t])
                        addc = kb.op("dve", lambda e, qd=qd, sl=sl: e.tensor_tensor(
                            out=xr[sl][:, qd * 512:(qd + 1) * 512], in0=bank(qd),
                            in1=xr[sl][:, qd * 512:(qd + 1) * 512], op=ALU.add))
                        bank_free[qd] = addc
                    frees = []
                    if mode in ("mid", "plain"):
                        kb.wait("act", addc)
                        dstd = x1_dst if mode == "mid" else out_dst
                        frees.append(kb.dma("act", dstd[t * 128:(t + 1) * 128, :], xr[sl], xr_st[sl]))
                    if mode != "plain":
                        kb.wait("act", addc)
                        if mode == "mid":
                            kb.wait("act", state["hn_free"][hs])
                        jk = hn3[hs] if mode == "mid" else hn3[0]
                        a1 = kb.op("act", lambda e, sl=sl, jk=jk: e.activation(
                            out=jk, in_=xr[sl], func=AF.Square, scale=float(2048 ** -0.5),
                            accum_out=stats[:, sl:sl + 1]))
                        kb.wait("act", a1)
                        a2 = kb.op("act", lambda e, sl=sl: e.activation(
                            out=stats[:, 4 + sl:5 + sl], in_=stats[:, sl:sl + 1], func=AF.Sqrt, bias=epsT[:, 0:1]))
                        kb.wait("dve", a2)
                        d1 = kb.op("dve", lambda e, sl=sl: e.reciprocal(out=stats[:, 8 + sl:9 + sl],
                                                                        in_=stats[:, 4 + sl:5 + sl]))
                        kb.wait("dve", d1)
                        kb.wait("dve", lncond)
                        if mode == "mid":
                            d2 = kb.op("dve", lambda e, sl=sl, hs=hs: e.scalar_tensor_tensor(
                                out=hn3[hs], in0=xr[sl], scalar=stats[:, 8 + sl:9 + sl], in1=lnB3,
                                op0=ALU.mult, op1=ALU.mult))
                            frees.append(d2)
                            cur = (hs, t, d2)
                        else:
                            d2 = kb.op("dve", lambda e, sl=sl: e.scalar_tensor_tensor(
                                out=xr[sl], in0=xr[sl], scalar=stats[:, 8 + sl:9 + sl], in1=lnB3,
                                op0=ALU.mult, op1=ALU.mult))
                            kb.wait("act", d2)
                            frees.append(kb.dma("act", out_dst[t * 128:(t + 1) * 128, :], xr[sl], xr_st[sl]))
                    state["xr_free"][sl] = frees
                if pend is not None:
                    hs0, t0, d20 = pend
                    kb.wait("pe", d20)
                    kb.wait("pe", bank_free[4 + 2 * hs0])
                    kb.wait("pe", bank_free[5 + 2 * hs0])
                    pst = bank2_bf(4 + 2 * hs0)
                    for fc in range(16):
                        pc = kb.op("pe", lambda e, hs0=hs0, fc=fc, pst=pst: e.transpose(
                            out=pst[:, fc * 128:(fc + 1) * 128], in_=hn3[hs0][:, fc * 128:(fc + 1) * 128],
                            identity=ident[:]), ms=(fc == 15))
                    state["hn_free"][hs0] = pc
                    kb.wait("act", pc)
                    hdst = yT3[:, :, t0 * 128:(t0 + 1) * 128]
                    ev = kb.op("act", lambda e, hdst=hdst, pst=pst: e.activation(
                        out=hdst, in_=pst.rearrange("p (a b) -> p a b", a=16), func=AF.Copy))
                    bank_free[4 + 2 * hs0] = ev
                    bank_free[5 + 2 * hs0] = ev
                pend = cur

        def wait_stores(eng):
            for s in xo_st + x_st + xr_st:
                if s.count:
                    kb.wait(eng, (s, s.count))

        for (sq, L, items, qjobs) in plan:
            rows = slice(sq * S, (sq + 1) * S)
            first_layer = (L == do_layers[0])
            last_layer = (L == do_layers[-1])
            if L == 0:
                role.update(hnT=bufA3, yT=bufB3, hflat=bufA, yflat=bufB)
            else:
                role.update(hnT=bufB3, yT=bufA3, hflat=bufB, yflat=bufA)
            src = x_d[rows, :] if first_layer else x1_d
            w_out = wab_out if L == 0 else wc_out
            if first_layer:
                wait_stores("sp")
                phase_norm(src, 0 if L == 0 else 1, None)
            hflat = role["hflat"]
            phase_heads(items, after_last_proj=lambda w_out=w_out, hflat=hflat: issue_wq(w_out, hflat))
            wait_stores("sp")
            if not last_layer:
                phase_out(qjobs, src, "mid", x1_d, None, 1)
            elif final_norm:
                phase_out(qjobs, src, "final", None, out_d[rows, :], 2)
            else:
                phase_out(qjobs, src, "plain", None, out_d[rows, :], 0)
        kb.barrier()
        wait_stores("sp")

        with nc.Block() as block:
            @block.tensor
            def _(e):
                for f in kb.q["pe"]:
                    f(e)

            @block.scalar
            def _(e):
                for f in kb.q["act"]:
                    f(e)

            @block.vector
            def _(e):
                for f in kb.q["dve"]:
                    f(e)

            @block.gpsimd
            def _(e):
                for f in kb.q["pool"]:
                    f(e)

            @block.sync
            def _(e):
                for f in kb.q["sp"]:
                    f(e)
    return nc


def _w_in_layout(w):
    C = w.shape[1]
    g = C // 128
    return np.ascontiguousarray(w.reshape(16, 128, g, 128).transpose(2, 1, 0, 3)).reshape(g, 128, 2048)


def _w_out_layout(w):
    return np.ascontiguousarray(w.reshape(16, 128, 4, 512).transpose(2, 1, 0, 3)).reshape(4, 128, 8192)


def _alibi_tables():
    slopes = np.exp2(-8.0 * np.arange(1, 17, dtype=np.float64) / 16)
    p = np.arange(128)[:, None]
    EA = np.zeros((8, 128, 1152), np.float32)
    u = np.arange(1152)[None, :]
    dl = u - 512 - p
    for h in range(8):
        EA[h] = np.where(np.abs(dl) <= 128, np.exp(-slopes[h] * np.abs(dl)), 0.0)
    EB = np.zeros((8, 128, 2944), np.float32)
    u = np.arange(2944)[None, :]
    dl = u - 1408 - p
    ad = np.abs(dl)
    mult = (ad <= 64).astype(np.float64) + ((dl % 4 == 0) & (ad <= 256)) + ((dl % 16 == 0) & (ad <= 1024))
    for h in range(8):
        EB[h] = mult * np.exp(-slopes[8 + h] * ad)
    return EA, EB


def _rpb_strips(rpb):
    p = np.arange(128)
    rl = (p // 64)[:, None, None]
    kc = (p % 64)[:, None, None]
    i = np.arange(24)[None, :, None]
    qc = np.arange(64)[None, None, :]
    dr = 14 - (i - rl - 4) + 0 * qc
    dc = kc - qc + 15 + 0 * i
    c0 = np.clip(qc - 8, 0, 48)
    colok = (kc >= c0) & (kc < c0 + 16) & (i >= 0)
    out = np.full((16, 128, 2, 24, 64), NEGB, np.float32)
    for var, (lo, hi) in enumerate(((3, 10), (0, 14))):
        ok = colok & (dr >= lo) & (dr <= hi)
        drc = np.clip(dr, 0, 14)
        dcc = np.clip(dc, 0, 30)
        gathered = rpb[:, drc, dcc]
        out[:, :, var] = np.where(ok[None], gathered, np.float32(NEGB))
    return out.reshape(16, 128, 3072)


_CACHE = {}


def _get_nc(nseq, do_layers=(0, 1), final_norm=True):
    key = (nseq, tuple(do_layers), final_norm)
    if key not in _CACHE:
        _CACHE[key] = build(nseq, do_layers, final_norm)
    return _CACHE[key]


def _common_inputs(ln_ab, w_in_ab, sink_a, w_out_ab, ln_c, w_in_c, rpb_c, w_out_c, ln_f):
    EA, EB = _alibi_tables()
    return {
        "wab_in": _w_in_layout(np.asarray(w_in_ab[0], np.float32)),
        "wab_out": _w_out_layout(np.asarray(w_out_ab[0], np.float32)),
        "wc_in": _w_in_layout(np.asarray(w_in_c[0], np.float32)),
        "wc_out": _w_out_layout(np.asarray(w_out_c[0], np.float32)),
        "ln": np.ascontiguousarray(np.stack([np.asarray(ln_ab[0]), np.asarray(ln_c[0]), np.asarray(ln_f)]).astype(np.float32)),
        "sink": np.ascontiguousarray(np.asarray(sink_a[0], np.float32)),
        "ident": np.eye(128, dtype=np.float32),
        "EA": EA, "EB": EB,
        "BC": _rpb_strips(np.asarray(rpb_c[0], np.float32)),
    }


def kernel(x, ln_ab, w_in_ab, sink_a, w_out_ab, ln_c, w_in_c, rpb_c, w_out_c, ln_f):
    x = np.asarray(x, np.float32)
    B = x.shape[0]
    nseq = B // N_CORES
    common = _common_inputs(ln_ab, w_in_ab, sink_a, w_out_ab, ln_c, w_in_c, rpb_c, w_out_c, ln_f)
    nc = _get_nc(nseq)
    in_maps = []
    for c in range(N_CORES):
        m = dict(common)
        m["x"] = np.ascontiguousarray(x[c * nseq:(c + 1) * nseq].reshape(nseq * S, D))
        in_maps.append(m)
    res = run_bass_kernel_spmd(nc, in_maps, core_ids=list(range(N_CORES)))
    outs = [np.asarray(r["out"]).reshape(nseq, S, D) for r in res.results]
    return np.concatenate(outs, axis=0).astype(np.float32)
```

```python
import contextlib
import numpy as np
import concourse.bass as bass
import concourse.mybir as mybir
from concourse.bass_utils import run_bass_kernel_spmd

F32 = mybir.dt.float32
BF16 = mybir.dt.bfloat16
AF = mybir.ActivationFunctionType
ALU = mybir.AluOpType

D = 2048
S = 2048
NT = 16
SCALE = float(128 ** -0.5)
NEGB = -30000.0
N_CORES = 8
EW = 3072
NWS = 8

ENGS = ("pe", "act", "dve", "pool", "sp")


class Sem:
    def __init__(self, h):
        self.h = h
        self.count = 0


class KB:
    def __init__(self, nc, es):
        self.nc = nc
        self.es = es
        self.q = {e: [] for e in ENGS}
        self.waited = {}
        self.S = {}
        for e in ("pe", "act", "dve"):
            self.S[e] = self.new_sem("S_" + e)
        self.last = {e: None for e in ("pe", "act", "dve")}

    def new_sem(self, name):
        return Sem(self.es.enter_context(self.nc.semaphore(name)))

    def wait(self, eng, cond):
        if cond is None:
            return
        sem, val = cond
        assert val <= sem.count, (eng, val, sem.count)
        key = (eng, id(sem))
        if self.waited.get(key, 0) >= val:
            return
        self.waited[key] = val
        h = sem.h
        self.q[eng].append(lambda e: e.wait_ge(h, val))

    def op(self, eng, fn, ms=True):
        if not ms:
            self.q[eng].append(fn)
            return None
        s = self.S[eng]
        s.count += 1
        h = s.h
        self.q[eng].append(lambda e: fn(e).then_inc(h, 1))
        self.last[eng] = (s, s.count)
        return (s, s.count)

    def dma(self, eng, out, in_, sem):
        sem.count += 16
        h = sem.h
        self.q[eng].append(lambda e: e.dma_start(out=out, in_=in_).then_inc(h, 16))
        return (sem, sem.count)

    def barrier(self):
        for e in ("pe", "act", "dve", "sp"):
            for o in ("pe", "act", "dve"):
                if o != e:
                    self.wait(e, self.last[o])


class WStream:
    def __init__(self, kb, Wt):
        self.kb = kb
        self.Wt = Wt
        self.jobs = []
        self.pos = 0
        self.next = 0
        self.slot_free = [None] * NWS
        self.slot_pending = [False] * NWS
        self.ld = [kb.new_sem("wld%d" % i) for i in range(NWS)]

    def add_job(self, src, n):
        if n == 4 and self.pos % 4:
            self.pos = (self.pos + 4 - self.pos % 4) % NWS
        job = {"src": src, "n": n, "slot0": self.pos, "ld": None}
        self.pos = (self.pos + n) % NWS
        self.jobs.append(job)
        return job

    def try_issue(self):
        kb = self.kb
        while self.next < len(self.jobs):
            job = self.jobs[self.next]
            slots = range(job["slot0"], job["slot0"] + job["n"])
            if any(self.slot_pending[s] for s in slots):
                return
            for s in slots:
                kb.wait("pool", self.slot_free[s])
                self.slot_pending[s] = True
            dst = self.Wt[:, job["slot0"] * 2048:(job["slot0"] + job["n"]) * 2048]
            job["ld"] = kb.dma("pool", dst, job["src"], self.ld[job["slot0"]])
            self.next += 1

    def release(self, job, cond):
        for s in range(job["slot0"], job["slot0"] + job["n"]):
            self.slot_free[s] = cond
            self.slot_pending[s] = False
        self.try_issue()


def blocks_A():
    res = []
    for b in range(4):
        lst = []
        for j in range(4 * b - 1, 4 * b + 5):
            if 0 <= j < 16:
                c0 = (max(j - 1, 4 * b) - 4 * b) * 128
                c1 = (min(j + 1, 4 * b + 3) + 1 - 4 * b) * 128
                lst.append((j, 512 * b - 128 * j + 512, c0, c1))
        res.append((512 * b, 512, lst))
    return res


def blocks_B():
    res = []
    for b in range(4):
        js = [j for j in range(16) if -1535 <= 512 * b - 128 * j <= 1151]
        res.append((512 * b, 512, [(j, 512 * b - 128 * j + 1408, 0, 512) for j in js]))
    return res


def blocks_C():
    res = [(0, 320, [(j, 1536 + (11 - 2 * j) * 64, 0, 320) for j in range(4)])]
    for ra, nr, j0 in ((5, 8, 0), (13, 8, 4), (21, 7, 8)):
        res.append((64 * ra, 64 * nr, [(j, (11 - 2 * j + ra) * 64, 0, 64 * nr) for j in range(j0, j0 + 8)]))
    res.append((64 * 28, 256, [(j, 1536 + (11 - 2 * j + 28) * 64, 0, 256) for j in range(12, 16)]))
    return res


def build(nseq, do_layers=(0, 1), final_norm=True):
    nc = bass.Bass("TRN2", target_bir_lowering=False)
    R = nseq * S
    x_d = nc.dram_tensor("x", [R, D], F32, kind="ExternalInput").ap()
    out_d = nc.dram_tensor("out", [R, D], F32, kind="ExternalOutput").ap()
    x1_d = nc.dram_tensor("x1s", [S, D], F32).ap()
    wab_in = nc.dram_tensor("wab_in", [52, 128, 2048], F32, kind="ExternalInput").ap()
    wab_out = nc.dram_tensor("wab_out", [4, 128, 8192], F32, kind="ExternalInput").ap()
    wc_in = nc.dram_tensor("wc_in", [64, 128, 2048], F32, kind="ExternalInput").ap()
    wc_out = nc.dram_tensor("wc_out", [4, 128, 8192], F32, kind="ExternalInput").ap()
    ln_d = nc.dram_tensor("ln", [3, D], F32, kind="ExternalInput").ap()
    sink_d = nc.dram_tensor("sink", [8], F32, kind="ExternalInput").ap()
    ident_d = nc.dram_tensor("ident", [128, 128], F32, kind="ExternalInput").ap()
    EA_d = nc.dram_tensor("EA", [8, 128, 1152], F32, kind="ExternalInput").ap()
    EB_d = nc.dram_tensor("EB", [8, 128, 2944], F32, kind="ExternalInput").ap()
    BC_d = nc.dram_tensor("BC", [16, 128, 3072], F32, kind="ExternalInput").ap()

    with contextlib.ExitStack() as es:
        def sb(name, shape, dt):
            return es.enter_context(nc.sbuf_tensor(name, shape, dt))

        bufA = sb("bufA", [128, 32768], BF16)
        bufB = sb("bufB", [128, 32768], BF16)
        bufA3 = bufA[:, :].rearrange("p (a b) -> p a b", a=16)
        bufB3 = bufB[:, :].rearrange("p (a b) -> p a b", a=16)
        role = {"hnT": bufA3, "yT": bufB3, "hflat": bufA, "yflat": bufB}
        Wt = sb("Wt", [128, NWS * 2048], BF16)
        Eb = sb("Eb", [128, EW], BF16)
        arena = sb("arena", [128, 20480], BF16)
        ident = sb("ident_sb", [128, 128], BF16)
        ones = sb("ones_sb", [128, 128], BF16)
        stats = sb("stats", [128, 16], F32)
        esink = sb("esink", [128, 16], F32)
        epsT = sb("epsT", [128, 1], F32)
        onesf = sb("onesf", [128, 1], F32)
        ps = es.enter_context(nc.psum_tensor("ps", [128, 4096], F32))

        kb = KB(nc, es)
        ws = WStream(kb, Wt)

        def bank(b):
            return ps[:, 512 * b:512 * (b + 1)]

        def bank2_bf(b):
            return ps[:, 512 * b:512 * (b + 2)].bitcast(BF16)

        bank_free = [None] * 8

        def abf(off, n):
            return arena[:, off:off + n]

        def af32(off, n):
            return arena[:, off:off + 2 * n].bitcast(F32)

        qT = abf(0, 2048)
        kTb = [abf(2048, 2048), abf(4096, 2048)]
        vT = abf(6144, 2048)
        sz2 = abf(8192, 2048)
        vtokb = [abf(10240, 2048), abf(12288, 2048)]
        NP = 6
        Pb = [abf(14336 + 512 * i, 512) for i in range(NP)]
        thf = af32(17408, 512)
        rbuf = af32(18432, 512)
        tbuf = af32(19456, 512)
        xt = [af32(0, 2048), af32(4096, 2048), af32(8192, 2048), af32(16384, 2048)]
        hnb = [abf(8192, 2048), abf(10240, 2048)]
        lnB = af32(12288, 2048)
        junk = ps[:, 2048:4096]
        resb = [af32(1024 * i, 512) for i in range(3)]
        xob = [af32(3072 + 1024 * i, 512) for i in range(3)]

        x_ld = [kb.new_sem("xld%d" % i) for i in range(4)]
        x_st = [kb.new_sem("xst%d" % i) for i in range(4)]
        ln_ld = kb.new_sem("lnld")
        e_ld = kb.new_sem("eld")
        c_ld = kb.new_sem("cld")
        r_ld = [kb.new_sem("rld%d" % i) for i in range(3)]
        xo_st = [kb.new_sem("xost%d" % i) for i in range(3)]

        def layer_items(L):
            items = []
            if L == 0:
                for kv in range(2):
                    for gi in range(4):
                        h = kv * 4 + gi
                        items.append(dict(hout=h, gv=10 + kv if gi == 0 else None, gk=8 + kv if gi == 0 else None,
                                          gq=h, gz=36 + h, blocks=blocks_A(), esrc=EA_d[h], ew=1152,
                                          need_exp=False, scol=h))
                for hb in range(8):
                    items.append(dict(hout=8 + hb, gv=28 + hb, gk=20 + hb, gq=12 + hb, gz=44 + hb,
                                      blocks=blocks_B(), esrc=EB_d[hb], ew=2944, need_exp=False, scol=8))
            else:
                for h in range(16):
                    items.append(dict(hout=h, gv=32 + h, gk=16 + h, gq=h, gz=48 + h, blocks=blocks_C(),
                                      esrc=BC_d[h], ew=3072, need_exp=True, scol=8))
            return items

        plan = []
        for sq in range(nseq):
            for L in do_layers:
                w_in = wab_in if L == 0 else wc_in
                w_out = wab_out if L == 0 else wc_out
                items = layer_items(L)
                for it in items:
                    it["jobs"] = {}
                    for kind in ("v", "k", "z", "q"):
                        g = it["g" + kind]
                        if g is not None:
                            it["jobs"][kind] = ws.add_job(w_in[g], 1)
                qjobs = [ws.add_job(w_out[qd], 4) for qd in range(2)]
                plan.append((sq, L, items, qjobs))

        kb.dma("pool", ident[:], ident_d[:, :], c_ld)
        kb.dma("pool", esink[:, 0:8], sink_d.partition_broadcast(128), c_ld)
        kb.op("dve", lambda e: e.memset(ones[:], 1.0))
        kb.op("dve", lambda e: e.memset(epsT[:], 1e-5))
        kb.op("dve", lambda e: e.memset(onesf[:], 1.0))
        kb.op("dve", lambda e: e.memset(esink[:, 8:16], 0.0))
        kb.wait("act", (c_ld, c_ld.count))
        kb.op("act", lambda e: e.activation(out=esink[:, 0:8], in_=esink[:, 0:8], func=AF.Exp))
        kb.wait("pe", (c_ld, c_ld.count))
        ws.try_issue()
        state = {"e_reader": None, "p1n": 0, "p4n": 0, "xt_free": [None, None, None, None], "hn_free": [None, None],
                 "rb": 0, "sb": 0, "qbn": 0, "r_free": None, "grp": 0,
                 "p3n": 0, "res_free": [None] * 3, "xo_free": [None] * 3, "th_free": None}

        def phase_norm(src, ln_idx, dst):
            kb.barrier()
            lncond = kb.dma("sp", lnB, ln_d[ln_idx].partition_broadcast(128), ln_ld)
            sids = [0, 1, 3] if dst is None else [0, 1, 2, 3]
            ns = len(sids)
            ldc = {}

            def issue_load(t):
                sid = sids[t % ns]
                kb.wait("sp", state["xt_free"][sid])
                ldc[t] = kb.dma("sp", xt[sid], src[t * 128:(t + 1) * 128, :], x_ld[sid])

            for t in range(min(ns - 1, NT)):
                issue_load(t)
            pend = None
            for t in range(NT + 1):
                if t < NT:
                    if t + ns - 1 < NT:
                        issue_load(t + ns - 1)
                    sid = sids[t % ns]
                    hs = t % 2
                    kb.wait("act", ldc[t])
                    a1 = kb.op("act", lambda e, sid=sid: e.activation(
                        out=junk, in_=xt[sid], func=AF.Square, scale=float(2048 ** -0.5),
                        accum_out=stats[:, sid:sid + 1]))
                    kb.wait("act", a1)
                    a2 = kb.op("act", lambda e, sid=sid: e.activation(
                        out=stats[:, 4 + sid:5 + sid], in_=stats[:, sid:sid + 1], func=AF.Sqrt, bias=epsT[:, 0:1]))
                    kb.wait("dve", a2)
                    d1 = kb.op("dve", lambda e, sid=sid: e.reciprocal(out=stats[:, 8 + sid:9 + sid],
                                                                      in_=stats[:, 4 + sid:5 + sid]))
                    kb.wait("dve", d1)
                    kb.wait("dve", lncond)
                    if dst is None:
                        kb.wait("dve", state["hn_free"][hs])
                        d2 = kb.op("dve", lambda e, sid=sid, hs=hs: e.scalar_tensor_tensor(
                            out=hnb[hs], in0=xt[sid], scalar=stats[:, 8 + sid:9 + sid], in1=lnB,
                            op0=ALU.mult, op1=ALU.mult))
                        state["xt_free"][sid] = d2
                        kb.wait("pe", d2)
                        kb.wait("pe", bank_free[2 * hs])
                        kb.wait("pe", bank_free[2 * hs + 1])
                        pst = bank2_bf(2 * hs)
                        for fc in range(16):
                            pc = kb.op("pe", lambda e, hs=hs, fc=fc, pst=pst: e.transpose(
                                out=pst[:, fc * 128:(fc + 1) * 128], in_=hnb[hs][:, fc * 128:(fc + 1) * 128],
                                identity=ident[:]), ms=(fc == 15))
                        state["hn_free"][hs] = pc
                        cur = (hs, t, pc, pst)
                    else:
                        d2 = kb.op("dve", lambda e, sid=sid: e.scalar_tensor_tensor(
                            out=xt[sid], in0=xt[sid], scalar=stats[:, 8 + sid:9 + sid], in1=lnB,
                            op0=ALU.mult, op1=ALU.mult))
                        kb.wait("sp", d2)
                        stc = kb.dma("sp", dst[t * 128:(t + 1) * 128, :], xt[sid], x_st[sid])
                        state["xt_free"][sid] = stc
                        cur = None
                else:
                    cur = None
                if pend is not None:
                    hs0, t0, pc0, pst0 = pend
                    kb.wait("act", pc0)
                    hdst = role["hnT"][:, :, t0 * 128:(t0 + 1) * 128]
                    ev = kb.op("act", lambda e, hdst=hdst, pst0=pst0: e.activation(
                        out=hdst, in_=pst0.rearrange("p (a b) -> p a b", a=16), func=AF.Copy))
                    bank_free[2 * hs0] = ev
                    bank_free[2 * hs0 + 1] = ev
                pend = cur

        def proj_fill(job, kind, tg, dst):
            slot = job["slot0"]
            kb.wait("pe", job["ld"])
            bk = state["rb"] % 4
            state["rb"] += 1
            kb.wait("pe", bank_free[bk])
            for kc in range(16):
                rhs_ap = role["hnT"][:, kc, tg * 512:(tg + 1) * 512]
                mc = kb.op("pe", lambda e, rhs_ap=rhs_ap, kc=kc, slot=slot, bk=bk: e.matmul(
                    bank(bk), lhsT=Wt[:, slot * 2048 + kc * 128: slot * 2048 + (kc + 1) * 128],
                    rhs=rhs_ap, start=(kc == 0), stop=(kc == 15)),
                    ms=(kc == 15))
            if tg == 3:
                ws.release(job, mc)
            cols = slice(tg * 512, (tg + 1) * 512)
            if kind == "q":
                kb.wait("act", mc)
                ev = kb.op("act", lambda e, bk=bk, cols=cols: e.activation(
                    out=dst[:, cols], in_=bank(bk), func=AF.Copy))
            elif kind in ("k", "v"):
                kb.wait("dve", mc)
                ev = kb.op("dve", lambda e, bk=bk, cols=cols: e.tensor_copy(out=dst[:, cols], in_=bank(bk)))
            else:
                kb.wait("act", mc)
                kb.wait("act", state["th_free"])
                a1 = kb.op("act", lambda e, bk=bk: e.activation(out=thf, in_=bank(bk), func=AF.Exp, scale=-1.0))
                kb.wait("act", a1)
                a2 = kb.op("act", lambda e: e.activation(out=thf, in_=thf, func=AF.Ln, bias=onesf[:, 0:1]))
                kb.wait("act", a2)
                a3 = kb.op("act", lambda e: e.activation(out=thf, in_=thf, func=AF.Exp, scale=-1.0))
                kb.wait("dve", a3)
                ev = kb.op("dve", lambda e, bk=bk, cols=cols: e.tensor_tensor(
                    out=dst[:, cols], in0=thf, in1=bank(bk), op=ALU.mult))
                state["th_free"] = ev
            bank_free[bk] = ev
            return ev

        def proj(job, kind, dst):
            ev = None
            for tg in range(4):
                ev = proj_fill(job, kind, tg, dst)
            return ev

        def vtrans(vcond, vdst):
            kb.wait("pe", vcond)
            kb.wait("pe", bank_free[0])
            kb.wait("pe", bank_free[1])
            psv = bank2_bf(0)
            for j in range(16):
                pc = kb.op("pe", lambda e, j=j: e.transpose(
                    out=psv[:, j * 128:(j + 1) * 128], in_=vT[:, j * 128:(j + 1) * 128], identity=ident[:]),
                    ms=(j == 15))
            kb.wait("dve", pc)
            ev = kb.op("dve", lambda e: e.tensor_copy(out=vdst, in_=psv))
            bank_free[0] = ev
            bank_free[1] = ev
            return ev

        class FillStream:
            def __init__(self, tasks):
                self.tasks = tasks
                self.ti = 0
                self.kc = 0

            def remaining(self):
                return sum(t[6] for t in self.tasks[self.ti:]) - self.kc

            def emit(self, nmm):
                while nmm > 0 and self.ti < len(self.tasks):
                    job, kind, tg, dst, st, key, nops = self.tasks[self.ti]
                    bk = 3
                    if self.kc == 0:
                        if kind == "vt":
                            kb.wait("pe", st["v"])
                        else:
                            kb.wait("pe", job["ld"])
                        kb.wait("pe", bank_free[bk])
                    kc = self.kc
                    if kind == "vt":
                        psv = bank(bk).bitcast(BF16)
                        jt = tg * 8 + kc
                        mc = kb.op("pe", lambda e, kc=kc, jt=jt, psv=psv: e.transpose(
                            out=psv[:, kc * 128:(kc + 1) * 128], in_=vT[:, jt * 128:(jt + 1) * 128],
                            identity=ident[:]), ms=(kc == nops - 1))
                    else:
                        slot = job["slot0"]
                        rhs_ap = role["hnT"][:, kc, tg * 512:(tg + 1) * 512]
                        mc = kb.op("pe", lambda e, rhs_ap=rhs_ap, kc=kc, slot=slot, bk=bk: e.matmul(
                            bank(bk), lhsT=Wt[:, slot * 2048 + kc * 128: slot * 2048 + (kc + 1) * 128],
                            rhs=rhs_ap, start=(kc == 0), stop=(kc == 15)),
                            ms=(kc == 15))
                    self.kc += 1
                    nmm -= 1
                    if self.kc == nops:
                        kb.wait("dve", mc)
                        if kind == "vt":
                            ev = kb.op("dve", lambda e, tg=tg, dst=dst, psv=psv: e.tensor_copy(
                                out=dst[:, tg * 1024:(tg + 1) * 1024], in_=psv))
                        else:
                            if tg == 3:
                                ws.release(job, mc)
                            cols = slice(tg * 512, (tg + 1) * 512)
                            ev = kb.op("dve", lambda e, bk=bk, cols=cols, dst=dst: e.tensor_copy(
                                out=dst[:, cols], in_=bank(bk)))
                        bank_free[bk] = ev
                        st[key] = ev
                        self.ti += 1
                        self.kc = 0
                        return

            def flush(self):
                while self.ti < len(self.tasks):
                    self.emit(16)

        def attention(it, econd, qcond, kcond, vcond, zcond, kT, vtok, fillers):
            LA = 2
            G = []
            for bi, (q0, n, lst) in enumerate(it["blocks"]):
                for i, (j, eoff, c0, c1) in enumerate(lst):
                    G.append((q0, n, j, eoff, i == 0, i == len(lst) - 1, c0, c1))
            hout = it["hout"]
            scol = it["scol"]
            p_free = state.setdefault("p_free", [None] * NP)
            gb = state.setdefault("gblk", 0)
            scond = {}
            sbank = {}
            pend_fin = []

            def emit_S(g):
                q0, n, j, eoff, first, last, c0, c1 = G[g]
                bk = state["sb"] % 3
                state["sb"] += 1
                sbank[g] = bk
                kb.wait("pe", bank_free[bk])
                if g == 0:
                    kb.wait("pe", qcond)
                    kb.wait("pe", kcond)
                scond[g] = kb.op("pe", lambda e, bk=bk, j=j, q0=q0, c0=c0, c1=c1: e.matmul(
                    bank(bk)[:, c0:c1], lhsT=kT[:, j * 128:(j + 1) * 128], rhs=qT[:, q0 + c0:q0 + c1],
                    start=True, stop=True))

            for g in range(min(LA, len(G))):
                emit_S(g)
            mul_last = None
            for g in range(len(G)):
                q0, n, j, eoff, first, last, c0, c1 = G[g]
                bk = sbank[g]
                psl = (gb + g) % NP
                if first:
                    state["qbn"] += 1
                ob = 4 + 2 * (state["qbn"] % 2)
                kb.wait("act", scond[g])
                kb.wait("act", p_free[psl])
                ec = kb.op("act", lambda e, bk=bk, psl=psl, c0=c0, c1=c1: e.activation(
                    out=Pb[psl][:, c0:c1], in_=bank(bk)[:, c0:c1], func=AF.Exp, scale=SCALE))
                bank_free[bk] = ec
                kb.wait("dve", ec)
                kb.wait("dve", econd)
                mc = kb.op("dve", lambda e, psl=psl, eoff=eoff, c0=c0, c1=c1: e.tensor_tensor(
                    out=Pb[psl][:, c0:c1], in0=Pb[psl][:, c0:c1], in1=Eb[:, eoff + c0:eoff + c1], op=ALU.mult))
                mul_last = mc
                if g + LA < len(G):
                    emit_S(g + LA)
                if fillers is not None and fillers.remaining() > 0:
                    nb = len(G) - g
                    fillers.emit(-(-fillers.remaining() // nb) + (4 if g == 0 else 0))
                kb.wait("pe", mc)
                if first:
                    kb.wait("pe", bank_free[ob])
                    kb.wait("pe", bank_free[ob + 1])
                if g == 0:
                    kb.wait("pe", vcond)
                kb.op("pe", lambda e, j=j, psl=psl, first=first, last=last, ob=ob, c0=c0, c1=c1: e.matmul(
                    bank(ob)[:, c0:c1], lhsT=vtok[:, j * 128:(j + 1) * 128], rhs=Pb[psl][:, c0:c1],
                    start=first, stop=last, skip_group_check=True), ms=False)
                pv = kb.op("pe", lambda e, psl=psl, first=first, last=last, ob=ob, c0=c0, c1=c1: e.matmul(
                    bank(ob + 1)[:, c0:c1], lhsT=ones[:], rhs=Pb[psl][:, c0:c1], start=first, stop=last,
                    skip_group_check=True))
                p_free[psl] = pv
                if last:
                    pend_fin.append((g + 2, pv, n, q0, ob))
                while pend_fin and (pend_fin[0][0] <= g or g == len(G) - 1):
                    _, pvc, fn_, fq0, fob = pend_fin.pop(0)
                    kb.wait("act", pvc)
                    kb.wait("act", state["r_free"])
                    f1 = kb.op("act", lambda e, n=fn_, ob=fob: e.activation(
                        out=rbuf[:, 0:n], in_=bank(ob + 1)[:, 0:n], func=AF.Ln, bias=esink[:, scol:scol + 1]))
                    kb.wait("act", f1)
                    f2 = kb.op("act", lambda e, n=fn_: e.activation(
                        out=rbuf[:, 0:n], in_=rbuf[:, 0:n], func=AF.Exp, scale=-1.0))
                    kb.wait("dve", f2)
                    f3 = kb.op("dve", lambda e, n=fn_, ob=fob: e.tensor_tensor(
                        out=tbuf[:, 0:n], in0=bank(ob)[:, 0:n], in1=rbuf[:, 0:n], op=ALU.mult))
                    bank_free[fob] = f3
                    bank_free[fob + 1] = f3
                    state["r_free"] = f3
                    kb.wait("dve", f3)
                    kb.wait("dve", zcond)
                    ydst = role["yT"][:, hout, fq0:fq0 + fn_]
                    kb.op("dve", lambda e, n=fn_, q0=fq0, ydst=ydst: e.tensor_tensor(
                        out=ydst, in0=tbuf[:, 0:n], in1=sz2[:, q0:q0 + n], op=ALU.mult))
            state["gblk"] = gb + len(G)
            state["e_reader"] = mul_last

        def phase_heads(items, after_last_proj=None):
            kb.barrier()
            for eng_ in ("pe", "act", "dve"):
                wait_stores(eng_)
            grp = state["grp"]
            kvst = {}

            def kv_stream(it, alt):
                st = {}
                tasks = [(it["jobs"]["v"], "v", tg, vT, st, "v", 16) for tg in range(4)]
                tasks += [(None, "vt", hf, vtokb[alt], st, "vt", 8) for hf in range(2)]
                tasks += [(it["jobs"]["k"], "k", tg, kTb[alt], st, "k", 16) for tg in range(4)]
                return st, FillStream(tasks)

            pending = None
            for i, it in enumerate(items):
                kb.wait("pool", state["e_reader"])
                ldc = kb.dma("pool", Eb[:, 0:it["ew"]], it["esrc"], e_ld)
                if it["need_exp"]:
                    kb.wait("act", ldc)
                    econd = kb.op("act", lambda e, w=it["ew"]: e.activation(
                        out=Eb[:, 0:w], in_=Eb[:, 0:w], func=AF.Exp))
                else:
                    econd = ldc
                jobs = it["jobs"]
                if "v" in jobs:
                    if pending is None or pending[0] != i:
                        grp += 1
                        st, stream = kv_stream(it, grp % 2)
                        pending = (i, st, stream, grp % 2)
                    _, st, stream, alt = pending
                    stream.flush()
                    kvst = {"k": st["k"], "alt": alt, "v": st["vt"]}
                    pending = None
                zcond = proj(jobs["z"], "z", sz2)
                qcond = proj(jobs["q"], "q", qT)
                if i == len(items) - 1 and after_last_proj is not None:
                    after_last_proj()
                fillers = None
                if i + 1 < len(items) and "v" in items[i + 1]["jobs"]:
                    grp += 1
                    st2, stream2 = kv_stream(items[i + 1], grp % 2)
                    pending = (i + 1, st2, stream2, grp % 2)
                    fillers = stream2
                attention(it, econd, qcond, kvst["k"], kvst["v"], zcond, kTb[kvst["alt"]], vtokb[kvst["alt"]], fillers)
            state["grp"] = grp

        xr = [af32(0, 2048), af32(4096, 2048), af32(8192, 2048)]
        hn3 = [abf(12288, 2048), abf(14336, 2048)]
        lnB3 = af32(16384, 2048)
        wq_ld = [kb.new_sem("wqld%d" % i) for i in range(2)]
        xr_ld = [kb.new_sem("xrld%d" % i) for i in range(3)]
        xr_st = [kb.new_sem("xrst%d" % i) for i in range(3)]
        state["xr_free"] = [[], [], []]

        def issue_wq(w_out, dead_flat):
            kb.wait("pool", kb.last["pe"])
            conds = []
            for i in range(2):
                conds.append(kb.dma("pool", dead_flat[:, i * 8192:(i + 1) * 8192], w_out[2 + i], wq_ld[i]))
            state["wq"] = (conds, dead_flat)

        def phase_out(qjobs, res_src, mode, x1_dst, out_dst, ln_idx):
            kb.barrier()
            yT3 = role["yT"]
            wq_conds, dead_flat = state["wq"]
            if mode != "plain":
                lncond = kb.dma("sp", lnB3, ln_d[ln_idx].partition_broadcast(128), ln_ld)
            ldc = {}

            def issue_load(t):
                sl = t % 3
                for c_ in state["xr_free"][sl]:
                    kb.wait("sp", c_)
                ldc[t] = kb.dma("sp", xr[sl], res_src[t * 128:(t + 1) * 128, :], xr_ld[sl])

            def rhs_q(qd, fc):
                if qd < 2:
                    slot = qjobs[qd]["slot0"]
                    return Wt[:, slot * 2048 + fc * 512: slot * 2048 + (fc + 1) * 512]
                return dead_flat[:, (qd - 2) * 8192 + fc * 512:(qd - 2) * 8192 + (fc + 1) * 512]

            issue_load(0)
            issue_load(1)
            pend = None
            for t in range(NT + 1):
                cur = None
                if t < NT:
                    if t + 2 < NT:
                        issue_load(t + 2)
                    sl = t % 3
                    hs = t % 2
                    addc = None
                    for qd in range(4):
                        if t == 0:
                            kb.wait("pe", qjobs[qd]["ld"] if qd < 2 else wq_conds[qd - 2])
                        kb.wait("pe", bank_free[qd])
                        for fc in range(16):
                            lhs_ap = yT3[:, fc, t * 128:(t + 1) * 128]
                            rhs_ap = rhs_q(qd, fc)
                            mc = kb.op("pe", lambda e, qd=qd, fc=fc, lhs_ap=lhs_ap, rhs_ap=rhs_ap: e.matmul(
                                bank(qd), lhsT=lhs_ap, rhs=rhs_ap, start=(fc == 0), stop=(fc == 15)),
                                ms=(fc == 15))
                        if t == NT - 1 and qd < 2:
                            ws.release(qjobs[qd], mc)
                        kb.wait("dve", mc)
                        kb.wait("dve", ldc[t])
                        addc = kb.op("dve", lambda e, qd=qd, sl=sl: e.tensor_tensor(
                            out=xr[sl][:, qd * 512:(qd + 1) * 512], in0=bank(qd),
                            in1=xr[sl][:, qd * 512:(qd + 1) * 512], op=ALU.add))
                        bank_free[qd] = addc
                    frees = []
                    if mode in ("mid", "plain"):
                        kb.wait("act", addc)
                        dstd = x1_dst if mode == "mid" else out_dst
                        frees.append(kb.dma("act", dstd[t * 128:(t + 1) * 128, :], xr[sl], xr_st[sl]))
                    if mode != "plain":
                        kb.wait("act", addc)
                        if mode == "mid":
                            kb.wait("act", state["hn_free"][hs])
                        jk = hn3[hs] if mode == "mid" else hn3[0]
                        a1 = kb.op("act", lambda e, sl=sl, jk=jk: e.activation(
                            out=jk, in_=xr[sl], func=AF.Square, scale=float(2048 ** -0.5),
                            accum_out=stats[:, sl:sl + 1]))
                        kb.wait("act", a1)
                        a2 = kb.op("act", lambda e, sl=sl: e.activation(
                            out=stats[:, 4 + sl:5 + sl], in_=stats[:, sl:sl + 1], func=AF.Sqrt, bias=epsT[:, 0:1]))
                        kb.wait("dve", a2)
                        d1 = kb.op("dve", lambda e, sl=sl: e.reciprocal(out=stats[:, 8 + sl:9 + sl],
                                                                        in_=stats[:, 4 + sl:5 + sl]))
                        kb.wait("dve", d1)
                        kb.wait("dve", lncond)
                        if mode == "mid":
                            d2 = kb.op("dve", lambda e, sl=sl, hs=hs: e.scalar_tensor_tensor(
                                out=hn3[hs], in0=xr[sl], scalar=stats[:, 8 + sl:9 + sl], in1=lnB3,
                                op0=ALU.mult, op1=ALU.mult))
                            frees.append(d2)
                            cur = (hs, t, d2)
                        else:
                            d2 = kb.op("dve", lambda e, sl=sl: e.scalar_tensor_tensor(
                                out=xr[sl], in0=xr[sl], scalar=stats[:, 8 + sl:9 + sl], in1=lnB3,
                                op0=ALU.mult, op1=ALU.mult))
                            kb.wait("act", d2)
                            frees.append(kb.dma("act", out_dst[t * 128:(t + 1) * 128, :], xr[sl], xr_st[sl]))
                    state["xr_free"][sl] = frees
                if pend is not None:
                    hs0, t0, d20 = pend
                    kb.wait("pe", d20)
                    kb.wait("pe", bank_free[4 + 2 * hs0])
                    kb.wait("pe", bank_free[5 + 2 * hs0])
                    pst = bank2_bf(4 + 2 * hs0)
                    for fc in range(16):
                        pc = kb.op("pe", lambda e, hs0=hs0, fc=fc, pst=pst: e.transpose(
                            out=pst[:, fc * 128:(fc + 1) * 128], in_=hn3[hs0][:, fc * 128:(fc + 1) * 128],
                            identity=ident[:]), ms=(fc == 15))
                    state["hn_free"][hs0] = pc
                    kb.wait("act", pc)
                    hdst = yT3[:, :, t0 * 128:(t0 + 1) * 128]
                    ev = kb.op("act", lambda e, hdst=hdst, pst=pst: e.activation(
                        out=hdst, in_=pst.rearrange("p (a b) -> p a b", a=16), func=AF.Copy))
                    bank_free[4 + 2 * hs0] = ev
                    bank_free[5 + 2 * hs0] = ev
                pend = cur

        def wait_stores(eng):
            for s in xo_st + x_st + xr_st:
                if s.count:
                    kb.wait(eng, (s, s.count))

        for (sq, L, items, qjobs) in plan:
            rows = slice(sq * S, (sq + 1) * S)
            first_layer = (L == do_layers[0])
            last_layer = (L == do_layers[-1])
            if L == 0:
                role.update(hnT=bufA3, yT=bufB3, hflat=bufA, yflat=bufB)
            else:
                role.update(hnT=bufB3, yT=bufA3, hflat=bufB, yflat=bufA)
            src = x_d[rows, :] if first_layer else x1_d
            w_out = wab_out if L == 0 else wc_out
            if first_layer:
                wait_stores("sp")
                phase_norm(src, 0 if L == 0 else 1, None)
            hflat = role["hflat"]
            phase_heads(items, after_last_proj=lambda w_out=w_out, hflat=hflat: issue_wq(w_out, hflat))
            wait_stores("sp")
            if not last_layer:
                phase_out(qjobs, src, "mid", x1_d, None, 1)
            elif final_norm:
                phase_out(qjobs, src, "final", None, out_d[rows, :], 2)
            else:
                phase_out(qjobs, src, "plain", None, out_d[rows, :], 0)
        kb.barrier()
        wait_stores("sp")

        with nc.Block() as block:
            @block.tensor
            def _(e):
                for f in kb.q["pe"]:
                    f(e)

            @block.scalar
            def _(e):
                for f in kb.q["act"]:
                    f(e)

            @block.vector
            def _(e):
                for f in kb.q["dve"]:
                    f(e)

            @block.gpsimd
            def _(e):
                for f in kb.q["pool"]:
                    f(e)

            @block.sync
            def _(e):
                for f in kb.q["sp"]:
                    f(e)
    return nc


def _w_in_layout(w):
    C = w.shape[1]
    g = C // 128
    return np.ascontiguousarray(w.reshape(16, 128, g, 128).transpose(2, 1, 0, 3)).reshape(g, 128, 2048)


def _w_out_layout(w):
    return np.ascontiguousarray(w.reshape(16, 128, 4, 512).transpose(2, 1, 0, 3)).reshape(4, 128, 8192)


def _alibi_tables():
    slopes = np.exp2(-8.0 * np.arange(1, 17, dtype=np.float64) / 16)
    p = np.arange(128)[:, None]
    EA = np.zeros((8, 128, 1152), np.float32)
    u = np.arange(1152)[None, :]
    dl = u - 512 - p
    for h in range(8):
        EA[h] = np.where(np.abs(dl) <= 128, np.exp(-slopes[h] * np.abs(dl)), 0.0)
    EB = np.zeros((8, 128, 2944), np.float32)
    u = np.arange(2944)[None, :]
    dl = u - 1408 - p
    ad = np.abs(dl)
    mult = (ad <= 64).astype(np.float64) + ((dl % 4 == 0) & (ad <= 256)) + ((dl % 16 == 0) & (ad <= 1024))
    for h in range(8):
        EB[h] = mult * np.exp(-slopes[8 + h] * ad)
    return EA, EB


def _rpb_strips(rpb):
    p = np.arange(128)
    rl = (p // 64)[:, None, None]
    kc = (p % 64)[:, None, None]
    i = np.arange(24)[None, :, None]
    qc = np.arange(64)[None, None, :]
    dr = 14 - (i - rl - 4) + 0 * qc
    dc = kc - qc + 15 + 0 * i
    c0 = np.clip(qc - 8, 0, 48)
    colok = (kc >= c0) & (kc < c0 + 16) & (i >= 0)
    out = np.full((16, 128, 2, 24, 64), NEGB, np.float32)
    for var, (lo, hi) in enumerate(((3, 10), (0, 14))):
        ok = colok & (dr >= lo) & (dr <= hi)
        drc = np.clip(dr, 0, 14)
        dcc = np.clip(dc, 0, 30)
        gathered = rpb[:, drc, dcc]
        out[:, :, var] = np.where(ok[None], gathered, np.float32(NEGB))
    return out.reshape(16, 128, 3072)


_CACHE = {}


def _get_nc(nseq, do_layers=(0, 1), final_norm=True):
    key = (nseq, tuple(do_layers), final_norm)
    if key not in _CACHE:
        _CACHE[key] = build(nseq, do_layers, final_norm)
    return _CACHE[key]


def _common_inputs(ln_ab, w_in_ab, sink_a, w_out_ab, ln_c, w_in_c, rpb_c, w_out_c, ln_f):
    EA, EB = _alibi_tables()
    return {
        "wab_in": _w_in_layout(np.asarray(w_in_ab[0], np.float32)),
        "wab_out": _w_out_layout(np.asarray(w_out_ab[0], np.float32)),
        "wc_in": _w_in_layout(np.asarray(w_in_c[0], np.float32)),
        "wc_out": _w_out_layout(np.asarray(w_out_c[0], np.float32)),
        "ln": np.ascontiguousarray(np.stack([np.asarray(ln_ab[0]), np.asarray(ln_c[0]), np.asarray(ln_f)]).astype(np.float32)),
        "sink": np.ascontiguousarray(np.asarray(sink_a[0], np.float32)),
        "ident": np.eye(128, dtype=np.float32),
        "EA": EA, "EB": EB,
        "BC": _rpb_strips(np.asarray(rpb_c[0], np.float32)),
    }


def kernel(x, ln_ab, w_in_ab, sink_a, w_out_ab, ln_c, w_in_c, rpb_c, w_out_c, ln_f):
    x = np.asarray(x, np.float32)
    B = x.shape[0]
    nseq = B // N_CORES
    common = _common_inputs(ln_ab, w_in_ab, sink_a, w_out_ab, ln_c, w_in_c, rpb_c, w_out_c, ln_f)
    nc = _get_nc(nseq)
    in_maps = []
    for c in range(N_CORES):
        m = dict(common)
        m["x"] = np.ascontiguousarray(x[c * nseq:(c + 1) * nseq].reshape(nseq * S, D))
        in_maps.append(m)
    res = run_bass_kernel_spmd(nc, in_maps, core_ids=list(range(N_CORES)))
    outs = [np.asarray(r["out"]).reshape(nseq, S, D) for r in res.results]
    return np.concatenate(outs, axis=0).astype(np.float32)
```

```python
import contextlib
import numpy as np
import concourse.bass as bass
import concourse.mybir as mybir
from concourse.bass_utils import run_bass_kernel_spmd

F32 = mybir.dt.float32
BF16 = mybir.dt.bfloat16
AF = mybir.ActivationFunctionType
ALU = mybir.AluOpType

D = 2048
S = 2048
NT = 16
SCALE = float(128 ** -0.5)
NEGB = -30000.0
N_CORES = 8
EW = 3072
NWS = 8

ENGS = ("pe", "act", "dve", "pool", "sp")


class Sem:
    def __init__(self, h):
        self.h = h
        self.count = 0


class KB:
    def __init__(self, nc, es):
        self.nc = nc
        self.es = es
        self.q = {e: [] for e in ENGS}
        self.waited = {}
        self.S = {}
        for e in ("pe", "act", "dve"):
            self.S[e] = self.new_sem("S_" + e)
        self.last = {e: None for e in ("pe", "act", "dve")}

    def new_sem(self, name):
        return Sem(self.es.enter_context(self.nc.semaphore(name)))

    def wait(self, eng, cond):
        if cond is None:
            return
        sem, val = cond
        assert val <= sem.count, (eng, val, sem.count)
        key = (eng, id(sem))
        if self.waited.get(key, 0) >= val:
            return
        self.waited[key] = val
        h = sem.h
        self.q[eng].append(lambda e: e.wait_ge(h, val))

    def op(self, eng, fn, ms=True):
        if not ms:
            self.q[eng].append(fn)
            return None
        s = self.S[eng]
        s.count += 1
        h = s.h
        self.q[eng].append(lambda e: fn(e).then_inc(h, 1))
        self.last[eng] = (s, s.count)
        return (s, s.count)

    def dma(self, eng, out, in_, sem):
        sem.count += 16
        h = sem.h
        self.q[eng].append(lambda e: e.dma_start(out=out, in_=in_).then_inc(h, 16))
        return (sem, sem.count)

    def barrier(self):
        for e in ("pe", "act", "dve", "sp"):
            for o in ("pe", "act", "dve"):
                if o != e:
                    self.wait(e, self.last[o])


class WStream:
    def __init__(self, kb, Wt):
        self.kb = kb
        self.Wt = Wt
        self.jobs = []
        self.pos = 0
        self.next = 0
        self.slot_free = [None] * NWS
        self.slot_pending = [False] * NWS
        self.ld = [kb.new_sem("wld%d" % i) for i in range(NWS)]

    def add_job(self, src, n):
        if n == 4 and self.pos % 4:
            self.pos = (self.pos + 4 - self.pos % 4) % NWS
        job = {"src": src, "n": n, "slot0": self.pos, "ld": None}
        self.pos = (self.pos + n) % NWS
        self.jobs.append(job)
        return job

    def try_issue(self):
        kb = self.kb
        while self.next < len(self.jobs):
            job = self.jobs[self.next]
            slots = range(job["slot0"], job["slot0"] + job["n"])
            if any(self.slot_pending[s] for s in slots):
                return
            for s in slots:
                kb.wait("pool", self.slot_free[s])
                self.slot_pending[s] = True
            dst = self.Wt[:, job["slot0"] * 2048:(job["slot0"] + job["n"]) * 2048]
            job["ld"] = kb.dma("pool", dst, job["src"], self.ld[job["slot0"]])
            self.next += 1

    def release(self, job, cond):
        for s in range(job["slot0"], job["slot0"] + job["n"]):
            self.slot_free[s] = cond
            self.slot_pending[s] = False
        self.try_issue()


def blocks_A():
    res = []
    for b in range(4):
        lst = []
        for j in range(4 * b - 1, 4 * b + 5):
            if 0 <= j < 16:
                c0 = (max(j - 1, 4 * b) - 4 * b) * 128
                c1 = (min(j + 1, 4 * b + 3) + 1 - 4 * b) * 128
                lst.append((j, 512 * b - 128 * j + 512, c0, c1))
        res.append((512 * b, 512, lst))
    return res


def blocks_B():
    res = []
    for b in range(4):
        js = [j for j in range(16) if -1535 <= 512 * b - 128 * j <= 1151]
        res.append((512 * b, 512, [(j, 512 * b - 128 * j + 1408, 0, 512) for j in js]))
    return res


def blocks_C():
    res = [(0, 320, [(j, 1536 + (11 - 2 * j) * 64, 0, 320) for j in range(4)])]
    for ra, nr, j0 in ((5, 8, 0), (13, 8, 4), (21, 7, 8)):
        res.append((64 * ra, 64 * nr, [(j, (11 - 2 * j + ra) * 64, 0, 64 * nr) for j in range(j0, j0 + 8)]))
    res.append((64 * 28, 256, [(j, 1536 + (11 - 2 * j + 28) * 64, 0, 256) for j in range(12, 16)]))
    return res


def build(nseq, do_layers=(0, 1), final_norm=True):
    nc = bass.Bass("TRN2", target_bir_lowering=False)
    R = nseq * S
    x_d = nc.dram_tensor("x", [R, D], F32, kind="ExternalInput").ap()
    out_d = nc.dram_tensor("out", [R, D], F32, kind="ExternalOutput").ap()
    x1_d = nc.dram_tensor("x1s", [S, D], F32).ap()
    wab_in = nc.dram_tensor("wab_in", [52, 128, 2048], F32, kind="ExternalInput").ap()
    wab_out = nc.dram_tensor("wab_out", [4, 128, 8192], F32, kind="ExternalInput").ap()
    wc_in = nc.dram_tensor("wc_in", [64, 128, 2048], F32, kind="ExternalInput").ap()
    wc_out = nc.dram_tensor("wc_out", [4, 128, 8192], F32, kind="ExternalInput").ap()
    ln_d = nc.dram_tensor("ln", [3, D], F32, kind="ExternalInput").ap()
    sink_d = nc.dram_tensor("sink", [8], F32, kind="ExternalInput").ap()
    ident_d = nc.dram_tensor("ident", [128, 128], F32, kind="ExternalInput").ap()
    EA_d = nc.dram_tensor("EA", [8, 128, 1152], F32, kind="ExternalInput").ap()
    EB_d = nc.dram_tensor("EB", [8, 128, 2944], F32, kind="ExternalInput").ap()
    BC_d = nc.dram_tensor("BC", [16, 128, 3072], F32, kind="ExternalInput").ap()

    with contextlib.ExitStack() as es:
        def sb(name, shape, dt):
            return es.enter_context(nc.sbuf_tensor(name, shape, dt))

        bufA = sb("bufA", [128, 32768], BF16)
        bufB = sb("bufB", [128, 32768], BF16)
        bufA3 = bufA[:, :].rearrange("p (a b) -> p a b", a=16)
        bufB3 = bufB[:, :].rearrange("p (a b) -> p a b", a=16)
        role = {"hnT": bufA3, "yT": bufB3, "hflat": bufA, "yflat": bufB}
        Wt = sb("Wt", [128, NWS * 2048], BF16)
        Eb = sb("Eb", [128, EW], BF16)
        arena = sb("arena", [128, 20480], BF16)
        ident = sb("ident_sb", [128, 128], BF16)
        ones = sb("ones_sb", [128, 128], BF16)
        stats = sb("stats", [128, 24], F32)
        esink = sb("esink", [128, 16], F32)
        epsT = sb("epsT", [128, 1], F32)
        onesf = sb("onesf", [128, 1], F32)
        ps = es.enter_context(nc.psum_tensor("ps", [128, 4096], F32))

        kb = KB(nc, es)
        ws = WStream(kb, Wt)

        def bank(b):
            return ps[:, 512 * b:512 * (b + 1)]

        def bank2_bf(b):
            return ps[:, 512 * b:512 * (b + 2)].bitcast(BF16)

        bank_free = [None] * 8

        def abf(off, n):
            return arena[:, off:off + n]

        def af32(off, n):
            return arena[:, off:off + 2 * n].bitcast(F32)

        qT = abf(0, 2048)
        kTb = [abf(2048, 2048), abf(4096, 2048)]
        vT = abf(6144, 2048)
        sz2 = abf(8192, 2048)
        vtokb = [abf(10240, 2048), abf(12288, 2048)]
        NP = 6
        Pb = [abf(14336 + 512 * i, 512) for i in range(NP)]
        thf = af32(17408, 512)
        rbuf = af32(18432, 512)
        tbuf = af32(19456, 512)
        xt = [af32(0, 2048), af32(4096, 2048), af32(8192, 2048), af32(16384, 2048)]
        hnb = [abf(8192, 2048), abf(10240, 2048)]
        lnB = af32(12288, 2048)
        junk = ps[:, 2048:4096]
        resb = [af32(1024 * i, 512) for i in range(3)]
        xob = [af32(3072 + 1024 * i, 512) for i in range(3)]

        x_ld = [kb.new_sem("xld%d" % i) for i in range(6)]
        x_st = [kb.new_sem("xst%d" % i) for i in range(6)]
        ln_ld = kb.new_sem("lnld")
        e_ld = kb.new_sem("eld")
        c_ld = kb.new_sem("cld")
        r_ld = [kb.new_sem("rld%d" % i) for i in range(3)]
        xo_st = [kb.new_sem("xost%d" % i) for i in range(3)]

        def layer_items(L):
            items = []
            if L == 0:
                for kv in range(2):
                    for gi in range(4):
                        h = kv * 4 + gi
                        items.append(dict(hout=h, gv=10 + kv if gi == 0 else None, gk=8 + kv if gi == 0 else None,
                                          gq=h, gz=36 + h, blocks=blocks_A(), esrc=EA_d[h], ew=1152,
                                          need_exp=False, scol=h))
                for hb in range(8):
                    items.append(dict(hout=8 + hb, gv=28 + hb, gk=20 + hb, gq=12 + hb, gz=44 + hb,
                                      blocks=blocks_B(), esrc=EB_d[hb], ew=2944, need_exp=False, scol=8))
            else:
                for h in range(16):
                    items.append(dict(hout=h, gv=32 + h, gk=16 + h, gq=h, gz=48 + h, blocks=blocks_C(),
                                      esrc=BC_d[h], ew=3072, need_exp=True, scol=8))
            return items

        plan = []
        for sq in range(nseq):
            for L in do_layers:
                w_in = wab_in if L == 0 else wc_in
                w_out = wab_out if L == 0 else wc_out
                items = layer_items(L)
                for it in items:
                    it["jobs"] = {}
                    for kind in ("v", "k", "z", "q"):
                        g = it["g" + kind]
                        if g is not None:
                            it["jobs"][kind] = ws.add_job(w_in[g], 1)
                qjobs = [ws.add_job(w_out[qd], 4) for qd in range(2)]
                plan.append((sq, L, items, qjobs))

        kb.dma("pool", ident[:], ident_d[:, :], c_ld)
        kb.dma("pool", esink[:, 0:8], sink_d.partition_broadcast(128), c_ld)
        kb.op("dve", lambda e: e.memset(ones[:], 1.0))
        kb.op("dve", lambda e: e.memset(epsT[:], 1e-5))
        kb.op("dve", lambda e: e.memset(onesf[:], 1.0))
        kb.op("dve", lambda e: e.memset(esink[:, 8:16], 0.0))
        kb.wait("act", (c_ld, c_ld.count))
        kb.op("act", lambda e: e.activation(out=esink[:, 0:8], in_=esink[:, 0:8], func=AF.Exp))
        kb.wait("pe", (c_ld, c_ld.count))
        ws.try_issue()
        state = {"e_reader": None, "p1n": 0, "p4n": 0, "xt_free": [None] * 6, "hn_free": [None, None],
                 "rb": 0, "sb": 0, "qbn": 0, "r_free": None, "grp": 0,
                 "p3n": 0, "res_free": [None] * 3, "xo_free": [None] * 3, "th_free": None}

        def phase_norm(src, ln_idx, dst):
            kb.barrier()
            lncond = kb.dma("sp", lnB, ln_d[ln_idx].partition_broadcast(128), ln_ld)
            yfl = role["yflat"]
            xt = [yfl[:, i * 4096:(i + 1) * 4096].bitcast(F32) for i in range(6)]
            sids = [0, 1, 2, 3, 4, 5]
            junk_b = yfl[:, 24576:26624]
            ns = len(sids)
            ldc = {}

            def issue_load(t):
                sid = sids[t % ns]
                kb.wait("sp", state["xt_free"][sid])
                ldc[t] = kb.dma("sp", xt[sid], src[t * 128:(t + 1) * 128, :], x_ld[sid])

            for t in range(min(ns - 1, NT)):
                issue_load(t)
            pendq = []
            for t in range(NT + 2):
                if t < NT:
                    if t + ns - 1 < NT:
                        issue_load(t + ns - 1)
                    sid = sids[t % ns]
                    hs = t % 2
                    kb.wait("act", ldc[t])
                    a1 = kb.op("act", lambda e, sid=sid: e.activation(
                        out=junk_b, in_=xt[sid], func=AF.Square, scale=float(2048 ** -0.5),
                        accum_out=stats[:, sid:sid + 1]))
                    kb.wait("act", a1)
                    a2 = kb.op("act", lambda e, sid=sid: e.activation(
                        out=stats[:, 6 + sid:7 + sid], in_=stats[:, sid:sid + 1], func=AF.Sqrt, bias=epsT[:, 0:1]))
                    kb.wait("dve", a2)
                    d1 = kb.op("dve", lambda e, sid=sid: e.reciprocal(out=stats[:, 12 + sid:13 + sid],
                                                                      in_=stats[:, 6 + sid:7 + sid]))
                    kb.wait("dve", d1)
                    kb.wait("dve", lncond)
                    if dst is None:
                        kb.wait("dve", state["hn_free"][hs])
                        d2 = kb.op("dve", lambda e, sid=sid, hs=hs: e.scalar_tensor_tensor(
                            out=hnb[hs], in0=xt[sid], scalar=stats[:, 12 + sid:13 + sid], in1=lnB,
                            op0=ALU.mult, op1=ALU.mult))
                        state["xt_free"][sid] = d2
                        kb.wait("pe", d2)
                        pp = t % 3
                        kb.wait("pe", bank_free[2 * pp])
                        kb.wait("pe", bank_free[2 * pp + 1])
                        pst = bank2_bf(2 * pp)
                        for fc in range(16):
                            pc = kb.op("pe", lambda e, hs=hs, fc=fc, pst=pst: e.transpose(
                                out=pst[:, fc * 128:(fc + 1) * 128], in_=hnb[hs][:, fc * 128:(fc + 1) * 128],
                                identity=ident[:]), ms=(fc == 15))
                        state["hn_free"][hs] = pc
                        cur = (pp, t, pc, pst)
                    else:
                        d2 = kb.op("dve", lambda e, sid=sid: e.scalar_tensor_tensor(
                            out=xt[sid], in0=xt[sid], scalar=stats[:, 12 + sid:13 + sid], in1=lnB,
                            op0=ALU.mult, op1=ALU.mult))
                        kb.wait("sp", d2)
                        stc = kb.dma("sp", dst[t * 128:(t + 1) * 128, :], xt[sid], x_st[sid])
                        state["xt_free"][sid] = stc
                        cur = None
                else:
                    cur = None
                if cur is not None:
                    pendq.append(cur)
                while pendq and (len(pendq) > 2 or t >= NT):
                    hs0, t0, pc0, pst0 = pendq.pop(0)
                    hd0 = role["hnT"][:, 0:8, t0 * 128:(t0 + 1) * 128]
                    hd1 = role["hnT"][:, 8:16, t0 * 128:(t0 + 1) * 128]
                    kb.wait("act", pc0)
                    ev0 = kb.op("act", lambda e, hd0=hd0, pst0=pst0: e.activation(
                        out=hd0, in_=pst0[:, 0:1024].rearrange("p (a b) -> p a b", a=8), func=AF.Copy))
                    kb.wait("dve", pc0)
                    ev1 = kb.op("dve", lambda e, hd1=hd1, pst0=pst0: e.tensor_copy(
                        out=hd1, in_=pst0[:, 1024:2048].rearrange("p (a b) -> p a b", a=8)))
                    bank_free[2 * hs0] = ev0
                    bank_free[2 * hs0 + 1] = ev1

        def proj_fill(job, kind, tg, dst):
            slot = job["slot0"]
            kb.wait("pe", job["ld"])
            bk = state["rb"] % 4
            state["rb"] += 1
            kb.wait("pe", bank_free[bk])
            for kc in range(16):
                rhs_ap = role["hnT"][:, kc, tg * 512:(tg + 1) * 512]
                mc = kb.op("pe", lambda e, rhs_ap=rhs_ap, kc=kc, slot=slot, bk=bk: e.matmul(
                    bank(bk), lhsT=Wt[:, slot * 2048 + kc * 128: slot * 2048 + (kc + 1) * 128],
                    rhs=rhs_ap, start=(kc == 0), stop=(kc == 15)),
                    ms=(kc == 15))
            if tg == 3:
                ws.release(job, mc)
            cols = slice(tg * 512, (tg + 1) * 512)
            if kind == "q":
                kb.wait("act", mc)
                ev = kb.op("act", lambda e, bk=bk, cols=cols: e.activation(
                    out=dst[:, cols], in_=bank(bk), func=AF.Copy))
            elif kind in ("k", "v"):
                kb.wait("dve", mc)
                ev = kb.op("dve", lambda e, bk=bk, cols=cols: e.tensor_copy(out=dst[:, cols], in_=bank(bk)))
            else:
                kb.wait("act", mc)
                kb.wait("act", state["th_free"])
                a1 = kb.op("act", lambda e, bk=bk: e.activation(out=thf, in_=bank(bk), func=AF.Exp, scale=-1.0))
                kb.wait("act", a1)
                a2 = kb.op("act", lambda e: e.activation(out=thf, in_=thf, func=AF.Ln, bias=onesf[:, 0:1]))
                kb.wait("act", a2)
                a3 = kb.op("act", lambda e: e.activation(out=thf, in_=thf, func=AF.Exp, scale=-1.0))
                kb.wait("dve", a3)
                ev = kb.op("dve", lambda e, bk=bk, cols=cols: e.tensor_tensor(
                    out=dst[:, cols], in0=thf, in1=bank(bk), op=ALU.mult))
                state["th_free"] = ev
            bank_free[bk] = ev
            return ev

        def proj(job, kind, dst):
            ev = None
            for tg in range(4):
                ev = proj_fill(job, kind, tg, dst)
            return ev

        def vtrans(vcond, vdst):
            kb.wait("pe", vcond)
            kb.wait("pe", bank_free[0])
            kb.wait("pe", bank_free[1])
            psv = bank2_bf(0)
            for j in range(16):
                pc = kb.op("pe", lambda e, j=j: e.transpose(
                    out=psv[:, j * 128:(j + 1) * 128], in_=vT[:, j * 128:(j + 1) * 128], identity=ident[:]),
                    ms=(j == 15))
            kb.wait("dve", pc)
            ev = kb.op("dve", lambda e: e.tensor_copy(out=vdst, in_=psv))
            bank_free[0] = ev
            bank_free[1] = ev
            return ev

        class FillStream:
            def __init__(self, tasks):
                self.tasks = tasks
                self.ti = 0
                self.kc = 0

            def remaining(self):
                return sum(t[6] for t in self.tasks[self.ti:]) - self.kc

            def emit(self, nmm):
                while nmm > 0 and self.ti < len(self.tasks):
                    job, kind, tg, dst, st, key, nops = self.tasks[self.ti]
                    bk = 3
                    if self.kc == 0:
                        if kind == "vt":
                            kb.wait("pe", st["v"])
                        else:
                            kb.wait("pe", job["ld"])
                        kb.wait("pe", bank_free[bk])
                    kc = self.kc
                    if kind == "vt":
                        psv = bank(bk).bitcast(BF16)
                        jt = tg * 8 + kc
                        mc = kb.op("pe", lambda e, kc=kc, jt=jt, psv=psv: e.transpose(
                            out=psv[:, kc * 128:(kc + 1) * 128], in_=vT[:, jt * 128:(jt + 1) * 128],
                            identity=ident[:]), ms=(kc == nops - 1))
                    else:
                        slot = job["slot0"]
                        rhs_ap = role["hnT"][:, kc, tg * 512:(tg + 1) * 512]
                        mc = kb.op("pe", lambda e, rhs_ap=rhs_ap, kc=kc, slot=slot, bk=bk: e.matmul(
                            bank(bk), lhsT=Wt[:, slot * 2048 + kc * 128: slot * 2048 + (kc + 1) * 128],
                            rhs=rhs_ap, start=(kc == 0), stop=(kc == 15)),
                            ms=(kc == 15))
                    self.kc += 1
                    nmm -= 1
                    if self.kc == nops:
                        kb.wait("dve", mc)
                        if kind == "vt":
                            ev = kb.op("dve", lambda e, tg=tg, dst=dst, psv=psv: e.tensor_copy(
                                out=dst[:, tg * 1024:(tg + 1) * 1024], in_=psv))
                        else:
                            if tg == 3:
                                ws.release(job, mc)
                            cols = slice(tg * 512, (tg + 1) * 512)
                            ev = kb.op("dve", lambda e, bk=bk, cols=cols, dst=dst: e.tensor_copy(
                                out=dst[:, cols], in_=bank(bk)))
                        bank_free[bk] = ev
                        st[key] = ev
                        self.ti += 1
                        self.kc = 0
                        return

            def flush(self):
                while self.ti < len(self.tasks):
                    self.emit(16)

        def attention(it, econd, qcond, kcond, vcond, zcond, kT, vtok, fillers):
            LA = 2
            G = []
            for bi, (q0, n, lst) in enumerate(it["blocks"]):
                for i, (j, eoff, c0, c1) in enumerate(lst):
                    G.append((q0, n, j, eoff, i == 0, i == len(lst) - 1, c0, c1))
            hout = it["hout"]
            scol = it["scol"]
            p_free = state.setdefault("p_free", [None] * NP)
            gb = state.setdefault("gblk", 0)
            scond = {}
            sbank = {}
            pend_fin = []

            def emit_S(g):
                q0, n, j, eoff, first, last, c0, c1 = G[g]
                bk = state["sb"] % 3
                state["sb"] += 1
                sbank[g] = bk
                kb.wait("pe", bank_free[bk])
                if g == 0:
                    kb.wait("pe", qcond)
                    kb.wait("pe", kcond)
                scond[g] = kb.op("pe", lambda e, bk=bk, j=j, q0=q0, c0=c0, c1=c1: e.matmul(
                    bank(bk)[:, c0:c1], lhsT=kT[:, j * 128:(j + 1) * 128], rhs=qT[:, q0 + c0:q0 + c1],
                    start=True, stop=True))

            for g in range(min(LA, len(G))):
                emit_S(g)
            mul_last = None
            for g in range(len(G)):
                q0, n, j, eoff, first, last, c0, c1 = G[g]
                bk = sbank[g]
                psl = (gb + g) % NP
                if first:
                    state["qbn"] += 1
                ob = 4 + 2 * (state["qbn"] % 2)
                kb.wait("act", scond[g])
                kb.wait("act", p_free[psl])
                ec = kb.op("act", lambda e, bk=bk, psl=psl, c0=c0, c1=c1: e.activation(
                    out=Pb[psl][:, c0:c1], in_=bank(bk)[:, c0:c1], func=AF.Exp, scale=SCALE))
                bank_free[bk] = ec
                kb.wait("dve", ec)
                kb.wait("dve", econd)
                mc = kb.op("dve", lambda e, psl=psl, eoff=eoff, c0=c0, c1=c1: e.tensor_tensor(
                    out=Pb[psl][:, c0:c1], in0=Pb[psl][:, c0:c1], in1=Eb[:, eoff + c0:eoff + c1], op=ALU.mult))
                mul_last = mc
                if g + LA < len(G):
                    emit_S(g + LA)
                if fillers is not None and fillers.remaining() > 0:
                    nb = len(G) - g
                    fillers.emit(-(-fillers.remaining() // nb) + (4 if g == 0 else 0))
                kb.wait("pe", mc)
                if first:
                    kb.wait("pe", bank_free[ob])
                    kb.wait("pe", bank_free[ob + 1])
                if g == 0:
                    kb.wait("pe", vcond)
                kb.op("pe", lambda e, j=j, psl=psl, first=first, last=last, ob=ob, c0=c0, c1=c1: e.matmul(
                    bank(ob)[:, c0:c1], lhsT=vtok[:, j * 128:(j + 1) * 128], rhs=Pb[psl][:, c0:c1],
                    start=first, stop=last, skip_group_check=True), ms=False)
                pv = kb.op("pe", lambda e, psl=psl, first=first, last=last, ob=ob, c0=c0, c1=c1: e.matmul(
                    bank(ob + 1)[:, c0:c1], lhsT=ones[:], rhs=Pb[psl][:, c0:c1], start=first, stop=last,
                    skip_group_check=True))
                p_free[psl] = pv
                if last:
                    pend_fin.append((g + 2, pv, n, q0, ob))
                while pend_fin and (pend_fin[0][0] <= g or g == len(G) - 1):
                    _, pvc, fn_, fq0, fob = pend_fin.pop(0)
                    kb.wait("act", pvc)
                    kb.wait("act", state["r_free"])
                    f1 = kb.op("act", lambda e, n=fn_, ob=fob: e.activation(
                        out=rbuf[:, 0:n], in_=bank(ob + 1)[:, 0:n], func=AF.Ln, bias=esink[:, scol:scol + 1]))
                    kb.wait("act", f1)
                    f2 = kb.op("act", lambda e, n=fn_: e.activation(
                        out=rbuf[:, 0:n], in_=rbuf[:, 0:n], func=AF.Exp, scale=-1.0))
                    kb.wait("dve", f2)
                    f3 = kb.op("dve", lambda e, n=fn_, ob=fob: e.tensor_tensor(
                        out=tbuf[:, 0:n], in0=bank(ob)[:, 0:n], in1=rbuf[:, 0:n], op=ALU.mult))
                    bank_free[fob] = f3
                    bank_free[fob + 1] = f3
                    state["r_free"] = f3
                    kb.wait("dve", f3)
                    kb.wait("dve", zcond)
                    ydst = role["yT"][:, hout, fq0:fq0 + fn_]
                    kb.op("dve", lambda e, n=fn_, q0=fq0, ydst=ydst: e.tensor_tensor(
                        out=ydst, in0=tbuf[:, 0:n], in1=sz2[:, q0:q0 + n], op=ALU.mult))
            state["gblk"] = gb + len(G)
            state["e_reader"] = mul_last

        def phase_heads(items, after_last_proj=None):
            kb.barrier()
            for eng_ in ("pe", "act", "dve"):
                wait_stores(eng_)
            grp = state["grp"]
            kvst = {}

            def kv_stream(it, alt):
                st = {}
                tasks = [(it["jobs"]["v"], "v", tg, vT, st, "v", 16) for tg in range(4)]
                tasks += [(None, "vt", hf, vtokb[alt], st, "vt", 8) for hf in range(2)]
                tasks += [(it["jobs"]["k"], "k", tg, kTb[alt], st, "k", 16) for tg in range(4)]
                return st, FillStream(tasks)

            pending = None
            for i, it in enumerate(items):
                kb.wait("pool", state["e_reader"])
                ldc = kb.dma("pool", Eb[:, 0:it["ew"]], it["esrc"], e_ld)
                if it["need_exp"]:
                    kb.wait("act", ldc)
                    econd = kb.op("act", lambda e, w=it["ew"]: e.activation(
                        out=Eb[:, 0:w], in_=Eb[:, 0:w], func=AF.Exp))
                else:
                    econd = ldc
                jobs = it["jobs"]
                if "v" in jobs:
                    if pending is None or pending[0] != i:
                        grp += 1
                        st, stream = kv_stream(it, grp % 2)
                        pending = (i, st, stream, grp % 2)
                    _, st, stream, alt = pending
                    stream.flush()
                    kvst = {"k": st["k"], "alt": alt, "v": st["vt"]}
                    pending = None
                zcond = proj(jobs["z"], "z", sz2)
                qcond = proj(jobs["q"], "q", qT)
                if i == len(items) - 1 and after_last_proj is not None:
                    after_last_proj()
                fillers = None
                if i + 1 < len(items) and "v" in items[i + 1]["jobs"]:
                    grp += 1
                    st2, stream2 = kv_stream(items[i + 1], grp % 2)
                    pending = (i + 1, st2, stream2, grp % 2)
                    fillers = stream2
                attention(it, econd, qcond, kvst["k"], kvst["v"], zcond, kTb[kvst["alt"]], vtokb[kvst["alt"]], fillers)
            state["grp"] = grp

        xr = [af32(0, 2048), af32(4096, 2048), af32(8192, 2048)]
        hn3 = [abf(12288, 2048), abf(14336, 2048)]
        lnB3 = af32(16384, 2048)
        wq_ld = [kb.new_sem("wqld%d" % i) for i in range(2)]
        xr_ld = [kb.new_sem("xrld%d" % i) for i in range(3)]
        xr_st = [kb.new_sem("xrst%d" % i) for i in range(3)]
        state["xr_free"] = [[], [], []]

        def issue_wq(w_out, dead_flat):
            kb.wait("pool", kb.last["pe"])
            conds = []
            for i in range(2):
                conds.append(kb.dma("pool", dead_flat[:, i * 8192:(i + 1) * 8192], w_out[2 + i], wq_ld[i]))
            state["wq"] = (conds, dead_flat)

        def phase_out(qjobs, res_src, mode, x1_dst, out_dst, ln_idx):
            kb.barrier()
            yT3 = role["yT"]
            wq_conds, dead_flat = state["wq"]
            if mode != "plain":
                lncond = kb.dma("sp", lnB3, ln_d[ln_idx].partition_broadcast(128), ln_ld)
            ldc = {}

            def issue_load(t):
                sl = t % 3
                for c_ in state["xr_free"][sl]:
                    kb.wait("sp", c_)
                ldc[t] = kb.dma("sp", xr[sl], res_src[t * 128:(t + 1) * 128, :], xr_ld[sl])

            def rhs_q(qd, fc):
                if qd < 2:
                    slot = qjobs[qd]["slot0"]
                    return Wt[:, slot * 2048 + fc * 512: slot * 2048 + (fc + 1) * 512]
                return dead_flat[:, (qd - 2) * 8192 + fc * 512:(qd - 2) * 8192 + (fc + 1) * 512]

            issue_load(0)
            issue_load(1)
            pend = None
            for t in range(NT + 1):
                cur = None
                if t < NT:
                    if t + 2 < NT:
                        issue_load(t + 2)
                    sl = t % 3
                    hs = t % 2
                    addc = None
                    for qd in range(4):
                        if t == 0:
                            kb.wait("pe", qjobs[qd]["ld"] if qd < 2 else wq_conds[qd - 2])
                        kb.wait("pe", bank_free[qd])
                        for fc in range(16):
                            lhs_ap = yT3[:, fc, t * 128:(t + 1) * 128]
                            rhs_ap = rhs_q(qd, fc)
                            mc = kb.op("pe", lambda e, qd=qd, fc=fc, lhs_ap=lhs_ap, rhs_ap=rhs_ap: e.matmul(
                                bank(qd), lhsT=lhs_ap, rhs=rhs_ap, start=(fc == 0), stop=(fc == 15)),
                                ms=(fc == 15))
                        if t == NT - 1 and qd < 2:
                            ws.release(qjobs[qd], mc)
                        kb.wait("dve", mc)
                        kb.wait("dve", ldc[t])
                        addc = kb.op("dve", lambda e, qd=qd, sl=sl: e.tensor_tensor(
                            out=xr[sl][:, qd * 512:(qd + 1) * 512], in0=bank(qd),
                            in1=xr[sl][:, qd * 512:(qd + 1) * 512], op=ALU.add))
                        bank_free[qd] = addc
                    frees = []
                    if mode in ("mid", "plain"):
                        kb.wait("act", addc)
                        dstd = x1_dst if mode == "mid" else out_dst
                        frees.append(kb.dma("act", dstd[t * 128:(t + 1) * 128, :], xr[sl], xr_st[sl]))
                    if mode != "plain":
                        kb.wait("act", addc)
                        if mode == "mid":
                            kb.wait("act", state["hn_free"][hs])
                        jk = hn3[hs] if mode == "mid" else hn3[0]
                        a1 = kb.op("act", lambda e, sl=sl, jk=jk: e.activation(
                            out=jk, in_=xr[sl], func=AF.Square, scale=float(2048 ** -0.5),
                            accum_out=stats[:, sl:sl + 1]))
                        kb.wait("act", a1)
                        a2 = kb.op("act", lambda e, sl=sl: e.activation(
                            out=stats[:, 4 + sl:5 + sl], in_=stats[:, sl:sl + 1], func=AF.Sqrt, bias=epsT[:, 0:1]))
                        kb.wait("dve", a2)
                        d1 = kb.op("dve", lambda e, sl=sl: e.reciprocal(out=stats[:, 8 + sl:9 + sl],
                                                                        in_=stats[:, 4 + sl:5 + sl]))
                        kb.wait("dve", d1)
                        kb.wait("dve", lncond)
                        if mode == "mid":
                            d2 = kb.op("dve", lambda e, sl=sl, hs=hs: e.scalar_tensor_tensor(
                                out=hn3[hs], in0=xr[sl], scalar=stats[:, 8 + sl:9 + sl], in1=lnB3,
                                op0=ALU.mult, op1=ALU.mult))
                            frees.append(d2)
                            cur = (hs, t, d2)
                        else:
                            d2 = kb.op("dve", lambda e, sl=sl: e.scalar_tensor_tensor(
                                out=xr[sl], in0=xr[sl], scalar=stats[:, 8 + sl:9 + sl], in1=lnB3,
                                op0=ALU.mult, op1=ALU.mult))
                            kb.wait("act", d2)
                            frees.append(kb.dma("act", out_dst[t * 128:(t + 1) * 128, :], xr[sl], xr_st[sl]))
                    state["xr_free"][sl] = frees
                if pend is not None:
                    hs0, t0, d20 = pend
                    kb.wait("pe", d20)
                    kb.wait("pe", bank_free[4 + 2 * hs0])
                    kb.wait("pe", bank_free[5 + 2 * hs0])
                    pst = bank2_bf(4 + 2 * hs0)
                    for fc in range(16):
                        pc = kb.op("pe", lambda e, hs0=hs0, fc=fc, pst=pst: e.transpose(
                            out=pst[:, fc * 128:(fc + 1) * 128], in_=hn3[hs0][:, fc * 128:(fc + 1) * 128],
                            identity=ident[:]), ms=(fc == 15))
                    state["hn_free"][hs0] = pc
                    kb.wait("act", pc)
                    hdst = yT3[:, :, t0 * 128:(t0 + 1) * 128]
                    ev = kb.op("act", lambda e, hdst=hdst, pst=pst: e.activation(
                        out=hdst, in_=pst.rearrange("p (a b) -> p a b", a=16), func=AF.Copy))
                    bank_free[4 + 2 * hs0] = ev
                    bank_free[5 + 2 * hs0] = ev
                pend = cur

        def wait_stores(eng):
            for s in xo_st + x_st + xr_st:
                if s.count:
                    kb.wait(eng, (s, s.count))

        for (sq, L, items, qjobs) in plan:
            rows = slice(sq * S, (sq + 1) * S)
            first_layer = (L == do_layers[0])
            last_layer = (L == do_layers[-1])
            if L == 0:
                role.update(hnT=bufA3, yT=bufB3, hflat=bufA, yflat=bufB)
            else:
                role.update(hnT=bufB3, yT=bufA3, hflat=bufB, yflat=bufA)
            src = x_d[rows, :] if first_layer else x1_d
            w_out = wab_out if L == 0 else wc_out
            if first_layer:
                wait_stores("sp")
                phase_norm(src, 0 if L == 0 else 1, None)
            hflat = role["hflat"]
            phase_heads(items, after_last_proj=lambda w_out=w_out, hflat=hflat: issue_wq(w_out, hflat))
            wait_stores("sp")
            if not last_layer:
                phase_out(qjobs, src, "mid", x1_d, None, 1)
            elif final_norm:
                phase_out(qjobs, src, "final", None, out_d[rows, :], 2)
            else:
                phase_out(qjobs, src, "plain", None, out_d[rows, :], 0)
        kb.barrier()
        wait_stores("sp")

        with nc.Block() as block:
            @block.tensor
            def _(e):
                for f in kb.q["pe"]:
                    f(e)

            @block.scalar
            def _(e):
                for f in kb.q["act"]:
                    f(e)

            @block.vector
            def _(e):
                for f in kb.q["dve"]:
                    f(e)

            @block.gpsimd
            def _(e):
                for f in kb.q["pool"]:
                    f(e)

            @block.sync
            def _(e):
                for f in kb.q["sp"]:
                    f(e)
    return nc


def _w_in_layout(w):
    C = w.shape[1]
    g = C // 128
    return np.ascontiguousarray(w.reshape(16, 128, g, 128).transpose(2, 1, 0, 3)).reshape(g, 128, 2048)


def _w_out_layout(w):
    return np.ascontiguousarray(w.reshape(16, 128, 4, 512).transpose(2, 1, 0, 3)).reshape(4, 128, 8192)


def _alibi_tables():
    slopes = np.exp2(-8.0 * np.arange(1, 17, dtype=np.float64) / 16)
    p = np.arange(128)[:, None]
    EA = np.zeros((8, 128, 1152), np.float32)
    u = np.arange(1152)[None, :]
    dl = u - 512 - p
    for h in range(8):
        EA[h] = np.where(np.abs(dl) <= 128, np.exp(-slopes[h] * np.abs(dl)), 0.0)
    EB = np.zeros((8, 128, 2944), np.float32)
    u = np.arange(2944)[None, :]
    dl = u - 1408 - p
    ad = np.abs(dl)
    mult = (ad <= 64).astype(np.float64) + ((dl % 4 == 0) & (ad <= 256)) + ((dl % 16 == 0) & (ad <= 1024))
    for h in range(8):
        EB[h] = mult * np.exp(-slopes[8 + h] * ad)
    return EA, EB


def _rpb_strips(rpb):
    p = np.arange(128)
    rl = (p // 64)[:, None, None]
    kc = (p % 64)[:, None, None]
    i = np.arange(24)[None, :, None]
    qc = np.arange(64)[None, None, :]
    dr = 14 - (i - rl - 4) + 0 * qc
    dc = kc - qc + 15 + 0 * i
    c0 = np.clip(qc - 8, 0, 48)
    colok = (kc >= c0) & (kc < c0 + 16) & (i >= 0)
    out = np.full((16, 128, 2, 24, 64), NEGB, np.float32)
    for var, (lo, hi) in enumerate(((3, 10), (0, 14))):
        ok = colok & (dr >= lo) & (dr <= hi)
        drc = np.clip(dr, 0, 14)
        dcc = np.clip(dc, 0, 30)
        gathered = rpb[:, drc, dcc]
        out[:, :, var] = np.where(ok[None], gathered, np.float32(NEGB))
    return out.reshape(16, 128, 3072)


_CACHE = {}


def _get_nc(nseq, do_layers=(0, 1), final_norm=True):
    key = (nseq, tuple(do_layers), final_norm)
    if key not in _CACHE:
        _CACHE[key] = build(nseq, do_layers, final_norm)
    return _CACHE[key]


def _common_inputs(ln_ab, w_in_ab, sink_a, w_out_ab, ln_c, w_in_c, rpb_c, w_out_c, ln_f):
    EA, EB = _alibi_tables()
    return {
        "wab_in": _w_in_layout(np.asarray(w_in_ab[0], np.float32)),
        "wab_out": _w_out_layout(np.asarray(w_out_ab[0], np.float32)),
        "wc_in": _w_in_layout(np.asarray(w_in_c[0], np.float32)),
        "wc_out": _w_out_layout(np.asarray(w_out_c[0], np.float32)),
        "ln": np.ascontiguousarray(np.stack([np.asarray(ln_ab[0]), np.asarray(ln_c[0]), np.asarray(ln_f)]).astype(np.float32)),
        "sink": np.ascontiguousarray(np.asarray(sink_a[0], np.float32)),
        "ident": np.eye(128, dtype=np.float32),
        "EA": EA, "EB": EB,
        "BC": _rpb_strips(np.asarray(rpb_c[0], np.float32)),
    }


def kernel(x, ln_ab, w_in_ab, sink_a, w_out_ab, ln_c, w_in_c, rpb_c, w_out_c, ln_f):
    x = np.asarray(x, np.float32)
    B = x.shape[0]
    nseq = B // N_CORES
    common = _common_inputs(ln_ab, w_in_ab, sink_a, w_out_ab, ln_c, w_in_c, rpb_c, w_out_c, ln_f)
    nc = _get_nc(nseq)
    in_maps = []
    for c in range(N_CORES):
        m = dict(common)
        m["x"] = np.ascontiguousarray(x[c * nseq:(c + 1) * nseq].reshape(nseq * S, D))
        in_maps.append(m)
    res = run_bass_kernel_spmd(nc, in_maps, core_ids=list(range(N_CORES)))
    outs = [np.asarray(r["out"]).reshape(nseq, S, D) for r in res.results]
    return np.concatenate(outs, axis=0).astype(np.float32)
```

```python
import contextlib
import numpy as np
import concourse.bass as bass
import concourse.mybir as mybir
from concourse.bass_utils import run_bass_kernel_spmd

F32 = mybir.dt.float32
BF16 = mybir.dt.bfloat16
AF = mybir.ActivationFunctionType
ALU = mybir.AluOpType

D = 2048
S = 2048
NT = 16
SCALE = float(128 ** -0.5)
NEGB = -30000.0
N_CORES = 8
EW = 3072
NWS = 8

ENGS = ("pe", "act", "dve", "pool", "sp")


class Sem:
    def __init__(self, h):
        self.h = h
        self.count = 0


class KB:
    def __init__(self, nc, es):
        self.nc = nc
        self.es = es
        self.q = {e: [] for e in ENGS}
        self.waited = {}
        self.S = {}
        for e in ("pe", "act", "dve"):
            self.S[e] = self.new_sem("S_" + e)
        self.last = {e: None for e in ("pe", "act", "dve")}

    def new_sem(self, name):
        return Sem(self.es.enter_context(self.nc.semaphore(name)))

    def wait(self, eng, cond):
        if cond is None:
            return
        sem, val = cond
        assert val <= sem.count, (eng, val, sem.count)
        key = (eng, id(sem))
        if self.waited.get(key, 0) >= val:
            return
        self.waited[key] = val
        h = sem.h
        self.q[eng].append(lambda e: e.wait_ge(h, val))

    def op(self, eng, fn, ms=True):
        if not ms:
            self.q[eng].append(fn)
            return None
        s = self.S[eng]
        s.count += 1
        h = s.h
        self.q[eng].append(lambda e: fn(e).then_inc(h, 1))
        self.last[eng] = (s, s.count)
        return (s, s.count)

    def dma(self, eng, out, in_, sem):
        sem.count += 16
        h = sem.h
        self.q[eng].append(lambda e: e.dma_start(out=out, in_=in_).then_inc(h, 16))
        return (sem, sem.count)

    def barrier(self):
        for e in ("pe", "act", "dve", "sp"):
            for o in ("pe", "act", "dve"):
                if o != e:
                    self.wait(e, self.last[o])


class WStream:
    def __init__(self, kb, Wt):
        self.kb = kb
        self.Wt = Wt
        self.jobs = []
        self.pos = 0
        self.next = 0
        self.slot_free = [None] * NWS
        self.slot_pending = [False] * NWS
        self.ld = [kb.new_sem("wld%d" % i) for i in range(NWS)]

    def add_job(self, src, n):
        if n == 4 and self.pos % 4:
            self.pos = (self.pos + 4 - self.pos % 4) % NWS
        job = {"src": src, "n": n, "slot0": self.pos, "ld": None}
        self.pos = (self.pos + n) % NWS
        self.jobs.append(job)
        return job

    def try_issue(self):
        kb = self.kb
        while self.next < len(self.jobs):
            job = self.jobs[self.next]
            slots = range(job["slot0"], job["slot0"] + job["n"])
            if any(self.slot_pending[s] for s in slots):
                return
            for s in slots:
                kb.wait("pool", self.slot_free[s])
                self.slot_pending[s] = True
            dst = self.Wt[:, job["slot0"] * 2048:(job["slot0"] + job["n"]) * 2048]
            job["ld"] = kb.dma("pool", dst, job["src"], self.ld[job["slot0"]])
            self.next += 1

    def release(self, job, cond):
        for s in range(job["slot0"], job["slot0"] + job["n"]):
            self.slot_free[s] = cond
            self.slot_pending[s] = False
        self.try_issue()


def blocks_A():
    res = []
    for b in range(4):
        lst = []
        for j in range(4 * b - 1, 4 * b + 5):
            if 0 <= j < 16:
                c0 = (max(j - 1, 4 * b) - 4 * b) * 128
                c1 = (min(j + 1, 4 * b + 3) + 1 - 4 * b) * 128
                lst.append((j, 512 * b - 128 * j + 512, c0, c1))
        res.append((512 * b, 512, lst))
    return res


def blocks_B():
    res = []
    for b in range(4):
        lst = []
        for j in range(16):
            off = 512 * b - 128 * j
            if not (-1535 <= off <= 1151):
                continue
            c0 = (max(0, -1024 - off) // 128) * 128
            c1 = -(-min(512, 1024 - off + 128) // 128) * 128
            lst.append((j, off + 1408, c0, c1))
        res.append((512 * b, 512, lst))
    return res


def blocks_C():
    res = [(0, 320, [(j, 1536 + (11 - 2 * j) * 64, 0, 320) for j in range(4)])]
    for ra, nr, j0 in ((5, 8, 0), (13, 8, 4), (21, 7, 8)):
        lst = []
        for j in range(j0, j0 + 8):
            r_lo = max(ra, 2 * j - 3)
            r_hi = min(ra + nr - 1, 2 * j + 5)
            c0 = ((r_lo - ra) * 64 // 128) * 128
            c1 = min(64 * nr, -(-((r_hi - ra + 1) * 64) // 128) * 128)
            lst.append((j, (11 - 2 * j + ra) * 64, c0, c1))
        res.append((64 * ra, 64 * nr, lst))
    res.append((64 * 28, 256, [(j, 1536 + (11 - 2 * j + 28) * 64, 0, 256) for j in range(12, 16)]))
    return res


def build(nseq, do_layers=(0, 1), final_norm=True):
    nc = bass.Bass("TRN2", target_bir_lowering=False)
    R = nseq * S
    x_d = nc.dram_tensor("x", [R, D], F32, kind="ExternalInput").ap()
    out_d = nc.dram_tensor("out", [R, D], F32, kind="ExternalOutput").ap()
    x1_d = nc.dram_tensor("x1s", [S, D], F32).ap()
    wab_in = nc.dram_tensor("wab_in", [52, 128, 2048], F32, kind="ExternalInput").ap()
    wab_out = nc.dram_tensor("wab_out", [4, 128, 8192], F32, kind="ExternalInput").ap()
    wc_in = nc.dram_tensor("wc_in", [64, 128, 2048], F32, kind="ExternalInput").ap()
    wc_out = nc.dram_tensor("wc_out", [4, 128, 8192], F32, kind="ExternalInput").ap()
    ln_d = nc.dram_tensor("ln", [3, D], F32, kind="ExternalInput").ap()
    sink_d = nc.dram_tensor("sink", [8], F32, kind="ExternalInput").ap()
    ident_d = nc.dram_tensor("ident", [128, 128], F32, kind="ExternalInput").ap()
    EA_d = nc.dram_tensor("EA", [8, 128, 1152], F32, kind="ExternalInput").ap()
    EB_d = nc.dram_tensor("EB", [8, 128, 2944], F32, kind="ExternalInput").ap()
    BC_d = nc.dram_tensor("BC", [16, 128, 3072], F32, kind="ExternalInput").ap()

    with contextlib.ExitStack() as es:
        def sb(name, shape, dt):
            return es.enter_context(nc.sbuf_tensor(name, shape, dt))

        bufA = sb("bufA", [128, 32768], BF16)
        bufB = sb("bufB", [128, 32768], BF16)
        bufA3 = bufA[:, :].rearrange("p (a b) -> p a b", a=16)
        bufB3 = bufB[:, :].rearrange("p (a b) -> p a b", a=16)
        role = {"hnT": bufA3, "yT": bufB3, "hflat": bufA, "yflat": bufB}
        Wt = sb("Wt", [128, NWS * 2048], BF16)
        Eb = sb("Eb", [128, EW], BF16)
        arena = sb("arena", [128, 20480], BF16)
        ident = sb("ident_sb", [128, 128], BF16)
        ones = sb("ones_sb", [128, 128], BF16)
        stats = sb("stats", [128, 24], F32)
        esink = sb("esink", [128, 16], F32)
        epsT = sb("epsT", [128, 1], F32)
        onesf = sb("onesf", [128, 1], F32)
        ps = es.enter_context(nc.psum_tensor("ps", [128, 4096], F32))

        kb = KB(nc, es)
        ws = WStream(kb, Wt)

        def bank(b):
            return ps[:, 512 * b:512 * (b + 1)]

        def bank2_bf(b):
            return ps[:, 512 * b:512 * (b + 2)].bitcast(BF16)

        bank_free = [None] * 8

        def abf(off, n):
            return arena[:, off:off + n]

        def af32(off, n):
            return arena[:, off:off + 2 * n].bitcast(F32)

        qT = abf(0, 2048)
        kTb = [abf(2048, 2048), abf(4096, 2048)]
        vT = abf(6144, 2048)
        sz2 = abf(8192, 2048)
        vtokb = [abf(10240, 2048), abf(12288, 2048)]
        NP = 6
        Pb = [abf(14336 + 512 * i, 512) for i in range(NP)]
        thf = af32(17408, 512)
        rbuf = af32(18432, 512)
        tbuf = af32(19456, 512)
        xt = [af32(0, 2048), af32(4096, 2048), af32(8192, 2048), af32(16384, 2048)]
        hnb = [abf(8192, 2048), abf(10240, 2048)]
        lnB = af32(12288, 2048)
        junk = ps[:, 2048:4096]
        resb = [af32(1024 * i, 512) for i in range(3)]
        xob = [af32(3072 + 1024 * i, 512) for i in range(3)]

        x_ld = [kb.new_sem("xld%d" % i) for i in range(6)]
        x_st = [kb.new_sem("xst%d" % i) for i in range(6)]
        ln_ld = kb.new_sem("lnld")
        e_ld = kb.new_sem("eld")
        c_ld = kb.new_sem("cld")
        r_ld = [kb.new_sem("rld%d" % i) for i in range(3)]
        xo_st = [kb.new_sem("xost%d" % i) for i in range(3)]

        def layer_items(L):
            items = []
            if L == 0:
                for kv in range(2):
                    for gi in range(4):
                        h = kv * 4 + gi
                        items.append(dict(hout=h, gv=10 + kv if gi == 0 else None, gk=8 + kv if gi == 0 else None,
                                          gq=h, gz=36 + h, blocks=blocks_A(), esrc=EA_d[h], ew=1152,
                                          need_exp=False, scol=h))
                for hb in range(8):
                    items.append(dict(hout=8 + hb, gv=28 + hb, gk=20 + hb, gq=12 + hb, gz=44 + hb,
                                      blocks=blocks_B(), esrc=EB_d[hb], ew=2944, need_exp=False, scol=8))
            else:
                for h in range(16):
                    items.append(dict(hout=h, gv=32 + h, gk=16 + h, gq=h, gz=48 + h, blocks=blocks_C(),
                                      esrc=BC_d[h], ew=3072, need_exp=True, scol=8))
            return items

        plan = []
        for sq in range(nseq):
            for L in do_layers:
                w_in = wab_in if L == 0 else wc_in
                w_out = wab_out if L == 0 else wc_out
                items = layer_items(L)
                for it in items:
                    it["jobs"] = {}
                    for kind in ("v", "k", "z", "q"):
                        g = it["g" + kind]
                        if g is not None:
                            it["jobs"][kind] = ws.add_job(w_in[g], 1)
                qjobs = [ws.add_job(w_out[qd], 4) for qd in range(2)]
                plan.append((sq, L, items, qjobs))

        kb.dma("pool", ident[:], ident_d[:, :], c_ld)
        kb.dma("pool", esink[:, 0:8], sink_d.partition_broadcast(128), c_ld)
        kb.op("dve", lambda e: e.memset(ones[:], 1.0))
        kb.op("dve", lambda e: e.memset(epsT[:], 1e-5))
        kb.op("dve", lambda e: e.memset(onesf[:], 1.0))
        kb.op("dve", lambda e: e.memset(esink[:, 8:16], 0.0))
        kb.wait("act", (c_ld, c_ld.count))
        kb.op("act", lambda e: e.activation(out=esink[:, 0:8], in_=esink[:, 0:8], func=AF.Exp))
        kb.wait("pe", (c_ld, c_ld.count))
        ws.try_issue()
        state = {"e_reader": None, "p1n": 0, "p4n": 0, "xt_free": [None] * 6, "hn_free": [None, None],
                 "rb": 0, "sb": 0, "qbn": 0, "r_free": None, "grp": 0,
                 "p3n": 0, "res_free": [None] * 3, "xo_free": [None] * 3, "th_free": None}

        def phase_norm(src, ln_idx, dst):
            kb.barrier()
            lncond = kb.dma("sp", lnB, ln_d[ln_idx].partition_broadcast(128), ln_ld)
            yfl = role["yflat"]
            xt = [yfl[:, i * 4096:(i + 1) * 4096].bitcast(F32) for i in range(6)]
            sids = [0, 1, 2, 3, 4, 5]
            junk_b = yfl[:, 24576:26624]
            ns = len(sids)
            ldc = {}

            def issue_load(t):
                sid = sids[t % ns]
                kb.wait("sp", state["xt_free"][sid])
                ldc[t] = kb.dma("sp", xt[sid], src[t * 128:(t + 1) * 128, :], x_ld[sid])

            for t in range(min(ns - 1, NT)):
                issue_load(t)
            pendq = []
            for t in range(NT + 2):
                if t < NT:
                    if t + ns - 1 < NT:
                        issue_load(t + ns - 1)
                    sid = sids[t % ns]
                    hs = t % 2
                    kb.wait("act", ldc[t])
                    a1 = kb.op("act", lambda e, sid=sid: e.activation(
                        out=junk_b, in_=xt[sid], func=AF.Square, scale=float(2048 ** -0.5),
                        accum_out=stats[:, sid:sid + 1]))
                    kb.wait("act", a1)
                    a2 = kb.op("act", lambda e, sid=sid: e.activation(
                        out=stats[:, 6 + sid:7 + sid], in_=stats[:, sid:sid + 1], func=AF.Sqrt, bias=epsT[:, 0:1]))
                    kb.wait("dve", a2)
                    d1 = kb.op("dve", lambda e, sid=sid: e.reciprocal(out=stats[:, 12 + sid:13 + sid],
                                                                      in_=stats[:, 6 + sid:7 + sid]))
                    kb.wait("dve", d1)
                    kb.wait("dve", lncond)
                    if dst is None:
                        kb.wait("dve", state["hn_free"][hs])
                        d2 = kb.op("dve", lambda e, sid=sid, hs=hs: e.scalar_tensor_tensor(
                            out=hnb[hs], in0=xt[sid], scalar=stats[:, 12 + sid:13 + sid], in1=lnB,
                            op0=ALU.mult, op1=ALU.mult))
                        state["xt_free"][sid] = d2
                        kb.wait("pe", d2)
                        pp = t % 3
                        kb.wait("pe", bank_free[2 * pp])
                        kb.wait("pe", bank_free[2 * pp + 1])
                        pst = bank2_bf(2 * pp)
                        for fc in range(16):
                            pc = kb.op("pe", lambda e, hs=hs, fc=fc, pst=pst: e.transpose(
                                out=pst[:, fc * 128:(fc + 1) * 128], in_=hnb[hs][:, fc * 128:(fc + 1) * 128],
                                identity=ident[:]), ms=(fc == 15))
                        state["hn_free"][hs] = pc
                        cur = (pp, t, pc, pst)
                    else:
                        d2 = kb.op("dve", lambda e, sid=sid: e.scalar_tensor_tensor(
                            out=xt[sid], in0=xt[sid], scalar=stats[:, 12 + sid:13 + sid], in1=lnB,
                            op0=ALU.mult, op1=ALU.mult))
                        kb.wait("sp", d2)
                        stc = kb.dma("sp", dst[t * 128:(t + 1) * 128, :], xt[sid], x_st[sid])
                        state["xt_free"][sid] = stc
                        cur = None
                else:
                    cur = None
                if cur is not None:
                    pendq.append(cur)
                while pendq and (len(pendq) > 2 or t >= NT):
                    hs0, t0, pc0, pst0 = pendq.pop(0)
                    hd0 = role["hnT"][:, 0:8, t0 * 128:(t0 + 1) * 128]
                    hd1 = role["hnT"][:, 8:16, t0 * 128:(t0 + 1) * 128]
                    kb.wait("act", pc0)
                    ev0 = kb.op("act", lambda e, hd0=hd0, pst0=pst0: e.activation(
                        out=hd0, in_=pst0[:, 0:1024].rearrange("p (a b) -> p a b", a=8), func=AF.Copy))
                    kb.wait("dve", pc0)
                    ev1 = kb.op("dve", lambda e, hd1=hd1, pst0=pst0: e.tensor_copy(
                        out=hd1, in_=pst0[:, 1024:2048].rearrange("p (a b) -> p a b", a=8)))
                    bank_free[2 * hs0] = ev0
                    bank_free[2 * hs0 + 1] = ev1

        def proj_fill(job, kind, tg, dst):
            slot = job["slot0"]
            kb.wait("pe", job["ld"])
            bk = state["rb"] % 4
            state["rb"] += 1
            kb.wait("pe", bank_free[bk])
            for kc in range(16):
                rhs_ap = role["hnT"][:, kc, tg * 512:(tg + 1) * 512]
                mc = kb.op("pe", lambda e, rhs_ap=rhs_ap, kc=kc, slot=slot, bk=bk: e.matmul(
                    bank(bk), lhsT=Wt[:, slot * 2048 + kc * 128: slot * 2048 + (kc + 1) * 128],
                    rhs=rhs_ap, start=(kc == 0), stop=(kc == 15)),
                    ms=(kc == 15))
            if tg == 3:
                ws.release(job, mc)
            cols = slice(tg * 512, (tg + 1) * 512)
            if kind == "q":
                kb.wait("act", mc)
                ev = kb.op("act", lambda e, bk=bk, cols=cols: e.activation(
                    out=dst[:, cols], in_=bank(bk), func=AF.Copy))
            elif kind in ("k", "v"):
                kb.wait("dve", mc)
                ev = kb.op("dve", lambda e, bk=bk, cols=cols: e.tensor_copy(out=dst[:, cols], in_=bank(bk)))
            else:
                kb.wait("act", mc)
                kb.wait("act", state["th_free"])
                a1 = kb.op("act", lambda e, bk=bk: e.activation(out=thf, in_=bank(bk), func=AF.Exp, scale=-1.0))
                kb.wait("act", a1)
                a2 = kb.op("act", lambda e: e.activation(out=thf, in_=thf, func=AF.Ln, bias=onesf[:, 0:1]))
                kb.wait("act", a2)
                a3 = kb.op("act", lambda e: e.activation(out=thf, in_=thf, func=AF.Exp, scale=-1.0))
                kb.wait("dve", a3)
                ev = kb.op("dve", lambda e, bk=bk, cols=cols: e.tensor_tensor(
                    out=dst[:, cols], in0=thf, in1=bank(bk), op=ALU.mult))
                state["th_free"] = ev
            bank_free[bk] = ev
            return ev

        def proj(job, kind, dst):
            ev = None
            for tg in range(4):
                ev = proj_fill(job, kind, tg, dst)
            return ev

        def vtrans(vcond, vdst):
            kb.wait("pe", vcond)
            kb.wait("pe", bank_free[0])
            kb.wait("pe", bank_free[1])
            psv = bank2_bf(0)
            for j in range(16):
                pc = kb.op("pe", lambda e, j=j: e.transpose(
                    out=psv[:, j * 128:(j + 1) * 128], in_=vT[:, j * 128:(j + 1) * 128], identity=ident[:]),
                    ms=(j == 15))
            kb.wait("dve", pc)
            ev = kb.op("dve", lambda e: e.tensor_copy(out=vdst, in_=psv))
            bank_free[0] = ev
            bank_free[1] = ev
            return ev

        class FillStream:
            def __init__(self, tasks):
                self.tasks = tasks
                self.ti = 0
                self.kc = 0

            def remaining(self):
                return sum(t[6] for t in self.tasks[self.ti:]) - self.kc

            def emit(self, nmm, force=False):
                if getattr(self, "cool", 0) and not force:
                    self.cool -= 1
                    return
                while nmm > 0 and self.ti < len(self.tasks):
                    job, kind, tg, dst, st, key, nops = self.tasks[self.ti]
                    bk = 3
                    if self.kc == 0:
                        if kind == "vt":
                            kb.wait("pe", st["v"])
                        else:
                            kb.wait("pe", job["ld"])
                        kb.wait("pe", bank_free[bk])
                    kc = self.kc
                    if kind == "vt":
                        psv = bank(bk).bitcast(BF16)
                        jt = tg * 8 + kc
                        mc = kb.op("pe", lambda e, kc=kc, jt=jt, psv=psv: e.transpose(
                            out=psv[:, kc * 128:(kc + 1) * 128], in_=vT[:, jt * 128:(jt + 1) * 128],
                            identity=ident[:]), ms=(kc == nops - 1))
                    else:
                        slot = job["slot0"]
                        rhs_ap = role["hnT"][:, kc, tg * 512:(tg + 1) * 512]
                        mc = kb.op("pe", lambda e, rhs_ap=rhs_ap, kc=kc, slot=slot, bk=bk: e.matmul(
                            bank(bk), lhsT=Wt[:, slot * 2048 + kc * 128: slot * 2048 + (kc + 1) * 128],
                            rhs=rhs_ap, start=(kc == 0), stop=(kc == 15)),
                            ms=(kc == 15))
                    self.kc += 1
                    nmm -= 1
                    if self.kc == nops:
                        kb.wait("dve", mc)
                        if kind == "vt":
                            ev = kb.op("dve", lambda e, tg=tg, dst=dst, psv=psv: e.tensor_copy(
                                out=dst[:, tg * 1024:(tg + 1) * 1024], in_=psv))
                        else:
                            if tg == 3:
                                ws.release(job, mc)
                            cols = slice(tg * 512, (tg + 1) * 512)
                            ev = kb.op("dve", lambda e, bk=bk, cols=cols, dst=dst: e.tensor_copy(
                                out=dst[:, cols], in_=bank(bk)))
                        bank_free[bk] = ev
                        st[key] = ev
                        self.ti += 1
                        self.kc = 0
                        self.cool = 0
                        return

            def flush(self):
                while self.ti < len(self.tasks):
                    self.emit(16, force=True)

        def attention(it, econd, qcond, kcond, vcond, zcond, kT, vtok, fillers):
            LA = 2
            G = []
            for bi, (q0, n, lst) in enumerate(it["blocks"]):
                for i, (j, eoff, c0, c1) in enumerate(lst):
                    G.append((q0, n, j, eoff, i == 0, i == len(lst) - 1, c0, c1))
            hout = it["hout"]
            scol = it["scol"]
            p_free = state.setdefault("p_free", [None] * NP)
            gb = state.setdefault("gblk", 0)
            scond = {}
            sbank = {}
            pend_fin = []

            def emit_S(g):
                q0, n, j, eoff, first, last, c0, c1 = G[g]
                bk = state["sb"] % 3
                state["sb"] += 1
                sbank[g] = bk
                kb.wait("pe", bank_free[bk])
                if g == 0:
                    kb.wait("pe", qcond)
                    kb.wait("pe", kcond)
                scond[g] = kb.op("pe", lambda e, bk=bk, j=j, q0=q0, c0=c0, c1=c1: e.matmul(
                    bank(bk)[:, c0:c1], lhsT=kT[:, j * 128:(j + 1) * 128], rhs=qT[:, q0 + c0:q0 + c1],
                    start=True, stop=True))

            for g in range(min(LA, len(G))):
                emit_S(g)
            mul_last = None
            for g in range(len(G)):
                q0, n, j, eoff, first, last, c0, c1 = G[g]
                bk = sbank[g]
                psl = (gb + g) % NP
                if first:
                    state["qbn"] += 1
                ob = 4 + 2 * (state["qbn"] % 2)
                kb.wait("act", scond[g])
                kb.wait("act", p_free[psl])
                ec = kb.op("act", lambda e, bk=bk, psl=psl, c0=c0, c1=c1: e.activation(
                    out=Pb[psl][:, c0:c1], in_=bank(bk)[:, c0:c1], func=AF.Exp, scale=SCALE))
                bank_free[bk] = ec
                kb.wait("dve", ec)
                kb.wait("dve", econd)
                mc = kb.op("dve", lambda e, psl=psl, eoff=eoff, c0=c0, c1=c1: e.tensor_tensor(
                    out=Pb[psl][:, c0:c1], in0=Pb[psl][:, c0:c1], in1=Eb[:, eoff + c0:eoff + c1], op=ALU.mult))
                mul_last = mc
                if g + LA < len(G):
                    emit_S(g + LA)
                if fillers is not None and fillers.remaining() > 0:
                    nb = len(G) - g
                    fillers.emit(-(-fillers.remaining() // nb) + (4 if g == 0 else 0))
                kb.wait("pe", mc)
                if first:
                    kb.wait("pe", bank_free[ob])
                    kb.wait("pe", bank_free[ob + 1])
                if g == 0:
                    kb.wait("pe", vcond)
                kb.op("pe", lambda e, j=j, psl=psl, first=first, last=last, ob=ob, c0=c0, c1=c1: e.matmul(
                    bank(ob)[:, c0:c1], lhsT=vtok[:, j * 128:(j + 1) * 128], rhs=Pb[psl][:, c0:c1],
                    start=first, stop=last, skip_group_check=True), ms=False)
                pv = kb.op("pe", lambda e, psl=psl, first=first, last=last, ob=ob, c0=c0, c1=c1: e.matmul(
                    bank(ob + 1)[:, c0:c1], lhsT=ones[:], rhs=Pb[psl][:, c0:c1], start=first, stop=last,
                    skip_group_check=True))
                p_free[psl] = pv
                if last:
                    pend_fin.append((g + 2, pv, n, q0, ob))
                while pend_fin and (pend_fin[0][0] <= g or g == len(G) - 1):
                    _, pvc, fn_, fq0, fob = pend_fin.pop(0)
                    kb.wait("act", pvc)
                    kb.wait("act", state["r_free"])
                    f1 = kb.op("act", lambda e, n=fn_, ob=fob: e.activation(
                        out=rbuf[:, 0:n], in_=bank(ob + 1)[:, 0:n], func=AF.Ln, bias=esink[:, scol:scol + 1]))
                    kb.wait("act", f1)
                    f2 = kb.op("act", lambda e, n=fn_: e.activation(
                        out=rbuf[:, 0:n], in_=rbuf[:, 0:n], func=AF.Exp, scale=-1.0))
                    kb.wait("dve", f2)
                    f3 = kb.op("dve", lambda e, n=fn_, ob=fob: e.tensor_tensor(
                        out=tbuf[:, 0:n], in0=bank(ob)[:, 0:n], in1=rbuf[:, 0:n], op=ALU.mult))
                    bank_free[fob] = f3
                    bank_free[fob + 1] = f3
                    state["r_free"] = f3
                    kb.wait("dve", f3)
                    kb.wait("dve", zcond)
                    ydst = role["yT"][:, hout, fq0:fq0 + fn_]
                    kb.op("dve", lambda e, n=fn_, q0=fq0, ydst=ydst: e.tensor_tensor(
                        out=ydst, in0=tbuf[:, 0:n], in1=sz2[:, q0:q0 + n], op=ALU.mult))
            state["gblk"] = gb + len(G)
            state["e_reader"] = mul_last

        def phase_heads(items, after_last_proj=None):
            kb.barrier()
            for eng_ in ("pe", "act", "dve"):
                wait_stores(eng_)
            grp = state["grp"]
            kvst = {}

            def kv_stream(it, alt):
                st = {}
                tasks = [(it["jobs"]["v"], "v", tg, vT, st, "v", 16) for tg in range(4)]
                tasks += [(None, "vt", hf, vtokb[alt], st, "vt", 8) for hf in range(2)]
                tasks += [(it["jobs"]["k"], "k", tg, kTb[alt], st, "k", 16) for tg in range(4)]
                return st, FillStream(tasks)

            pending = None
            for i, it in enumerate(items):
                kb.wait("pool", state["e_reader"])
                ldc = kb.dma("pool", Eb[:, 0:it["ew"]], it["esrc"], e_ld)
                if it["need_exp"]:
                    kb.wait("act", ldc)
                    econd = kb.op("act", lambda e, w=it["ew"]: e.activation(
                        out=Eb[:, 0:w], in_=Eb[:, 0:w], func=AF.Exp))
                else:
                    econd = ldc
                jobs = it["jobs"]
                if "v" in jobs:
                    if pending is None or pending[0] != i:
                        grp += 1
                        st, stream = kv_stream(it, grp % 2)
                        pending = (i, st, stream, grp % 2)
                    _, st, stream, alt = pending
                    stream.flush()
                    kvst = {"k": st["k"], "alt": alt, "v": st["vt"]}
                    pending = None
                zcond = proj(jobs["z"], "z", sz2)
                qcond = proj(jobs["q"], "q", qT)
                if i == len(items) - 1 and after_last_proj is not None:
                    after_last_proj()
                fillers = None
                if i + 1 < len(items) and "v" in items[i + 1]["jobs"]:
                    grp += 1
                    st2, stream2 = kv_stream(items[i + 1], grp % 2)
                    pending = (i + 1, st2, stream2, grp % 2)
                    fillers = stream2
                attention(it, econd, qcond, kvst["k"], kvst["v"], zcond, kTb[kvst["alt"]], vtokb[kvst["alt"]], fillers)
            state["grp"] = grp

        xr = [af32(0, 2048), af32(4096, 2048), af32(8192, 2048)]
        hn3 = [abf(12288, 2048), abf(14336, 2048)]
        lnB3 = af32(16384, 2048)
        wq_ld = [kb.new_sem("wqld%d" % i) for i in range(2)]
        xr_ld = [kb.new_sem("xrld%d" % i) for i in range(3)]
        xr_st = [kb.new_sem("xrst%d" % i) for i in range(3)]
        state["xr_free"] = [[], [], []]

        def issue_wq(w_out, dead_flat):
            kb.wait("pool", kb.last["pe"])
            conds = []
            for i in range(2):
                conds.append(kb.dma("pool", dead_flat[:, i * 8192:(i + 1) * 8192], w_out[2 + i], wq_ld[i]))
            state["wq"] = (conds, dead_flat)

        def phase_out(qjobs, res_src, mode, x1_dst, out_dst, ln_idx):
            kb.barrier()
            yT3 = role["yT"]
            wq_conds, dead_flat = state["wq"]
            if mode != "plain":
                lncond = kb.dma("sp", lnB3, ln_d[ln_idx].partition_broadcast(128), ln_ld)
            ldc = {}

            def issue_load(t):
                sl = t % 3
                for c_ in state["xr_free"][sl]:
                    kb.wait("sp", c_)
                ldc[t] = kb.dma("sp", xr[sl], res_src[t * 128:(t + 1) * 128, :], xr_ld[sl])

            def rhs_q(qd, fc):
                if qd < 2:
                    slot = qjobs[qd]["slot0"]
                    return Wt[:, slot * 2048 + fc * 512: slot * 2048 + (fc + 1) * 512]
                return dead_flat[:, (qd - 2) * 8192 + fc * 512:(qd - 2) * 8192 + (fc + 1) * 512]

            issue_load(0)
            issue_load(1)
            pend = None
            for t in range(NT + 1):
                cur = None
                if t < NT:
                    if t + 2 < NT:
                        issue_load(t + 2)
                    sl = t % 3
                    hs = t % 2
                    addc = None
                    for qd in range(4):
                        if t == 0:
                            kb.wait("pe", qjobs[qd]["ld"] if qd < 2 else wq_conds[qd - 2])
                        kb.wait("pe", bank_free[qd])
                        for fc in range(16):
                            lhs_ap = yT3[:, fc, t * 128:(t + 1) * 128]
                            rhs_ap = rhs_q(qd, fc)
                            mc = kb.op("pe", lambda e, qd=qd, fc=fc, lhs_ap=lhs_ap, rhs_ap=rhs_ap: e.matmul(
                                bank(qd), lhsT=lhs_ap, rhs=rhs_ap, start=(fc == 0), stop=(fc == 15)),
                                ms=(fc == 15))
                        if t == NT - 1 and qd < 2:
                            ws.release(qjobs[qd], mc)
                        kb.wait("dve", mc)
                        kb.wait("dve", ldc[t])
                        addc = kb.op("dve", lambda e, qd=qd, sl=sl: e.tensor_tensor(
                            out=xr[sl][:, qd * 512:(qd + 1) * 512], in0=bank(qd),
                            in1=xr[sl][:, qd * 512:(qd + 1) * 512], op=ALU.add))
                        bank_free[qd] = addc
                    frees = []
                    if mode in ("mid", "plain"):
                        kb.wait("act", addc)
                        dstd = x1_dst if mode == "mid" else out_dst
                        frees.append(kb.dma("act", dstd[t * 128:(t + 1) * 128, :], xr[sl], xr_st[sl]))
                    if mode != "plain":
                        kb.wait("act", addc)
                        if mode == "mid":
                            kb.wait("act", state["hn_free"][hs])
                        jk = hn3[hs] if mode == "mid" else hn3[0]
                        a1 = kb.op("act", lambda e, sl=sl, jk=jk: e.activation(
                            out=jk, in_=xr[sl], func=AF.Square, scale=float(2048 ** -0.5),
                            accum_out=stats[:, sl:sl + 1]))
                        kb.wait("act", a1)
                        a2 = kb.op("act", lambda e, sl=sl: e.activation(
                            out=stats[:, 4 + sl:5 + sl], in_=stats[:, sl:sl + 1], func=AF.Sqrt, bias=epsT[:, 0:1]))
                        kb.wait("dve", a2)
                        d1 = kb.op("dve", lambda e, sl=sl: e.reciprocal(out=stats[:, 8 + sl:9 + sl],
                                                                        in_=stats[:, 4 + sl:5 + sl]))
                        kb.wait("dve", d1)
                        kb.wait("dve", lncond)
                        if mode == "mid":
                            d2 = kb.op("dve", lambda e, sl=sl, hs=hs: e.scalar_tensor_tensor(
                                out=hn3[hs], in0=xr[sl], scalar=stats[:, 8 + sl:9 + sl], in1=lnB3,
                                op0=ALU.mult, op1=ALU.mult))
                            frees.append(d2)
                            cur = (hs, t, d2)
                        else:
                            d2 = kb.op("dve", lambda e, sl=sl: e.scalar_tensor_tensor(
                                out=xr[sl], in0=xr[sl], scalar=stats[:, 8 + sl:9 + sl], in1=lnB3,
                                op0=ALU.mult, op1=ALU.mult))
                            kb.wait("act", d2)
                            frees.append(kb.dma("act", out_dst[t * 128:(t + 1) * 128, :], xr[sl], xr_st[sl]))
                    state["xr_free"][sl] = frees
                if pend is not None:
                    hs0, t0, d20 = pend
                    kb.wait("pe", d20)
                    kb.wait("pe", bank_free[4 + 2 * hs0])
                    kb.wait("pe", bank_free[5 + 2 * hs0])
                    pst = bank2_bf(4 + 2 * hs0)
                    for fc in range(16):
                        pc = kb.op("pe", lambda e, hs0=hs0, fc=fc, pst=pst: e.transpose(
                            out=pst[:, fc * 128:(fc + 1) * 128], in_=hn3[hs0][:, fc * 128:(fc + 1) * 128],
                            identity=ident[:]), ms=(fc == 15))
                    state["hn_free"][hs0] = pc
                    kb.wait("act", pc)
                    hdst = yT3[:, :, t0 * 128:(t0 + 1) * 128]
                    ev = kb.op("act", lambda e, hdst=hdst, pst=pst: e.activation(
                        out=hdst, in_=pst.rearrange("p (a b) -> p a b", a=16), func=AF.Copy))
                    bank_free[4 + 2 * hs0] = ev
                    bank_free[5 + 2 * hs0] = ev
                pend = cur

        def wait_stores(eng):
            for s in xo_st + x_st + xr_st:
                if s.count:
                    kb.wait(eng, (s, s.count))

        for (sq, L, items, qjobs) in plan:
            rows = slice(sq * S, (sq + 1) * S)
            first_layer = (L == do_layers[0])
            last_layer = (L == do_layers[-1])
            if L == 0:
                role.update(hnT=bufA3, yT=bufB3, hflat=bufA, yflat=bufB)
            else:
                role.update(hnT=bufB3, yT=bufA3, hflat=bufB, yflat=bufA)
            src = x_d[rows, :] if first_layer else x1_d
            w_out = wab_out if L == 0 else wc_out
            if first_layer:
                wait_stores("sp")
                phase_norm(src, 0 if L == 0 else 1, None)
            hflat = role["hflat"]
            phase_heads(items, after_last_proj=lambda w_out=w_out, hflat=hflat: issue_wq(w_out, hflat))
            wait_stores("sp")
            if not last_layer:
                phase_out(qjobs, src, "mid", x1_d, None, 1)
            elif final_norm:
                phase_out(qjobs, src, "final", None, out_d[rows, :], 2)
            else:
                phase_out(qjobs, src, "plain", None, out_d[rows, :], 0)
        kb.barrier()
        wait_stores("sp")

        with nc.Block() as block:
            @block.tensor
            def _(e):
                for f in kb.q["pe"]:
                    f(e)

            @block.scalar
            def _(e):
                for f in kb.q["act"]:
                    f(e)

            @block.vector
            def _(e):
                for f in kb.q["dve"]:
                    f(e)

            @block.gpsimd
            def _(e):
                for f in kb.q["pool"]:
                    f(e)

            @block.sync
            def _(e):
                for f in kb.q["sp"]:
                    f(e)
    return nc


def _w_in_layout(w):
    C = w.shape[1]
    g = C // 128
    return np.ascontiguousarray(w.reshape(16, 128, g, 128).transpose(2, 1, 0, 3)).reshape(g, 128, 2048)


def _w_out_layout(w):
    return np.ascontiguousarray(w.reshape(16, 128, 4, 512).transpose(2, 1, 0, 3)).reshape(4, 128, 8192)


def _alibi_tables():
    slopes = np.exp2(-8.0 * np.arange(1, 17, dtype=np.float64) / 16)
    p = np.arange(128)[:, None]
    EA = np.zeros((8, 128, 1152), np.float32)
    u = np.arange(1152)[None, :]
    dl = u - 512 - p
    for h in range(8):
        EA[h] = np.where(np.abs(dl) <= 128, np.exp(-slopes[h] * np.abs(dl)), 0.0)
    EB = np.zeros((8, 128, 2944), np.float32)
    u = np.arange(2944)[None, :]
    dl = u - 1408 - p
    ad = np.abs(dl)
    mult = (ad <= 64).astype(np.float64) + ((dl % 4 == 0) & (ad <= 256)) + ((dl % 16 == 0) & (ad <= 1024))
    for h in range(8):
        EB[h] = mult * np.exp(-slopes[8 + h] * ad)
    return EA, EB


def _rpb_strips(rpb):
    p = np.arange(128)
    rl = (p // 64)[:, None, None]
    kc = (p % 64)[:, None, None]
    i = np.arange(24)[None, :, None]
    qc = np.arange(64)[None, None, :]
    dr = 14 - (i - rl - 4) + 0 * qc
    dc = kc - qc + 15 + 0 * i
    c0 = np.clip(qc - 8, 0, 48)
    colok = (kc >= c0) & (kc < c0 + 16) & (i >= 0)
    out = np.full((16, 128, 2, 24, 64), NEGB, np.float32)
    for var, (lo, hi) in enumerate(((3, 10), (0, 14))):
        ok = colok & (dr >= lo) & (dr <= hi)
        drc = np.clip(dr, 0, 14)
        dcc = np.clip(dc, 0, 30)
        gathered = rpb[:, drc, dcc]
        out[:, :, var] = np.where(ok[None], gathered, np.float32(NEGB))
    return out.reshape(16, 128, 3072)


_CACHE = {}


def _get_nc(nseq, do_layers=(0, 1), final_norm=True):
    key = (nseq, tuple(do_layers), final_norm)
    if key not in _CACHE:
        _CACHE[key] = build(nseq, do_layers, final_norm)
    return _CACHE[key]


def _common_inputs(ln_ab, w_in_ab, sink_a, w_out_ab, ln_c, w_in_c, rpb_c, w_out_c, ln_f):
    EA, EB = _alibi_tables()
    return {
        "wab_in": _w_in_layout(np.asarray(w_in_ab[0], np.float32)),
        "wab_out": _w_out_layout(np.asarray(w_out_ab[0], np.float32)),
        "wc_in": _w_in_layout(np.asarray(w_in_c[0], np.float32)),
        "wc_out": _w_out_layout(np.asarray(w_out_c[0], np.float32)),
        "ln": np.ascontiguousarray(np.stack([np.asarray(ln_ab[0]), np.asarray(ln_c[0]), np.asarray(ln_f)]).astype(np.float32)),
        "sink": np.ascontiguousarray(np.asarray(sink_a[0], np.float32)),
        "ident": np.eye(128, dtype=np.float32),
        "EA": EA, "EB": EB,
        "BC": _rpb_strips(np.asarray(rpb_c[0], np.float32)),
    }


def kernel(x, ln_ab, w_in_ab, sink_a, w_out_ab, ln_c, w_in_c, rpb_c, w_out_c, ln_f):
    x = np.asarray(x, np.float32)
    B = x.shape[0]
    nseq = B // N_CORES
    common = _common_inputs(ln_ab, w_in_ab, sink_a, w_out_ab, ln_c, w_in_c, rpb_c, w_out_c, ln_f)
    nc = _get_nc(nseq)
    in_maps = []
    for c in range(N_CORES):
        m = dict(common)
        m["x"] = np.ascontiguousarray(x[c * nseq:(c + 1) * nseq].reshape(nseq * S, D))
        in_maps.append(m)
    res = run_bass_kernel_spmd(nc, in_maps, core_ids=list(range(N_CORES)))
    outs = [np.asarray(r["out"]).reshape(nseq, S, D) for r in res.results]
    return np.concatenate(outs, axis=0).astype(np.float32)
```
